# Optimizing a Trainium2 kernel written in Bass

```python
import math
import jax, jax.numpy as jnp
from jax import lax
import numpy as np

D_MODEL = 2048
BATCH = 4
SEQ = 2048
DEPTH = 4
DEC_BATCH = 128
DEC_SEQ = 8
PAST_LEN = 16384
PAGE_SIZE = 128

N_EVEN = (DEPTH + 1) // 2
N_ODD = DEPTH // 2
EPS = 1e-6
D_RG = D_MODEL // 2
RG_BLOCKS = 16
RG_BW = D_RG // RG_BLOCKS
RG_C = 8.0
CONV_K = 4
ML_HEADS = 4
ML_DK = D_MODEL // 16
ML_DV = D_MODEL // 8
ML_CHUNK = 64
ML_HK = ML_HEADS * ML_DK
ML_HV = ML_HEADS * ML_DV
CONV_W = D_RG + 2 * ML_HK
E_IN = CONV_W + D_RG + 2 * ML_HV + 2 * ML_HEADS
D_C = D_MODEL
C_GROUPS = 8
C_GW = D_C // C_GROUPS
C_CHUNK = 128
D_FF = 5632
FFN_K = 3

kernel_name = 'hybrid_rglru_mlstm_chunkmlp_step'


def rmsnorm(x, g):
    xf = x.astype(jnp.float32)
    y = xf * lax.rsqrt(jnp.mean(xf * xf, axis=-1, keepdims=True) + EPS)
    return (y * g.astype(jnp.float32)).astype(x.dtype)


def causal_dwconv(x, buf, w, b):
    S = x.shape[1]
    xp = jnp.concatenate([buf.astype(x.dtype), x], axis=1)
    y = b.astype(x.dtype)
    for k in range(w.shape[0]):
        y = y + w[k].astype(x.dtype) * xp[:, k:k + S]
    return y, xp[:, S:]


def rglru(xc, h0, wa, ba, wx, bx, lam):
    B, S, _ = xc.shape
    xb = xc.reshape(B, S, RG_BLOCKS, RG_BW)
    r = jax.nn.sigmoid(jnp.einsum('bsnc,ncd->bsnd', xb, wa.astype(jnp.float32)).reshape(B, S, D_RG) + ba)
    i = jax.nn.sigmoid(jnp.einsum('bsnc,ncd->bsnd', xb, wx.astype(jnp.float32)).reshape(B, S, D_RG) + bx)
    log_a = -RG_C * r * jax.nn.softplus(-lam.astype(jnp.float32))
    u = jnp.sqrt(-jnp.expm1(2.0 * log_a)) * (i * xc)

    def combine(e, l):
        return e[0] * l[0], l[0] * e[1] + l[1]

    a_cum, b_cum = lax.associative_scan(combine, (jnp.exp(log_a), u), axis=1)
    h = a_cum * h0[:, None, :] + b_cum
    return h, h[:, -1]


def mlstm(q, k, v, logi, logf, C0, n0, m0):
    B, S, H, _ = q.shape
    L = math.gcd(ML_CHUNK, S)
    NC = S // L

    def chunks(t):
        return jnp.moveaxis(t.reshape((B, NC, L) + t.shape[2:]), 1, 0)

    causal = jnp.tril(jnp.ones((L, L), dtype=bool))

    def step(carry, xs):
        C, n, m = carry
        qc, kc, vc, ic, fc = xs
        b = jnp.cumsum(fc, axis=1).transpose(0, 2, 1)
        ic = ic.transpose(0, 2, 1)
        D = b[..., :, None] - b[..., None, :] + ic[..., None, :]
        D = jnp.where(causal, D, -jnp.inf)
        inter = b + m[..., None]
        m_t = jnp.maximum(inter, jnp.max(D, axis=-1))
        s = jnp.einsum('blhd,bshd->bhls', qc, kc) * jnp.exp(D - m_t[..., None])
        w_inter = jnp.exp(inter - m_t)
        num = w_inter[..., None] * jnp.einsum('blhd,bhde->bhle', qc, C) + jnp.einsum('bhls,bshe->bhle', s, vc)
        den = w_inter * jnp.einsum('blhd,bhd->bhl', qc, n) + jnp.sum(s, axis=-1)
        h = num / jnp.maximum(jnp.abs(den), jnp.exp(-m_t))[..., None]
        m_new = m_t[..., -1]
        w_state = jnp.exp(b[..., -1:] - b + ic - m_new[..., None])
        decay = jnp.exp(b[..., -1] + m - m_new)
        C_new = decay[..., None, None] * C + jnp.einsum('bhs,bshd,bshe->bhde', w_state, kc, vc)
        n_new = decay[..., None] * n + jnp.einsum('bhs,bshd->bhd', w_state, kc)
        return (C_new, n_new, m_new), h.transpose(0, 2, 1, 3)

    (C, n, m), hs = lax.scan(step, (C0, n0, m0),
                             (chunks(q), chunks(k), chunks(v), chunks(logi), chunks(logf)))
    h = jnp.moveaxis(hs, 0, 1).reshape(B, S, H, ML_DV)
    return h, C, n, m


def even_mixer(xn, conv_buf, h0, C0, n0, m0, P, j):
    f32 = jnp.float32
    B, S, _ = xn.shape
    proj = (xn @ P['w_in_even'][j]).astype(f32)
    o1 = CONV_W
    o2 = o1 + D_RG
    o3 = o2 + ML_HV
    o4 = o3 + ML_HV
    conv_out, conv_new = causal_dwconv(proj[..., :o1], conv_buf, P['conv_even_w'][j], P['conv_even_b'][j])
    h, h_last = rglru(conv_out[..., :D_RG], h0.astype(f32), P['rg_wa'][j], P['rg_ba'][j],
                      P['rg_wx'][j], P['rg_bx'][j], P['rg_lambda'][j])
    rg_out = jax.nn.gelu(proj[..., o1:o2]) * h
    qk = jax.nn.silu(conv_out[..., D_RG:])
    q = qk[..., :ML_HK].reshape(B, S, ML_HEADS, ML_DK)
    k = qk[..., ML_HK:].reshape(B, S, ML_HEADS, ML_DK) * (ML_DK ** -0.5)
    v = proj[..., o2:o3].reshape(B, S, ML_HEADS, ML_DV)
    o_gate = jax.nn.sigmoid(proj[..., o3:o4])
    gates = proj[..., o4:].reshape(B, S, 2, ML_HEADS) + P['ml_gate_b'][j].astype(f32)
    hm, C, n, m = mlstm(q, k, v, gates[..., 0, :], jax.nn.log_sigmoid(gates[..., 1, :]),
                        C0.astype(f32), n0.astype(f32), m0.astype(f32))
    hm = hm * lax.rsqrt(jnp.mean(hm * hm, axis=-1, keepdims=True) + EPS) * P['ml_norm_g'][j].astype(f32)
    ml_out = hm.reshape(B, S, ML_HV) * o_gate
    mixed = jnp.concatenate([rg_out, ml_out], axis=-1).astype(xn.dtype)
    return mixed @ P['w_out_even'][j], (conv_new, h_last, C, n, m)


def odd_mixer(xn, P, j):
    f32 = jnp.float32
    B, S, _ = xn.shape
    proj = jax.nn.gelu((xn @ P['w_in_odd'][j]).astype(f32))
    u, v = proj[..., :D_C], proj[..., D_C:]
    mu = jnp.mean(v, axis=-1, keepdims=True)
    var = jnp.mean(jnp.square(v - mu), axis=-1, keepdims=True)
    v = (v - mu) * lax.rsqrt(var + EPS) * P['sgu_ln_g'][j].astype(f32) + P['sgu_ln_b'][j].astype(f32)
    Sp = -(-S // C_CHUNK) * C_CHUNK
    vp = jnp.pad(v, ((0, 0), (0, Sp - S), (0, 0))).reshape(B, Sp // C_CHUNK, C_CHUNK, C_GROUPS, C_GW)
    ws = jnp.tril(P['sgu_ws'][j].astype(f32))
    mix = jnp.einsum('gts,bnsgc->bntgc', ws, vp) + P['sgu_b'][j].astype(f32).T[:, :, None]
    mix = mix.reshape(B, Sp, D_C)[:, :S]
    out = (u * mix).astype(xn.dtype) @ P['w_out_odd'][j]
    return out, v


def conv_ffn(xn, buf, P, layer):
    up = xn @ P['ffn_w_up'][layer]
    g, u = up[..., :D_FF], up[..., D_FF:]
    gc, buf_new = causal_dwconv(g.astype(jnp.float32), buf, P['ffn_conv_w'][layer], P['ffn_conv_b'][layer])
    hid = (jax.nn.gelu(gc) * u.astype(jnp.float32)).astype(xn.dtype)
    return hid @ P['ffn_w_down'][layer], buf_new


def trunk(x, conv_buf, rg_h, ml_C, ml_n, ml_m, ffn_buf, P):
    conv_l, h_l, C_l, n_l, m_l, v_l, f_l = [], [], [], [], [], [], []
    for layer in range(DEPTH):
        j = layer // 2
        xn = rmsnorm(x, P['norm_mix'][layer])
        if layer % 2 == 0:
            mix, (cb, hl, C, n, m) = even_mixer(xn, conv_buf[j], rg_h[j], ml_C[j], ml_n[j], ml_m[j], P, j)
            conv_l.append(cb)
            h_l.append(hl)
            C_l.append(C)
            n_l.append(n)
            m_l.append(m)
        else:
            mix, v = odd_mixer(xn, P, j)
            v_l.append(v)
        x = x + mix.astype(x.dtype)
        f, fb = conv_ffn(rmsnorm(x, P['norm_ffn'][layer]), ffn_buf[layer], P, layer)
        f_l.append(fb)
        x = x + f.astype(x.dtype)
    y = rmsnorm(x, P['norm_final'])
    dt = x.dtype
    return (y, jnp.stack(conv_l).astype(dt), jnp.stack(h_l).astype(dt), jnp.stack(C_l).astype(dt),
            jnp.stack(n_l).astype(dt), jnp.stack(m_l).astype(dt), v_l, jnp.stack(f_l).astype(dt))


def setup_inputs(seed: int = 0) -> dict:
    key = jax.random.key(seed)
    ks = iter(jax.random.split(key, 40))

    def nrm(shape, s):
        return s * jax.random.normal(next(ks), shape, jnp.float32)

    a0 = jax.random.uniform(next(ks), (N_EVEN, D_RG), jnp.float32, 0.9, 0.999) ** (1.0 / RG_C)
    lam = jnp.log(a0) - jnp.log1p(-a0)
    i_b = nrm((N_EVEN, ML_HEADS), 0.1)
    f_b = jnp.linspace(3.0, 6.0, ML_HEADS, dtype=jnp.float32) + nrm((N_EVEN, ML_HEADS), 0.1)
    ml_gate_b = jnp.stack([i_b, f_b], axis=1)
    return {
        'x_prompt': nrm((BATCH, SEQ, D_MODEL), 1.0),
        'x_sample': nrm((DEC_BATCH, DEC_SEQ, D_MODEL), 1.0),
        'state_conv_mix': nrm((N_EVEN, DEC_BATCH, CONV_K - 1, CONV_W), 1.0),
        'state_rglru_h': nrm((N_EVEN, DEC_BATCH, D_RG), 0.5),
        'state_mlstm_C': nrm((N_EVEN, DEC_BATCH, ML_HEADS, ML_DK, ML_DV), 0.1),
        'state_mlstm_n': nrm((N_EVEN, DEC_BATCH, ML_HEADS, ML_DK), 0.1),
        'state_mlstm_m': nrm((N_EVEN, DEC_BATCH, ML_HEADS), 0.5),
        'state_ffn_conv': nrm((DEPTH, DEC_BATCH, FFN_K - 1, D_FF), 1.0),
        'norm_mix': 1.0 + nrm((DEPTH, D_MODEL), 0.02),
        'norm_ffn': 1.0 + nrm((DEPTH, D_MODEL), 0.02),
        'norm_final': 1.0 + nrm((D_MODEL,), 0.02),
        'w_in_even': nrm((N_EVEN, D_MODEL, E_IN), D_MODEL ** -0.5),
        'conv_even_w': nrm((N_EVEN, CONV_K, CONV_W), 0.5),
        'conv_even_b': nrm((N_EVEN, CONV_W), 0.01),
        'rg_wa': nrm((N_EVEN, RG_BLOCKS, RG_BW, RG_BW), RG_BW ** -0.5),
        'rg_ba': nrm((N_EVEN, D_RG), 0.01),
        'rg_wx': nrm((N_EVEN, RG_BLOCKS, RG_BW, RG_BW), RG_BW ** -0.5),
        'rg_bx': nrm((N_EVEN, D_RG), 0.01),
        'rg_lambda': lam,
        'ml_gate_b': ml_gate_b,
        'ml_norm_g': 1.0 + nrm((N_EVEN, ML_HEADS, ML_DV), 0.02),
        'w_out_even': nrm((N_EVEN, D_RG + ML_HV, D_MODEL), 0.5 * (D_RG + ML_HV) ** -0.5),
        'w_in_odd': nrm((N_ODD, D_MODEL, 2 * D_C), D_MODEL ** -0.5),
        'sgu_ln_g': 1.0 + nrm((N_ODD, D_C), 0.02),
        'sgu_ln_b': nrm((N_ODD, D_C), 0.01),
        'sgu_ws': nrm((N_ODD, C_GROUPS, C_CHUNK, C_CHUNK), C_CHUNK ** -0.5),
        'sgu_b': 1.0 + nrm((N_ODD, C_GROUPS, C_CHUNK), 0.1),
        'w_out_odd': nrm((N_ODD, D_C, D_MODEL), 0.5 * D_C ** -0.5),
        'ffn_w_up': nrm((DEPTH, D_MODEL, 2 * D_FF), D_MODEL ** -0.5),
        'ffn_conv_w': nrm((DEPTH, FFN_K, D_FF), FFN_K ** -0.5),
        'ffn_conv_b': nrm((DEPTH, D_FF), 0.01),
        'ffn_w_down': nrm((DEPTH, D_FF, D_MODEL), 0.5 * D_FF ** -0.5),
    }


def reference(x_prompt, x_sample, state_conv_mix, state_rglru_h, state_mlstm_C, state_mlstm_n,
              state_mlstm_m, state_ffn_conv, norm_mix, norm_ffn, norm_final, w_in_even,
              conv_even_w, conv_even_b, rg_wa, rg_ba, rg_wx, rg_bx, rg_lambda, ml_gate_b,
              ml_norm_g, w_out_even, w_in_odd, sgu_ln_g, sgu_ln_b, sgu_ws, sgu_b, w_out_odd,
              ffn_w_up, ffn_conv_w, ffn_conv_b, ffn_w_down):
    P = {
        'norm_mix': norm_mix, 'norm_ffn': norm_ffn, 'norm_final': norm_final,
        'w_in_even': w_in_even, 'conv_even_w': conv_even_w, 'conv_even_b': conv_even_b,
        'rg_wa': rg_wa, 'rg_ba': rg_ba, 'rg_wx': rg_wx, 'rg_bx': rg_bx, 'rg_lambda': rg_lambda,
        'ml_gate_b': ml_gate_b, 'ml_norm_g': ml_norm_g, 'w_out_even': w_out_even,
        'w_in_odd': w_in_odd, 'sgu_ln_g': sgu_ln_g, 'sgu_ln_b': sgu_ln_b, 'sgu_ws': sgu_ws,
        'sgu_b': sgu_b, 'w_out_odd': w_out_odd, 'ffn_w_up': ffn_w_up, 'ffn_conv_w': ffn_conv_w,
        'ffn_conv_b': ffn_conv_b, 'ffn_w_down': ffn_w_down,
    }
    Bp = x_prompt.shape[0]
    dt = x_prompt.dtype
    z_conv = jnp.zeros((N_EVEN, Bp, CONV_K - 1, CONV_W), dt)
    z_h = jnp.zeros((N_EVEN, Bp, D_RG), dt)
    z_C = jnp.zeros((N_EVEN, Bp, ML_HEADS, ML_DK, ML_DV), dt)
    z_n = jnp.zeros((N_EVEN, Bp, ML_HEADS, ML_DK), dt)
    z_m = jnp.zeros((N_EVEN, Bp, ML_HEADS), dt)
    z_f = jnp.zeros((DEPTH, Bp, FFN_K - 1, D_FF), dt)
    y_prompt, conv_p, h_p, C_p, n_p, m_p, _, f_p = trunk(x_prompt, z_conv, z_h, z_C, z_n, z_m, z_f, P)
    y_sample, conv_s, h_s, C_s, n_s, m_s, v_list, f_s = trunk(
        x_sample, state_conv_mix, state_rglru_h, state_mlstm_C, state_mlstm_n, state_mlstm_m,
        state_ffn_conv, P)
    chunk_v_s = jnp.stack(v_list).astype(x_sample.dtype)
    return (y_prompt, y_sample, conv_p, conv_s, h_p, h_s, C_p, C_s, n_p, n_s, m_p, m_s,
            chunk_v_s, f_p, f_s)
```

```python
from contextlib import ExitStack
import numpy as np
import ml_dtypes
import concourse.bass as bass
import concourse.mybir as mybir
from concourse.bass_utils import run_bass_kernel_spmd

F32 = mybir.dt.float32
BF16 = mybir.dt.bfloat16
I32 = mybir.dt.int32
AF = mybir.ActivationFunctionType
ALU = mybir.AluOpType

D = 2048
DEPTH = 4
TP = 1024
NSQ = 8
TS = 64
T = TP + TS
NTL = [(0, 512), (512, 512), (1024, 64)]
NB = 9
DFF = 5632
NJ = 44
EIN = 5128
EPS = 1e-6
SW_ = 1120


def blk(tb):
    return (tb * 128, 128 if tb < 8 else 64)


class Tracker:
    def __init__(self, nc, stack):
        self.nc = nc
        self.engs = ['pe', 'act', 'dve', 'pool', 'sp']
        self.sem = {k: stack.enter_context(nc.semaphore("s_" + k)) for k in ['pe', 'act', 'dve', 'pool']}
        self.cnt = {k: 0 for k in self.sem}
        self.q = {}
        for qn, issuer, R in [('sp', 'sp', 14), ('gq', 'pool', 8)]:
            self.q[qn] = dict(issuer=issuer, R=R, n=0,
                              sems=[stack.enter_context(nc.semaphore("q_%s%d" % (qn, i))) for i in range(R)])
        self.prog = {k: [] for k in self.engs}
        self.waited = {k: {} for k in self.engs}
        self.W = {}
        self.Rd = {}
        self.alias = {}

    def _exp(self, keys):
        out = []
        seen = set()
        stack = list(keys)
        while stack:
            k = stack.pop()
            if k in seen:
                continue
            seen.add(k)
            out.append(k)
            if k in self.alias:
                stack.extend(self.alias[k])
        return out

    def _deps(self, reads, writes):
        reads = self._exp(reads)
        writes = self._exp(writes)
        d = {}

        def add(m):
            for k, v in m.items():
                if d.get(k, 0) < v:
                    d[k] = v
        for r in reads:
            add(self.W.get(r, {}))
        for w in writes:
            add(self.W.get(w, {}))
            add(self.Rd.get(w, {}))
        return d

    def _waits(self, issuer, deps):
        wd = self.waited[issuer]
        wl = []
        for k, v in deps.items():
            if wd.get(k, 0) < v:
                wd[k] = v
                wl.append((self._semof(k), v))
        return wl

    def _semof(self, k):
        if k[0] == 'e':
            return self.sem[k[1]]
        return self.q[k[1]]['sems'][k[2]]

    def _record(self, key, val, reads, writes):
        reads = self._exp(reads)
        writes = self._exp(writes)
        for r in reads:
            self.Rd.setdefault(r, {})[key] = val
        for w in writes:
            self.W.setdefault(w, {})[key] = val

    def op(self, eng, fn, reads=(), writes=()):
        writes = list(writes) + [k for k in reads if k[0] == 'ps']
        deps = self._deps(reads, writes)
        wl = self._waits(eng, deps)
        self.cnt[eng] += 1
        v = self.cnt[eng]
        self.prog[eng].append((wl, fn, self.sem[eng], 1))
        self._record(('e', eng), v, reads, writes)

    def dma(self, qn, out, in_, reads=(), writes=(), nc_ok=False):
        q = self.q[qn]
        n = q['n']
        q['n'] += 1
        slot = n % q['R']
        k = n // q['R']
        deps = self._deps(reads, writes)
        key = ('q', qn, slot)
        if k >= 1:
            deps[key] = max(deps.get(key, 0), 16 * k)
        wl = self._waits(q['issuer'], deps)

        def fn(e, out=out, in_=in_, nc_ok=nc_ok):
            if nc_ok:
                return e.dma_start(out=out, in_=in_, allow_slow_non_contiguous=True)
            return e.dma_start(out=out, in_=in_)
        self.prog[q['issuer']].append((wl, fn, q['sems'][slot], 16))
        self._record(key, 16 * (k + 1), reads, writes)

    def finish(self):
        deps = {}
        for qn, q in self.q.items():
            for s in range(q['R']):
                cnt = (q['n'] - s + q['R'] - 1) // q['R']
                if cnt > 0:
                    deps[('q', qn, s)] = 16 * cnt
        for e in self.sem:
            if self.cnt[e] > 0:
                deps[('e', e)] = self.cnt[e]
        wl = self._waits('sp', deps)
        self.prog['sp'].append((wl, None, None, 0))

    def emit(self):
        nc = self.nc
        prog = self.prog

        def run(name, e):
            for wl, fn, sem, inc in prog[name]:
                for s, v in wl:
                    e.wait_ge(s, v)
                if fn is not None:
                    ins = fn(e)
                    ins.then_inc(sem, inc)
        with nc.Block() as block:
            @block.tensor
            def _(e):
                run('pe', e)

            @block.scalar
            def _(e):
                run('act', e)

            @block.vector
            def _(e):
                run('dve', e)

            @block.gpsimd
            def _(e):
                run('pool', e)

            @block.sync
            def _(e):
                run('sp', e)


def build(depth=DEPTH):
    nc = bass.Bass("TRN2", target_bir_lowering=False)
    st = ExitStack()
    tk = Tracker(nc, st)

    def din(name, shape, dt=F32):
        return nc.dram_tensor(name, list(shape), dt, kind="ExternalInput").ap()

    def dout(name, shape, dt=F32):
        return nc.dram_tensor(name, list(shape), dt, kind="ExternalOutput").ap()

    xin = din("xin", [2, T, D])
    st_conv = din("st_conv", [2, 16, 3, D])
    st_h = din("st_h", [2, 16, 1024])
    st_C = din("st_C", [2, 16, 4, 128, 256])
    st_n = din("st_n", [2, 16, 4, 128])
    st_m = din("st_m", [2, 16, 4])
    st_f = din("st_f", [4, 16, 2, DFF])
    norm_mix = din("norm_mix", [4, D])
    norm_ffn = din("norm_ffn", [4, D])
    norm_final = din("norm_final", [D])
    w_in_even = din("w_in_even", [2, D, EIN])
    conv_even_w = din("conv_even_w", [2, 4, D])
    conv_even_b = din("conv_even_b", [2, D])
    rg_wa = din("rg_wa", [2, 16, 64, 64])
    rg_ba = din("rg_ba", [2, 1024])
    rg_wx = din("rg_wx", [2, 16, 64, 64])
    rg_bx = din("rg_bx", [2, 1024])
    rg_lambda = din("rg_lambda", [2, 1024])
    ml_gate_b = din("ml_gate_b", [2, 2, 4])
    ml_norm_g = din("ml_norm_g", [2, 4, 256])
    w_out_even = din("w_out_even", [2, D, D])
    w_in_odd = din("w_in_odd", [2, D, 2 * D])
    sgu_ln_g = din("sgu_ln_g", [2, D])
    sgu_ln_b = din("sgu_ln_b", [2, D])
    sgu_ws = din("sgu_ws", [2, 8, 128, 128])
    sgu_b = din("sgu_b", [2, 8, 128])
    w_out_odd = din("w_out_odd", [2, D, D])
    ffn_w_up = din("ffn_w_up", [4, D, 2 * DFF])
    ffn_conv_w = din("ffn_conv_w", [4, 3, DFF])
    ffn_conv_b = din("ffn_conv_b", [4, DFF])
    ffn_w_down = din("ffn_w_down", [4, DFF, D])
    c_identf = din("c_identf", [128, 128])
    c_identb = din("c_identb", [128, 128], BF16)
    c_onesb = din("c_onesb", [128, 128], BF16)
    c_onesf = din("c_onesf", [128, 128])
    c_causal = din("c_causal", [128, 128])
    c_bdc = din("c_bdc", [64, 64])
    c_bd = din("c_bd", [64, 64])
    c_ind = din("c_ind", [64, 8])
    c_eye4 = din("c_eye4", [4, 4, 16])
    c_rep = din("c_rep", [8, 64], BF16)

    y_d = dout("y_d", [2, T, D])
    o_conv = dout("o_conv", [2, 2, 9, 3, D])
    o_h = dout("o_h", [2, 2, 9, 1024])
    o_C = dout("o_C", [2, 2, 9, 4, 128, 256])
    o_n = dout("o_n", [2, 2, 9, 4, 128])
    o_m = dout("o_m", [2, 2, 9, 4])
    o_v = dout("o_v", [2, 2, 64, D])
    o_f = dout("o_f", [2, 4, 9, 2, DFF])
    xTd = nc.dram_tensor("xTd", [128, 16, T], F32, kind="Internal").ap()

    def sb(name, shape, dt=F32):
        return st.enter_context(nc.sbuf_tensor(name, list(shape), dt))

    def ps(name, shape, dt=F32):
        return st.enter_context(nc.psum_tensor(name, list(shape), dt))

    xnT = sb("xnT", [128, 16, T], BF16)
    big = sb("big", [128, 32, T], BF16)
    xf = big[:].bitcast(F32)
    NSLOT = 3
    wring = [sb("wr%d" % i, [128, 16 * 256], BF16) for i in range(NSLOT)]
    S = [sb("S%d" % i, [128, SW_], F32) for i in range(6)]
    xres = [sb("xres%d" % i, [128, T], F32) for i in range(2)]
    identf = sb("identf", [128, 128])
    identb = sb("identb", [128, 128], BF16)
    onesb = sb("onesb", [128, 128], BF16)
    onesf = sb("onesf", [128, 128])
    causal = sb("causal", [128, 128])
    bdc = sb("bdc", [64, 64])
    bdm = sb("bdm", [64, 64])
    ind = sb("ind", [64, 8])
    eye4 = sb("eye4", [4, 4, 16])
    rep = sb("rep", [8, 64], BF16)
    epsc = sb("epsc", [128, 1])
    gcol = sb("gcol", [128, 16])
    fcw = sb("fcw", [128, NJ, 3])
    fcb = sb("fcb", [128, NJ])
    fcarry = sb("fcarry", [128, 4, NJ, 2])
    ecw = sb("ecw", [128, 16, 4])
    ecb = sb("ecb", [128, 16])
    ecarry = sb("ecarry", [128, 2, 16, 3])
    hcarry = sb("hcarry", [128, 2, 8])
    Cst = sb("Cst", [128, 2, 4, 257])
    mcarry = sb("mcarry", [4, 2])
    strow4 = sb("strow", [32, 4, 128])
    orow4 = sb("orow", [32, 4, 128])
    stage4 = sb("stage", [128, 4, 32])
    strow = strow4[:, 3, :]
    orow = orow4[:, 3, :]
    stage = stage4[:, 3, :]

    psA = ps("psA", [128, 3, 512])
    psB = ps("psB", [128, 3, 512])
    psM = ps("psM", [128, 512])
    psT = ps("psT", [128, 1024], BF16)

    def xfc(kc):
        return big[:, 2 * kc:2 * kc + 2, :].rearrange("p a t -> p (a t)").bitcast(F32)

    def XF(kc):
        return [('big', 2 * kc), ('big', 2 * kc + 1)]

    PSA = [('ps', 0), ('ps', 1), ('ps', 2)]
    PSB = [('ps', 3), ('ps', 4), ('ps', 5)]
    PSM = [('ps', 6)]
    PST = [('ps', 7)]
    HB = [(('hq',), 0, 2176), (('hk',), 2176, 2176), (('ktm',), 4352, 2304)]
    HB += [(('vex', t_), 6656 + t_ * 516, 516) for t_ in range(NB)]
    HB += [(('osg', t_), 11304 + t_ * 1024, 1024) for t_ in range(NB)]
    HB += [(('vw', t_), 20520 + t_ * 516, 516) for t_ in range(NB)]
    HB += [(('sT', t_), 25164 + t_ * 256, 256) for t_ in range(NB)]
    HB += [(('Cdb', t_), 27468 + t_ * 516, 516) for t_ in range(8)]
    HB += [(('mlo', t_), 31596 + t_ * 512, 512) for t_ in range(3)]
    HB += [(('qTm',), 33132, 1024)]

    def ov(lo, hi):
        return [k for (k, o, n_) in HB if o < hi and o + n_ > lo]
    for c_ in range(16):
        lo, hi = c_ * 2176, (c_ + 1) * 2176
        tk.alias[('big', 16 + c_)] = [('S1', i) for i in range(6) if i * SW_ * 4 < hi and (i + 1) * SW_ * 4 > lo] \
            + ov(lo, hi)
    for i in range(6):
        tk.alias[('S1', i)] = ov(i * SW_ * 4, (i + 1) * SW_ * 4)
    tk.alias[('xres', 0)] = [('hm', 0), ('hm', 1), ('hm', 2), ('xst', 0), ('xst', 1)]
    tk.alias[('xres', 1)] = [('gml',), ('xst', 2), ('xst', 3)]
    for k_ in range(8):
        lo, hi = k_ * 2176, (k_ + 1) * 2176
        tk.alias[('xnT', k_)] = [('yrow', hb_, q_) for hb_ in range(2) for q_ in range(4)
                                 if hb_ * 8192 + q_ * 2048 < hi and hb_ * 8192 + (q_ + 1) * 2048 > lo]

    def act(fn, reads=(), writes=()):
        tk.op('act', fn, reads, writes)

    def dve(fn, reads=(), writes=()):
        tk.op('dve', fn, reads, writes)

    def pe(fn, reads=(), writes=()):
        tk.op('pe', fn, reads, writes)

    uq = {'n': 0}

    def ld(out, in_, reads=(), writes=None, nc_ok=False):
        if writes is None:
            uq['n'] += 1
            writes = [('uq', uq['n'])]
        tk.dma('sp', out, in_, reads, writes, nc_ok)

    def ldw(out, in_, reads=(), writes=()):
        tk.dma('gq', out, in_, reads, writes)

    wstate = {'n': 0}

    def wslot():
        i = wstate['n'] % NSLOT
        wstate['n'] += 1
        return i

    def A_copy(out, in_, reads, writes):
        act(lambda e: e.activation(out=out, in_=in_, func=AF.Copy), reads, writes)

    for (t_, d_, k_) in [(identf, c_identf, 'identf'), (identb, c_identb, 'identb'), (onesb, c_onesb, 'onesb'),
                         (onesf, c_onesf, 'onesf'), (causal, c_causal, 'causal'), (bdc, c_bdc, 'bdc'),
                         (bdm, c_bd, 'bdm'), (ind, c_ind, 'ind'), (eye4, c_eye4, 'eye4'), (rep, c_rep, 'rep')]:
        ld(t_[:], d_, writes=[(k_,)])
    dve(lambda e: e.memset(epsc[:], EPS), writes=[('epsc',)])
    dve(lambda e: e.memset(fcarry[:], 0.0), writes=[('fcarry',)])
    dve(lambda e: e.memset(ecarry[:], 0.0), writes=[('ecarry',)])
    dve(lambda e: e.memset(hcarry[:], 0.0), writes=[('hcarry',)])
    dve(lambda e: e.memset(Cst[:], 0.0), writes=[('Cst',)])
    dve(lambda e: e.memset(mcarry[:], 0.0), writes=[('mcarry',)])
    dve(lambda e: e.memset(stage4[:], 0.0), writes=[('stage',), ('stage', 0), ('stage', 1), ('stage', 2), ('stage', 3)])

    def mm_fm(psg, PSG, lhs_fn, nk, rhs_fn, reads):
        def fn(e):
            ins = None
            for kc in range(nk):
                for nt, (c0, n) in enumerate(NTL):
                    ins = e.matmul(psg[:, nt, 0:n], lhsT=lhs_fn(kc), rhs=rhs_fn(kc, c0, n),
                                   start=(kc == 0), stop=(kc == nk - 1))
            return ins
        pe(fn, reads, PSG)

    def load_w2(dram2d, c0, c1):
        i = wslot()
        w = wring[i]
        v = w[:, 0:16 * 256].rearrange("p (k n) -> p k n", n=256)
        src = dram2d.rearrange("(k p) n -> p k n", p=128)
        if c1 == c0 + 128:
            ldw(v[:, :, :], src[:, :, c0:c0 + 256], writes=[('w', i)])
        else:
            ldw(v[:, :, 0:128], src[:, :, c0:c0 + 128], writes=[('w', i)])
            if c1 is not None:
                ldw(v[:, :, 128:256], src[:, :, c1:c1 + 128], writes=[('w', i)])
        return i, v

    def load_wk(dram2d, k0, nk, c0):
        i = wslot()
        w = wring[i]
        v = w[:, 0:nk * 128].rearrange("p (k n) -> p k n", n=128)
        src = dram2d.rearrange("(k p) n -> p k n", p=128)
        ldw(v[:, :, :], src[:, k0:k0 + nk, c0:c0 + 128], writes=[('w', i)])
        return i, v

    rs_state = {'n': 0}

    def resid_add(psg, PSG, m, sq=False):
        b = rs_state['n'] % 2
        rs_state['n'] += 1
        xr = xres[b]
        ld(xr[:, :], xTd[:, m, :], reads=[('xTd', m)], writes=[('xres', b)])
        dve(lambda e: e.tensor_tensor(out=xr[:, 0:1024], in0=xr[:, 0:1024],
                                      in1=psg[:, 0:2, :].rearrange("p a n -> p (a n)"), op=ALU.add),
            reads=[('xres', b)] + PSG[0:2], writes=[('xres', b)])
        dve(lambda e: e.tensor_tensor(out=xr[:, 1024:T], in0=xr[:, 1024:T], in1=psg[:, 2, 0:64], op=ALU.add),
            reads=[('xres', b)] + PSG[2:3], writes=[('xres', b)])
        ld(xTd[:, m, :], xr[:, :], reads=[('xres', b)], writes=[('xTd', m)])
        if sq:
            if m == 0:
                act(lambda e: e.activation(out=S[3][:, 0:T], in_=xr[:, 0:T], func=AF.Square),
                    reads=[('xres', b)], writes=[('S', 3)])
            else:
                act(lambda e: e.activation(out=S[4][:, 0:T], in_=xr[:, 0:T], func=AF.Square),
                    reads=[('xres', b)], writes=[('S', 4)])
                dve(lambda e: e.tensor_tensor(out=S[3][:, 0:T], in0=S[3][:, 0:T], in1=S[4][:, 0:T], op=ALU.add),
                    reads=[('S', 3), ('S', 4)], writes=[('S', 3)])
            pre_sq[0] = True
            if gfill[0]:
                dve(lambda e: e.tensor_scalar(out=xnT[:, m, :], in0=xr[:, 0:T], scalar1=gcol[:, m:m + 1], scalar2=None,
                                              op0=ALU.mult), reads=[('xres', b), ('gcol',)], writes=[('xnT', m)])
                pre_x[0] = True

    pre_sq = [False]
    pre_x = [False]
    gfill = [False]

    def prep_next_norm(g_dram):
        if g_dram is None:
            gfill[0] = False
            return
        ld(gcol[:, :], g_dram.rearrange("(k p) -> p k", p=128), writes=[('gcol',)], nc_ok=True)
        gfill[0] = True

    def load_x_tile(ti):
        BK6 = [(psA, 0, ('ps', 0)), (psA, 1, ('ps', 1)), (psA, 2, ('ps', 2)),
               (psB, 0, ('ps', 3)), (psB, 1, ('ps', 4)), (psB, 2, ('ps', 5))]
        k_ = 0
        for tb in range(NB):
            t0, n = blk(tb)
            b = tb % 3
            ld(S[2 * b][0:n, 0:1024], xin[ti, t0:t0 + n, 0:1024], writes=[('S', 2 * b)])
            ld(S[2 * b + 1][0:n, 0:1024], xin[ti, t0:t0 + n, 1024:2048], writes=[('S', 2 * b + 1)])
            for q4 in range(4):
                pt, pb_, pk = BK6[k_ % 6]
                sti = k_ % 4
                k_ += 1

                def fn(e, q4=q4, n=n, b=b, pt=pt, pb_=pb_):
                    ins = None
                    for i4 in range(4):
                        kc = q4 * 4 + i4
                        src = S[2 * b + kc // 8][0:n, (kc % 8) * 128:(kc % 8) * 128 + 128]
                        ins = e.matmul(pt[:, pb_, i4 * 128:i4 * 128 + n], lhsT=src, rhs=identf[0:n, 0:n],
                                       start=True, stop=True)
                    return ins
                pe(fn, reads=[('S', 2 * b), ('S', 2 * b + 1), ('identf',)], writes=[pk])
                xr = xres[sti // 2][:, (sti % 2) * 512:(sti % 2) * 512 + 512]
                dst = xr.rearrange("p (a n) -> p a n", n=128)[:, :, 0:n]
                src_ = pt[:, pb_, :].rearrange("p (a n) -> p a n", n=128)[:, :, 0:n]
                if k_ % 2 == 0:
                    A_copy(dst, src_, reads=[pk], writes=[('xst', sti)])
                else:
                    dve(lambda e, dst=dst, src_=src_: e.tensor_copy(out=dst, in_=src_), reads=[pk],
                        writes=[('xst', sti)])
                ld(xTd[:, q4 * 4:q4 * 4 + 4, t0:t0 + n], dst, reads=[('xst', sti)],
                   writes=[('xTd', q4 * 4 + i) for i in range(4)])

    def rmsnorm_phase(g_dram, final=False, ti=0):
        if pre_x[0] and pre_sq[0] and not final:
            pre_x[0] = False
            pre_sq[0] = False
            gfill[0] = False

            def fnq0(e):
                ins = None
                for nt, (c0, n) in enumerate(NTL):
                    ins = e.matmul(psA[:, nt, 0:n], lhsT=onesf[:, :], rhs=S[3][:, c0:c0 + n], start=True, stop=True)
                return ins
            pe(fnq0, reads=[('S', 3), ('onesf',)], writes=PSA)
            rt0 = S[2]
            act(lambda e: e.activation(out=rt0[:, 0:1024], in_=psA[:, 0:2, :].rearrange("p a n -> p (a n)"),
                                       func=AF.Sqrt, bias=epsc[:, 0:1], scale=1.0 / D),
                reads=PSA[0:2] + [('epsc',)], writes=[('S', 2)])
            act(lambda e: e.activation(out=rt0[:, 1024:T], in_=psA[:, 2, 0:64], func=AF.Sqrt, bias=epsc[:, 0:1],
                                       scale=1.0 / D), reads=PSA[2:3] + [('epsc',)], writes=[('S', 2)])
            dve(lambda e: e.reciprocal(out=rt0[:, 0:T], in_=rt0[:, 0:T]), reads=[('S', 2)], writes=[('S', 2)])
            for kc in range(16):
                dve(lambda e, kc=kc: e.tensor_tensor(out=xnT[:, kc, :], in0=xnT[:, kc, :], in1=rt0[:, 0:T],
                                                     op=ALU.mult), reads=[('xnT', kc), ('S', 2)], writes=[('xnT', kc)])
            return
        pre_x[0] = False
        gfill[0] = False
        ld(gcol[:, :], g_dram.rearrange("(k p) -> p k", p=128), writes=[('gcol',)], nc_ok=True)
        KCO = list(range(11, 16)) + list(range(0, 11))
        for kc in KCO:
            ld(xfc(kc), xTd[:, kc, :], reads=[('xTd', kc)], writes=XF(kc))
        if pre_sq[0]:
            pre_sq[0] = False

            def fnq(e):
                ins = None
                for nt, (c0, n) in enumerate(NTL):
                    ins = e.matmul(psA[:, nt, 0:n], lhsT=onesf[:, :], rhs=S[3][:, c0:c0 + n], start=True, stop=True)
                return ins
            pe(fnq, reads=[('S', 3), ('onesf',)], writes=PSA)
        else:
            for kc in range(16):
                sqb = S[kc % 2][:, 0:T // 2].bitcast(BF16)
                act(lambda e, kc=kc, sqb=sqb: e.activation(out=sqb, in_=xfc(kc), func=AF.Square),
                    reads=XF(kc), writes=[('S', kc % 2)])

                def fn(e, kc=kc, sqb=sqb):
                    ins = None
                    for nt, (c0, n) in enumerate(NTL):
                        ins = e.matmul(psA[:, nt, 0:n], lhsT=onesb[:, :], rhs=sqb[:, c0:c0 + n],
                                       start=(kc == 0), stop=(kc == 15))
                    return ins
                pe(fn, reads=[('S', kc % 2), ('onesb',)], writes=PSA)
        rt = S[2]
        act(lambda e: e.activation(out=rt[:, 0:1024], in_=psA[:, 0:2, :].rearrange("p a n -> p (a n)"),
                                   func=AF.Sqrt, bias=epsc[:, 0:1], scale=1.0 / D),
            reads=PSA[0:2] + [('epsc',)], writes=[('S', 2)])
        act(lambda e: e.activation(out=rt[:, 1024:T], in_=psA[:, 2, 0:64], func=AF.Sqrt, bias=epsc[:, 0:1],
                                   scale=1.0 / D),
            reads=PSA[2:3] + [('epsc',)], writes=[('S', 2)])
        dve(lambda e: e.reciprocal(out=rt[:, 0:T], in_=rt[:, 0:T]), reads=[('S', 2)], writes=[('S', 2)])
        for kc in KCO:
            if not final:
                dve(lambda e, kc=kc: e.scalar_tensor_tensor(out=xnT[:, kc, :], in0=xfc(kc), scalar=gcol[:, kc:kc + 1],
                                                            in1=rt[:, 0:T], op0=ALU.mult, op1=ALU.mult),
                    reads=XF(kc) + [('gcol',), ('S', 2)], writes=[('xnT', kc)])
            else:
                dve(lambda e, kc=kc: e.scalar_tensor_tensor(out=xfc(kc), in0=xfc(kc), scalar=gcol[:, kc:kc + 1],
                                                            in1=rt[:, 0:T], op0=ALU.mult, op1=ALU.mult),
                    reads=XF(kc) + [('gcol',), ('S', 2)], writes=XF(kc))
        if final:
            yrow = xnT[:, :, :].rearrange("p a t -> p (a t)").bitcast(F32)
            BK6 = [(psA, 0, ('ps', 0)), (psA, 1, ('ps', 1)), (psA, 2, ('ps', 2)),
                   (psB, 0, ('ps', 3)), (psB, 1, ('ps', 4)), (psB, 2, ('ps', 5))]
            k_ = 0
            for tb in range(NB):
                t0, n = blk(tb)
                hb = tb % 2
                for q4 in range(4):
                    pt, pb_, pk = BK6[k_ % 6]
                    k_ += 1

                    def fn(e, q4=q4, n=n, t0=t0, pt=pt, pb_=pb_):
                        ins = None
                        for i4 in range(4):
                            kc = q4 * 4 + i4
                            ins = e.matmul(pt[0:n, pb_, i4 * 128:i4 * 128 + 128], lhsT=xfc(kc)[:, t0:t0 + n],
                                           rhs=identf[:, :], start=True, stop=True)
                        return ins
                    pe(fn, reads=sum([XF(q4 * 4 + i) for i in range(4)], []) + [('identf',)], writes=[pk])
                    dst = yrow[0:n, hb * 2048 + q4 * 512: hb * 2048 + q4 * 512 + 512]
                    if k_ % 2 == 0:
                        A_copy(dst, pt[0:n, pb_, :], reads=[pk], writes=[('yrow', hb, q4)])
                    else:
                        dve(lambda e, dst=dst, pt=pt, pb_=pb_, n=n: e.tensor_copy(out=dst, in_=pt[0:n, pb_, :]),
                            reads=[pk], writes=[('yrow', hb, q4)])
                ld(y_d[ti, t0:t0 + n, :], yrow[0:n, hb * 2048:hb * 2048 + 2048],
                   reads=[('yrow', hb, q) for q in range(4)], writes=[('y_d', tb)])

    def conv_begin(K, carry, st_rows, xpad, XP, par):
        H = K - 1
        SWd = H + 8
        bs = H + 1024
        xp_s = xpad[:, bs:bs + 8 * SWd].rearrange("p (b w) -> p b w", w=SWd)
        strw = strow4[:, par, :]
        pmh = psM[:, par * 32:par * 32 + 8 * H]
        ldw(strw[0:8 * H, :], st_rows, writes=[('strow', par)])
        pe(lambda e: e.matmul(pmh, lhsT=strw[0:8 * H, :], rhs=identf[0:8 * H, 0:8 * H],
                              start=True, stop=True), reads=[('strow', par), ('identf',)], writes=PSM)
        A_copy(xpad[:, 0:H], carry, reads=[('carry',)], writes=XP)
        A_copy(xp_s[:, :, 0:H], pmh.rearrange("p (b h) -> p b h", h=H), reads=PSM, writes=XP)

    def conv_chunk(psg, PSG, K, wtap, bcol, carry, st_rows, out_rows, xpad, XP, acc, AC, wkeys, par, begun=False):
        H = K - 1
        SWd = H + 8
        bs = H + 1024
        xp_s = xpad[:, bs:bs + 8 * SWd].rearrange("p (b w) -> p b w", w=SWd)
        orw = orow4[:, par, :]
        stg = stage4[:, par, :]
        pmo = psM[0:9 * H, 128 + par * 128:256 + par * 128]
        PMO = PSM
        if not begun:
            conv_begin(K, carry, st_rows, xpad, XP, par)
        A_copy(xpad[:, H:H + 1024], psg[:, 0:2, :].rearrange("p a n -> p (a n)"), reads=PSG[0:2], writes=XP)
        A_copy(xp_s[:, :, H:H + 8], psg[:, 2, 0:64].rearrange("p (b t) -> p b t", t=8), reads=PSG[2:3], writes=XP)
        acc_s = acc[:, 1024:T].rearrange("p (b t) -> p b t", t=8)
        act(lambda e: e.activation(out=acc[:, 0:1024], in_=xpad[:, H:H + 1024], func=AF.Identity, bias=bcol,
                                   scale=wtap(K - 1)), reads=XP + wkeys, writes=AC)
        act(lambda e: e.activation(out=acc_s, in_=xp_s[:, :, H:H + 8], func=AF.Identity, bias=bcol,
                                   scale=wtap(K - 1)), reads=XP + wkeys, writes=AC)
        for k in range(K - 1):
            dve(lambda e, k=k: e.scalar_tensor_tensor(out=acc[:, 0:1024], in0=xpad[:, k:k + 1024], scalar=wtap(k),
                                                      in1=acc[:, 0:1024], op0=ALU.mult, op1=ALU.add),
                reads=XP + AC + wkeys, writes=AC)
            dve(lambda e, k=k: e.scalar_tensor_tensor(out=acc_s, in0=xp_s[:, :, k:k + 8], scalar=wtap(k),
                                                      in1=acc_s, op0=ALU.mult, op1=ALU.add),
                reads=XP + AC + wkeys, writes=AC)
        A_copy(carry, xpad[:, 1024:1024 + H], reads=XP, writes=[('carry',)])
        A_copy(stg[:, 0:8 * H].rearrange("p (b h) -> p b h", h=H), xp_s[:, :, 8:8 + H], reads=XP,
               writes=[('stage', par)])
        A_copy(stg[:, 8 * H:9 * H], xpad[:, 1024:1024 + H], reads=XP, writes=[('stage', par)])

        def finish():
            pe(lambda e: e.matmul(pmo, lhsT=stg[:, 0:9 * H], rhs=identf[:, :], start=True, stop=True),
               reads=[('stage', par), ('identf',)], writes=PMO)
            A_copy(orw[0:9 * H, :], pmo, reads=PMO, writes=[('orow', par)])
            ld(out_rows, orw[0:9 * H, :], reads=[('orow', par)])
        return acc, finish

    def ffn_phase(ti, layer):
        rmsnorm_phase(norm_ffn[layer])
        for k_ in range(3):
            ld(fcw[:, :, k_], ffn_conv_w[layer, k_].rearrange("(j p) -> p j", p=128), writes=[('fcw',)], nc_ok=True)
        ld(fcb[:, :], ffn_conv_b[layer].rearrange("(j p) -> p j", p=128), writes=[('fcw',)], nc_ok=True)
        wup = ffn_w_up[layer]
        wdn = ffn_w_down[layer]
        pend = [None]
        for half in range(2):
            for jj in range(22):
                j = half * 22 + jj
                xi, ai, gi = (0, 1, 2) if jj % 2 == 0 else (3, 4, 5)
                stf = st_f[layer, ti * 8:ti * 8 + 8, :, j * 128:j * 128 + 128].rearrange("b r c -> (b r) c")
                conv_begin(3, fcarry[:, layer, j, :], stf, S[xi], [('S', xi)], jj % 2)
                wi, wv = load_w2(wup, j * 128, DFF + j * 128)
                mm_fm(psA, PSA, lambda kc, wv=wv: wv[:, kc, 0:128], 16,
                      lambda kc, c0, n: xnT[:, kc, c0:c0 + n], [('w', wi)] + [('xnT', k) for k in range(16)])
                mm_fm(psB, PSB, lambda kc, wv=wv: wv[:, kc, 128:256], 16,
                      lambda kc, c0, n: xnT[:, kc, c0:c0 + n], [('w', wi)] + [('xnT', k) for k in range(16)])
                prev_fin = pend[0]
                acc, pend[0] = conv_chunk(
                    psA, PSA, 3, lambda k, j=j: fcw[:, j, k:k + 1], fcb[:, j:j + 1], fcarry[:, layer, j, :],
                    stf, o_f[ti, layer, :, :, j * 128:j * 128 + 128].rearrange("s r c -> (s r) c"),
                    S[xi], [('S', xi)], S[ai], [('S', ai)], [('fcw',)], jj % 2, begun=True)
                if prev_fin is not None:
                    prev_fin()
                gl = S[gi]
                act(lambda e, acc=acc, gl=gl: e.activation(out=gl[:, 0:T], in_=acc[:, 0:T], func=AF.Gelu_apprx_tanh),
                    reads=[('S', ai)], writes=[('S', gi)])
                dve(lambda e, gl=gl, jj=jj: e.tensor_tensor(out=big[:, jj, 0:1024], in0=gl[:, 0:1024],
                                                          in1=psB[:, 0:2, :].rearrange("p a n -> p (a n)"),
                                                          op=ALU.mult),
                    reads=[('S', gi)] + PSB[0:2], writes=[('big', jj)])
                dve(lambda e, gl=gl, jj=jj: e.tensor_tensor(out=big[:, jj, 1024:T], in0=gl[:, 1024:T],
                                                          in1=psB[:, 2, 0:64], op=ALU.mult),
                    reads=[('S', gi)] + PSB[2:3], writes=[('big', jj)])
            if pend[0] is not None:
                pend[0]()
                pend[0] = None
            if half == 1:
                prep_next_norm(norm_mix[layer + 1] if layer + 1 < depth else None)
            else:
                gfill[0] = False
            for m in range(16):
                wi, wv = load_wk(wdn, half * 22, 22, m * 128)
                psg, PSG = (psA, PSA) if m % 2 == 0 else (psB, PSB)
                mm_fm(psg, PSG, lambda kc, wv=wv: wv[:, kc, :], 22,
                      lambda kc, c0, n: big[:, kc, c0:c0 + n], [('w', wi)] + [('big', k) for k in range(22)])
                resid_add(psg, PSG, m, sq=(half == 1))

    def out_proj(wdram, gnext=None):
        prep_next_norm(gnext)
        for m in range(0, 16, 2):
            wi, wv = load_w2(wdram, m * 128, m * 128 + 128)
            for h2 in range(2):
                psg, PSG = (psA, PSA) if h2 == 0 else (psB, PSB)
                mm_fm(psg, PSG, lambda kc, wv=wv, h2=h2: wv[:, kc, h2 * 128:h2 * 128 + 128], 16,
                      lambda kc, c0, n: big[:, kc, c0:c0 + n], [('w', wi)] + [('big', k) for k in range(16)])
                resid_add(psg, PSG, m + h2, sq=True)

    from_even = {}

    def even_mixer(ti, layer):
        j = layer // 2
        rmsnorm_phase(norm_mix[layer])
        w = w_in_even[j]
        ALLX = [('xnT', k) for k in range(16)]
        rhsx = lambda kc, c0, n: xnT[:, kc, c0:c0 + n]
        for k_ in range(4):
            ld(ecw[:, :, k_], conv_even_w[j, k_].rearrange("(c p) -> p c", p=128), writes=[('ecw',)], nc_ok=True)
        ld(ecb[:, :], conv_even_b[j].rearrange("(c p) -> p c", p=128), writes=[('ecw',)], nc_ok=True)
        ld(rgp[:, 0, :], rg_ba[j].rearrange("(c p) -> p c", p=128), writes=[('rgp',)], nc_ok=True)
        ld(rgp[:, 1, :], rg_bx[j].rearrange("(c p) -> p c", p=128), writes=[('rgp',)], nc_ok=True)
        ld(rgp[:, 2, :], rg_lambda[j].rearrange("(c p) -> p c", p=128), writes=[('rgp',)], nc_ok=True)
        act(lambda e: e.activation(out=rgp[:, 3, :], in_=rgp[:, 2, :], func=AF.Exp, scale=-1.0),
            reads=[('rgp',)], writes=[('rgp',)])
        act(lambda e: e.activation(out=rgp[:, 3, :], in_=rgp[:, 3, :], func=AF.Ln, bias=onec[:, 0:1], scale=1.0),
            reads=[('rgp',), ('onec',)], writes=[('rgp',)])
        dve(lambda e: e.tensor_scalar(out=rgp[:, 4, :], in0=rgp[:, 3, :], scalar1=-8.0, scalar2=None, op0=ALU.mult),
            reads=[('rgp',)], writes=[('rgp',)])
        dve(lambda e: e.tensor_scalar(out=rgp[:, 5, :], in0=rgp[:, 3, :], scalar1=-16.0, scalar2=None, op0=ALU.mult),
            reads=[('rgp',)], writes=[('rgp',)])
        dve(lambda e: e.memset(wbd[:], 0.0), writes=[('wbd',)])
        for gi_, wsrc in enumerate([rg_wa[j], rg_wx[j]]):
            for half in range(2):
                ldw(wbd[half * 64:half * 64 + 64, :, gi_, half * 64:half * 64 + 64],
                    wsrc.rearrange("(c two) i o -> two i c o", two=2)[half], writes=[('wbd',)])

        R16 = big[:, 16:32, :].rearrange("p a t -> p (a t)").bitcast(F32)
        SETS = [
            [(S[i], [('S', i)]) for i in range(6)],
            [(R16[:, i * SW_:(i + 1) * SW_], [('S1', i)]) for i in range(6)],
        ]

        def rg_stage1(c):
            st_ = SETS[c % 2]
            (xpad, XP), (xcf, XC), (gg, GG) = st_[0], st_[1], st_[2]
            stc = st_conv[j, ti * 8:ti * 8 + 8, :, c * 128:c * 128 + 128].rearrange("b r c -> (b r) c")
            conv_begin(4, ecarry[:, j, c, :], stc, xpad, XP, c % 2)
            wi, wv = load_w2(w, c * 128, 2048 + c * 128)
            mm_fm(psA, PSA, lambda kc, wv=wv: wv[:, kc, 0:128], 16, rhsx, [('w', wi)] + ALLX)
            mm_fm(psB, PSB, lambda kc, wv=wv: wv[:, kc, 128:256], 16, rhsx, [('w', wi)] + ALLX)
            parh = 2 + c % 2
            strwh = strow4[:, parh, :]
            ldw(strwh[0:8, :], st_h[j, ti * 8:ti * 8 + 8, c * 128:c * 128 + 128], writes=[('strow', parh)])
            pT32 = psT[:, :].bitcast(F32)
            pe(lambda e: e.matmul(pT32[:, 384:392], lhsT=strwh[0:8, :], rhs=identf[0:8, 0:8], start=True, stop=True),
               reads=[('strow', parh), ('identf',)], writes=PST)
            A_copy(stage4[:, parh, 0:8], pT32[:, 384:392], reads=PST, writes=[('stage', parh)])
            xc, fin = conv_chunk(psA, PSA, 4, lambda k, c=c: ecw[:, c, k:k + 1], ecb[:, c:c + 1], ecarry[:, j, c, :],
                                 stc, o_conv[ti, j, :, :, c * 128:c * 128 + 128].rearrange("s r c -> (s r) c"),
                                 xpad, XP, xcf, XC, [('ecw',)], c % 2, begun=True)
            act(lambda e: e.activation(out=gg[:, 0:1024], in_=psB[:, 0:2, :].rearrange("p a n -> p (a n)"),
                                       func=AF.Gelu_apprx_tanh), reads=PSB[0:2], writes=GG)
            act(lambda e: e.activation(out=gg[:, 1024:T], in_=psB[:, 2, 0:64], func=AF.Gelu_apprx_tanh), reads=PSB[2:3],
                writes=GG)
            return fin

        def rg_stage3(c):
            st_ = SETS[c % 2]
            (xpad, XP), (xc, XC), (gg, GG), (r_, RR), (i_, II), (a_, AA) = st_
            par = 2 + c % 2
            strw = strow4[:, par, :]
            orw = orow4[:, par, :]
            stg = stage4[:, par, :]
            pmx = psT[:, :].bitcast(F32)[:, 0:128]
            PMX = PST
            xcb = xpad[:, 0:T // 2].bitcast(BF16)
            dve(lambda e: e.tensor_copy(out=xcb, in_=xc[:, 0:T]), reads=XC, writes=XP)
            for gi_, (psg, PSG) in enumerate([(psA, PSA), (psB, PSB)]):
                def fn(e, psg=psg, gi_=gi_):
                    ins = None
                    for nt, (c0, n) in enumerate(NTL):
                        ins = e.matmul(psg[:, nt, 0:n], lhsT=wbd[:, c, gi_, :], rhs=xcb[:, c0:c0 + n],
                                       start=True, stop=True)
                    return ins
                pe(fn, reads=[('wbd',)] + XP, writes=PSG)
            for (dst, DD, psg, PSG, bi) in [(r_, RR, psA, PSA, 0), (i_, II, psB, PSB, 1)]:
                act(lambda e, dst=dst, psg=psg, bi=bi: e.activation(
                    out=dst[:, 0:1024], in_=psg[:, 0:2, :].rearrange("p a n -> p (a n)"), func=AF.Sigmoid,
                    bias=rgp[:, bi, c:c + 1], scale=1.0), reads=PSG[0:2] + [('rgp',)], writes=DD)
                act(lambda e, dst=dst, psg=psg, bi=bi: e.activation(
                    out=dst[:, 1024:T], in_=psg[:, 2, 0:64], func=AF.Sigmoid, bias=rgp[:, bi, c:c + 1], scale=1.0),
                    reads=PSG[2:3] + [('rgp',)], writes=DD)
            act(lambda e: e.activation(out=a_[:, 0:T], in_=r_[:, 0:T], func=AF.Exp, scale=rgp[:, 4, c:c + 1]),
                reads=RR + [('rgp',)], writes=AA)
            act(lambda e: e.activation(out=r_[:, 0:T], in_=r_[:, 0:T], func=AF.Exp, scale=rgp[:, 5, c:c + 1]),
                reads=RR + [('rgp',)], writes=RR)
            dve(lambda e: e.tensor_scalar(out=r_[:, 0:T], in0=r_[:, 0:T], scalar1=-1.0, scalar2=1.0, op0=ALU.mult,
                                          op1=ALU.add), reads=RR, writes=RR)
            act(lambda e: e.activation(out=r_[:, 0:T], in_=r_[:, 0:T], func=AF.Sqrt), reads=RR, writes=RR)
            dve(lambda e: e.tensor_tensor(out=i_[:, 0:T], in0=i_[:, 0:T], in1=r_[:, 0:T], op=ALU.mult),
                reads=RR + II, writes=II)
            dve(lambda e: e.tensor_tensor(out=i_[:, 0:T], in0=i_[:, 0:T], in1=xc[:, 0:T], op=ALU.mult),
                reads=XC + II, writes=II)
            dve(lambda e: e.tensor_tensor_scan(out=r_[:, 0:1024], data0=a_[:, 0:1024], data1=i_[:, 0:1024],
                                               initial=hcarry[:, j, c:c + 1], op0=ALU.mult, op1=ALU.add),
                reads=AA + II + [('hcarry',)], writes=RR)
            for b in range(8):
                dve(lambda e, b=b: e.tensor_tensor_scan(out=r_[:, 1024 + b * 8:1032 + b * 8],
                                                        data0=a_[:, 1024 + b * 8:1032 + b * 8],
                                                        data1=i_[:, 1024 + b * 8:1032 + b * 8],
                                                        initial=stg[:, b:b + 1], op0=ALU.mult, op1=ALU.add),
                    reads=AA + II + [('stage', par)], writes=RR)
            dve(lambda e: e.tensor_copy(out=hcarry[:, j, c:c + 1], in_=r_[:, 1023:1024]), reads=RR,
                writes=[('hcarry',)])
            A_copy(stg[:, 16:24], r_[:, 1024:T].rearrange("p (b t) -> p b t", t=8)[:, :, 7], reads=RR,
                   writes=[('stage', par)])
            A_copy(stg[:, 24:25], r_[:, 1023:1024], reads=RR, writes=[('stage', par)])
            dve(lambda e: e.tensor_tensor(out=big[:, c, :], in0=gg[:, 0:T], in1=r_[:, 0:T], op=ALU.mult),
                reads=GG + RR, writes=[('big', c)])

            def finishH():
                pe(lambda e: e.matmul(pmx[0:9, 0:128], lhsT=stg[:, 16:25], rhs=identf[:, :], start=True, stop=True),
                   reads=[('stage', par), ('identf',)], writes=PMX)
                A_copy(orw[0:9, :], pmx[0:9, 0:128], reads=PMX, writes=[('orow', par)])
                ld(o_h[ti, j, :, c * 128:c * 128 + 128], orw[0:9, :], reads=[('orow', par)])
            return finishH

        fin = {0: rg_stage1(0)}
        finH = {}
        for c in range(8):
            if c + 1 < 8:
                fin[c + 1] = rg_stage1(c + 1)
            fin[c]()
            finH[c] = rg_stage3(c)
            if c >= 1:
                finH[c - 1]()
        finH[7]()

        wi, wv = load_w2(w, 5120 - 120, None)
        irow, frow = mlr[0], mlr[1]
        for gi_, dst in enumerate([irow, frow]):
            def fn(e, gi_=gi_, wv=wv):
                ins = None
                for kc in range(16):
                    for nt, (c0, n) in enumerate(NTL):
                        ins = e.matmul(psA[0:4, nt, 0:n], lhsT=wv[:, kc, 120 + gi_ * 4:124 + gi_ * 4],
                                       rhs=xnT[:, kc, c0:c0 + n], start=(kc == 0), stop=(kc == 15))
                return ins
            pe(fn, reads=[('w', wi)] + ALLX, writes=PSA)
            A_copy(dst[:, 0:1024], psA[0:4, 0:2, :].rearrange("p a n -> p (a n)"), reads=PSA[0:2],
                   writes=[('S', gi_)])
            A_copy(dst[:, 1024:T], psA[0:4, 2, 0:64], reads=PSA[2:3], writes=[('S', gi_)])
        ld(mlb[:, 0:2], ml_gate_b[j].rearrange("g h -> h g"), writes=[('mlb',)], nc_ok=True)
        dve(lambda e: e.tensor_scalar(out=mlb[:, 2:3], in0=mlb[:, 1:2], scalar1=-1.0, scalar2=None, op0=ALU.mult),
            reads=[('mlb',)], writes=[('mlb',)])
        MLR = [('S', i) for i in range(6)]
        G, A_, MU, WR = mlr[2], mlr[3], mlr[4], mlr[5]
        act(lambda e: e.activation(out=frow[:, 0:T], in_=frow[:, 0:T], func=AF.Exp, bias=mlb[:, 2:3], scale=-1.0),
            reads=MLR + [('mlb',)], writes=[('S', 1)])
        act(lambda e: e.activation(out=frow[:, 0:T], in_=frow[:, 0:T], func=AF.Ln, bias=onec[0:4, 0:1], scale=1.0),
            reads=MLR + [('onec',)], writes=[('S', 1)])
        ones4 = S[5][0:4, 0:1024]
        dve(lambda e: e.memset(ones4, 1.0), writes=[('S', 5)])
        dve(lambda e: e.tensor_tensor_scan(out=G[:, 0:1024], data0=ones4[:, 0:1024], data1=frow[:, 0:1024],
                                           initial=0.0, op0=ALU.mult, op1=ALU.add),
            reads=MLR + [('S', 5)], writes=[('S', 2)])
        for b in range(8):
            sl = slice(1024 + b * 8, 1032 + b * 8)
            dve(lambda e, sl=sl: e.tensor_tensor_scan(out=G[:, sl], data0=ones4[:, 0:8], data1=frow[:, sl],
                                                      initial=0.0, op0=ALU.mult, op1=ALU.add),
                reads=MLR + [('S', 5)], writes=[('S', 2)])
        dve(lambda e: e.scalar_tensor_tensor(out=A_[:, 0:T], in0=irow[:, 0:T], scalar=mlb[:, 0:1], in1=G[:, 0:T],
                                             op0=ALU.add, op1=ALU.add), reads=MLR + [('mlb',)], writes=[('S', 3)])
        ld(m0s[:, :], st_m[j, ti * 8:ti * 8 + 8, :].rearrange("b h -> h b"), writes=[('m0s',)], nc_ok=True)
        dve(lambda e: e.tensor_tensor_scan(out=MU[:, 0:1024], data0=ones4[:, 0:1024], data1=A_[:, 0:1024],
                                           initial=mcarry[:, j:j + 1], op0=ALU.mult, op1=ALU.max),
            reads=MLR + [('S', 5), ('mcarry',)], writes=[('S', 4)])
        for b in range(8):
            sl = slice(1024 + b * 8, 1032 + b * 8)
            dve(lambda e, sl=sl, b=b: e.tensor_tensor_scan(out=MU[:, sl], data0=ones4[:, 0:8], data1=A_[:, sl],
                                                           initial=m0s[:, b:b + 1], op0=ALU.mult, op1=ALU.max),
                reads=MLR + [('S', 5), ('m0s',)], writes=[('S', 4)])
        A_copy(mue[:, 0:8], MU[:, 0:1024].rearrange("p (n t) -> p n t", t=128)[:, :, 127], reads=MLR,
               writes=[('mue',)])
        A_copy(mue[:, 8:16], MU[:, 1024:T].rearrange("p (n t) -> p n t", t=8)[:, :, 7], reads=MLR, writes=[('mue',)])
        A_copy(muc[:, 0:1], mcarry[:, j:j + 1], reads=[('mcarry',)], writes=[('muc',)])
        A_copy(muc[:, 1:8], mue[:, 0:7], reads=[('mue',)], writes=[('muc',)])
        A_copy(muc[:, 8:16], m0s[:, 0:8], reads=[('m0s',)], writes=[('muc',)])
        dve(lambda e: e.tensor_tensor(out=dec[:, :], in0=muc[:, :], in1=mue[:, :], op=ALU.subtract),
            reads=[('muc',), ('mue',)], writes=[('dec',)])
        act(lambda e: e.activation(out=dec[:, :], in_=dec[:, :], func=AF.Exp), reads=[('dec',)], writes=[('dec',)])
        A_copy(mnew[:, 0:8], G[:, 1024:T].rearrange("p (n t) -> p n t", t=8)[:, :, 7], reads=MLR, writes=[('mnew',)])
        A_copy(mnew[:, 8:9], G[:, 1023:1024], reads=MLR, writes=[('mnew',)])
        dve(lambda e: e.tensor_tensor(out=mnew[:, 0:8], in0=mue[:, 8:16], in1=mnew[:, 0:8], op=ALU.subtract),
            reads=[('mue',), ('mnew',)], writes=[('mnew',)])
        dve(lambda e: e.tensor_tensor(out=mnew[:, 8:9], in0=mue[:, 7:8], in1=mnew[:, 8:9], op=ALU.subtract),
            reads=[('mue',), ('mnew',)], writes=[('mnew',)])
        dve(lambda e: e.tensor_copy(out=mcarry[:, j:j + 1], in_=mnew[:, 8:9]), reads=[('mnew',), ('muc',)],
            writes=[('mcarry',)])
        ld(o_m[ti, j].rearrange("s h -> h s"), mnew[:, 0:9], reads=[('mnew',)], nc_ok=True)
        WF = S[0][0:36, 0:T]
        dve(lambda e: e.memset(WF, 0.0), reads=MLR, writes=[('S', 0)])
        dve(lambda e: e.tensor_tensor(out=WR[:, 0:1024].rearrange("p (n t) -> p n t", t=128),
                                      in0=A_[:, 0:1024].rearrange("p (n t) -> p n t", t=128),
                                      in1=mue[:, 0:8].unsqueeze(2).to_broadcast([4, 8, 128]), op=ALU.subtract),
            reads=MLR + [('mue',)], writes=[('S', 5)])
        dve(lambda e: e.tensor_tensor(out=WR[:, 1024:T].rearrange("p (n t) -> p n t", t=8),
                                      in0=A_[:, 1024:T].rearrange("p (n t) -> p n t", t=8),
                                      in1=mue[:, 8:16].unsqueeze(2).to_broadcast([4, 8, 8]), op=ALU.subtract),
            reads=MLR + [('mue',)], writes=[('S', 5)])
        act(lambda e: e.activation(out=WF[0:4, 0:T], in_=WR[:, 0:T], func=AF.Exp), reads=MLR, writes=[('S', 0)])
        dve(lambda e: e.tensor_tensor(out=WR[:, 0:1024].rearrange("p (n t) -> p n t", t=128),
                                      in0=G[:, 0:1024].rearrange("p (n t) -> p n t", t=128),
                                      in1=mue[:, 0:8].unsqueeze(2).to_broadcast([4, 8, 128]), op=ALU.subtract),
            reads=MLR + [('mue',), ('S', 0)], writes=[('S', 5)])
        dve(lambda e: e.tensor_tensor(out=WR[:, 1024:T].rearrange("p (n t) -> p n t", t=8),
                                      in0=G[:, 1024:T].rearrange("p (n t) -> p n t", t=8),
                                      in1=mue[:, 8:16].unsqueeze(2).to_broadcast([4, 8, 8]), op=ALU.subtract),
            reads=MLR + [('mue',), ('S', 0)], writes=[('S', 5)])
        act(lambda e: e.activation(out=WF[32:36, 0:T], in_=WR[:, 0:T], func=AF.Exp), reads=MLR, writes=[('S', 0)])
        for tb in range(NB):
            t0, n = blk(tb)
            pe(lambda e, t0=t0, n=n: e.matmul(psM[0:n, 0:36], lhsT=WF[0:36, t0:t0 + n], rhs=identf[0:36, 0:36],
                                              start=True, stop=True), reads=[('S', 0), ('identf',)], writes=PSM)
            A_copy(wfc[0:n, tb, :], psM[0:n, 0:36], reads=PSM, writes=[('wfc',)])
        dve(lambda e: e.tensor_tensor(out=decx[:, :, :], in0=eye4[:, :, :],
                                      in1=dec[:, :].unsqueeze(1).to_broadcast([4, 4, 16]), op=ALU.mult),
            reads=[('dec',), ('eye4',)], writes=[('decx',)])
        pe(lambda e: e.matmul(psM[:, 0:64], lhsT=onesf[0:4, :], rhs=decx[:, :, :].rearrange("p a b -> p (a b)"),
                              start=True, stop=True), reads=[('decx',), ('onesf',)], writes=PSM)
        A_copy(decb[:, :, :].rearrange("p a b -> p (a b)"), psM[:, 0:64], reads=PSM, writes=[('decb',)])
        ld(gml[:, :], ml_norm_g[j].rearrange("h d -> (h d)").partition_broadcast(128), writes=[('gml',)])

        RB = big[:, 16:32, :].rearrange("p a t -> p (a t)")
        RF = RB.bitcast(F32)

        def rb(off, n):
            return RB[:, off // 2: off // 2 + n]

        qT = rb(0, T)
        kT = rb(2176, T)
        ktm = rb(4352, NB * 128).rearrange("p (b d) -> p b d", d=128)
        vex = rb(6656, NB * 258).rearrange("p (b d) -> p b d", d=258)
        osg = RF[:, 11304 // 4: 11304 // 4 + NB * 256].rearrange("p (b d) -> p b d", d=256)
        vwA = rb(20520, NB * 258).rearrange("p (b d) -> p b d", d=258)
        sTA = rb(25164, NB * 128).rearrange("p (b d) -> p b d", d=128)
        CdbA = rb(27468, 8 * 258).rearrange("p (b d) -> p b d", d=258)
        mloA = rb(31596, 3 * 256).rearrange("p (b d) -> p b d", d=256)
        qTm = rb(33132, 512).rearrange("p (b t) -> p b t", t=64)
        hmA = xres[0][:, 0:768].rearrange("p (b d) -> p b d", d=256)
        KQ = [('hq',), ('hk',)]
        KTM = [('ktm',)]
        BANKS6 = [(psA, 0), (psA, 1), (psA, 2), (psB, 0), (psB, 1), (psB, 2)]

        def bk(i):
            t_, b_ = BANKS6[i]
            return t_, b_, [('ps', b_ if t_ is psA else 3 + b_)]

        def head(h):
            wi, wv = load_w2(w, 1024 + h * 128, 1536 + h * 128)
            fins = []
            for (psg, PSG, half, cc, si) in [(psA, PSA, 0, 8 + h, 0), (psB, PSB, 1, 12 + h, 3)]:
                stc = st_conv[j, ti * 8:ti * 8 + 8, :, cc * 128:cc * 128 + 128].rearrange("b r c -> (b r) c")
                conv_begin(4, ecarry[:, j, cc, :], stc, S[si], [('S', si)], half)
                mm_fm(psg, PSG, lambda kc, wv=wv, half=half: wv[:, kc, half * 128:half * 128 + 128], 16, rhsx,
                      [('w', wi)] + ALLX)
                acc, fin_ = conv_chunk(psg, PSG, 4, lambda k, cc=cc: ecw[:, cc, k:k + 1], ecb[:, cc:cc + 1],
                                       ecarry[:, j, cc, :], stc,
                                       o_conv[ti, j, :, :, cc * 128:cc * 128 + 128].rearrange("s r c -> (s r) c"),
                                       S[si], [('S', si)], S[si + 1], [('S', si + 1)], [('ecw',)], half, begun=True)
                fins.append(fin_)
                if half == 0:
                    act(lambda e, acc=acc: e.activation(out=qT[:, 0:T], in_=acc[:, 0:T], func=AF.Silu),
                        reads=[('S', si + 1)], writes=[('hq',)])
                else:
                    act(lambda e, acc=acc: e.activation(out=S[5][:, 0:T], in_=acc[:, 0:T], func=AF.Silu),
                        reads=[('S', si + 1)], writes=[('S', 5)])
                    dve(lambda e: e.tensor_scalar(out=kT[:, 0:T], in0=S[5][:, 0:T], scalar1=128 ** -0.5,
                                                  scalar2=None, op0=ALU.mult), reads=[('S', 5)], writes=[('hk',)])
            wiv, wvv = load_w2(w, 3072 + h * 256, 3072 + h * 256 + 128)
            wio, wvo = load_w2(w, 4096 + h * 256, 4096 + h * 256 + 128)
            dve(lambda e: e.memset(vex[:, :, 256:257], 1.0), reads=[], writes=[('vex', t_) for t_ in range(NB)])

            def vo_mm(tb):
                t0, n = blk(tb)
                for (wi2, wv2, psg, pb) in [(wiv, wvv, psA, 0), (wio, wvo, psB, 3)]:
                    def fn(e, wv2=wv2, t0=t0, n=n, psg=psg, tb=tb):
                        ins = None
                        for kc in range(16):
                            ins = e.matmul(psg[0:n, tb % 3, 0:256], lhsT=xnT[:, kc, t0:t0 + n], rhs=wv2[:, kc, :],
                                           start=(kc == 0), stop=(kc == 15))
                        return ins
                    pe(fn, reads=[('w', wi2)] + ALLX, writes=[('ps', pb + tb % 3)])

            def vo_ev(tb):
                t0, n = blk(tb)
                A_copy(vex[0:n, tb, 0:256], psA[0:n, tb % 3, 0:256], reads=[('ps', tb % 3)], writes=[('vex', tb)])
                act(lambda e, n=n, tb=tb: e.activation(out=osg[0:n, tb, :], in_=psB[0:n, tb % 3, 0:256],
                                                       func=AF.Sigmoid), reads=[('ps', 3 + tb % 3)],
                    writes=[('osg', tb)])
            for tb in range(NB):
                vo_mm(tb)
                if tb >= 2:
                    vo_ev(tb - 2)
            vo_ev(NB - 2)
            vo_ev(NB - 1)
            for f_ in fins:
                f_()
            def fnT(e):
                ins = None
                for tb in range(8):
                    ins = e.transpose(out=psT[:, tb * 128:tb * 128 + 128], in_=kT[:, tb * 128:tb * 128 + 128],
                                      identity=identb[:, :])
                return ins
            pe(fnT, reads=KQ + [('identb',)], writes=PST)
            A_copy(ktm[:, 0:8, :], psT[:, :].rearrange("p (b d) -> p b d", d=128), reads=PST, writes=KTM)
            pe(lambda e: e.transpose(out=psT[0:64, 0:128], in_=kT[:, 1024:T], identity=identb[:, :]),
               reads=KQ + [('identb',)], writes=PST)
            A_copy(ktm[0:64, 8, :], psT[0:64, 0:128], reads=PST, writes=KTM)
            def sc_mm(tb):
                t0, n = blk(tb)
                pt, pb_, PK = bk(tb % 6)
                pe(lambda e, t0=t0, n=n, pt=pt, pb_=pb_: e.matmul(pt[0:n, pb_, 0:n], lhsT=kT[:, t0:t0 + n],
                                                                 rhs=qT[:, t0:t0 + n], start=True, stop=True),
                   reads=KQ, writes=PK)

            def sc_ev(tb):
                t0, n = blk(tb)
                pt, pb_, PK = bk(tb % 6)
                mask = causal if tb < 8 else bdc
                mkey = ('causal',) if tb < 8 else ('bdc',)
                dve(lambda e, n=n, mask=mask, pt=pt, pb_=pb_, tb=tb: e.tensor_tensor(
                    out=sTA[0:n, tb, 0:n], in0=pt[0:n, pb_, 0:n], in1=mask[0:n, 0:n], op=ALU.mult),
                    reads=PK + [mkey], writes=[('sT', tb)])
            for tb in range(NB):
                sc_mm(tb)
                if tb >= 4:
                    sc_ev(tb - 4)
            for tb in range(NB - 4, NB):
                sc_ev(tb)
            for tb in range(NB):
                t0, n = blk(tb)
                dve(lambda e, n=n, tb=tb: e.tensor_scalar(out=vwA[0:n, tb, 0:257], in0=vex[0:n, tb, 0:257],
                                                          scalar1=wfc[0:n, tb, h:h + 1], scalar2=None, op0=ALU.mult),
                    reads=[('vex', tb), ('wfc',)], writes=[('vw', tb)])
            Cpp = [Cst[:, j, h, :], Ctmp[:, :]]
            CK = [[('Cst',)], [('Ctmp',)]]

            def kv_mm(tb):
                pe(lambda e, tb=tb: e.matmul(psA[:, tb % 3, 0:257], lhsT=ktm[:, tb, :], rhs=vwA[:, tb, 0:257],
                                             start=True, stop=True), reads=KTM + [('vw', tb)],
                   writes=[('ps', tb % 3)])

            def chain(tb):
                prev, PK_ = Cpp[tb % 2], CK[tb % 2]
                new, NK_ = Cpp[(tb + 1) % 2], CK[(tb + 1) % 2]
                act(lambda e, tb=tb, prev=prev: e.activation(out=CdbA[:, tb, 0:257], in_=prev, func=AF.Copy,
                                                             scale=decb[:, h, tb:tb + 1]),
                    reads=PK_ + [('decb',)], writes=[('Cdb', tb)])
                dve(lambda e, tb=tb, prev=prev, new=new: e.scalar_tensor_tensor(
                    out=new, in0=prev, scalar=decb[:, h, tb:tb + 1], in1=psA[:, tb % 3, 0:257], op0=ALU.mult,
                    op1=ALU.add), reads=PK_ + [('decb',), ('ps', tb % 3)], writes=NK_)
            for tb in range(3):
                kv_mm(tb)
            for tb in range(8):
                chain(tb)
                if tb + 3 < 8:
                    kv_mm(tb + 3)
            ld(o_C[ti, j, 8, h], Cst[:, j, h, 0:256], reads=[('Cst',)])
            dve(lambda e: e.tensor_copy(out=nout[:, 8, h:h + 1], in_=Cst[:, j, h, 256:257]), reads=[('Cst',)],
                writes=[('nout',)])

            def norm_group(items):
                for (tb, i, pt, pb_, PK) in items:
                    t0, n = blk(tb)
                    act(lambda e, n=n, i=i, pt=pt, pb_=pb_: e.activation(out=sm[0:n, i, 0:1],
                                                                         in_=pt[0:n, pb_, 256:257], func=AF.Abs),
                        reads=PK, writes=[('sm', i)])
                for (tb, i, pt, pb_, PK) in items:
                    t0, n = blk(tb)
                    dve(lambda e, n=n, i=i, tb=tb: e.tensor_tensor(out=sm[0:n, i, 0:1], in0=sm[0:n, i, 0:1],
                                                                   in1=wfc[0:n, tb, 32 + h:33 + h], op=ALU.max),
                        reads=[('sm', i), ('wfc',)], writes=[('sm', i)])
                    dve(lambda e, n=n, i=i: e.reciprocal(out=sm[0:n, i, 1:2], in_=sm[0:n, i, 0:1]),
                        reads=[('sm', i)], writes=[('sm', i)])
                for (tb, i, pt, pb_, PK) in items:
                    t0, n = blk(tb)
                    act(lambda e, n=n, i=i, pt=pt, pb_=pb_: e.activation(out=hmA[0:n, i, :], in_=pt[0:n, pb_, 0:256],
                                                                         func=AF.Copy, scale=sm[0:n, i, 1:2]),
                        reads=PK + [('sm', i)], writes=[('hm', i)])
                for (tb, i, pt, pb_, PK) in items:
                    t0, n = blk(tb)
                    act(lambda e, n=n, i=i: e.activation(out=mloA[0:n, i, :], in_=hmA[0:n, i, :], func=AF.Square,
                                                         accum_out=sm[0:n, i, 2:3]), reads=[('hm', i)],
                        writes=[('sm', i), ('mlo', i)])
                for (tb, i, pt, pb_, PK) in items:
                    t0, n = blk(tb)
                    act(lambda e, n=n, i=i: e.activation(out=sm[0:n, i, 3:4], in_=sm[0:n, i, 2:3], func=AF.Sqrt,
                                                         bias=epsc[0:n, 0:1], scale=1.0 / 256),
                        reads=[('sm', i), ('epsc',)], writes=[('sm', i)])
                for (tb, i, pt, pb_, PK) in items:
                    t0, n = blk(tb)
                    dve(lambda e, n=n, i=i: e.reciprocal(out=sm[0:n, i, 4:5], in_=sm[0:n, i, 3:4]),
                        reads=[('sm', i)], writes=[('sm', i)])
                    dve(lambda e, n=n, i=i: e.scalar_tensor_tensor(out=hmA[0:n, i, :], in0=hmA[0:n, i, :],
                                                                  scalar=sm[0:n, i, 4:5],
                                                                  in1=gml[0:n, h * 256:h * 256 + 256], op0=ALU.mult,
                                                                  op1=ALU.mult),
                        reads=[('hm', i), ('sm', i), ('gml',)], writes=[('hm', i)])
                    dve(lambda e, n=n, i=i, tb=tb: e.tensor_tensor(out=mloA[0:n, i, :], in0=hmA[0:n, i, :],
                                                                   in1=osg[0:n, tb, :], op=ALU.mult),
                        reads=[('hm', i), ('osg', tb)], writes=[('mlo', i)])

                def fnT2(e):
                    ins = None
                    for (tb, i, pt, pb_, PK) in items:
                        t0, n = blk(tb)
                        for half in range(2):
                            ins = e.transpose(out=psT[:, (2 * i + half) * 128:(2 * i + half) * 128 + n],
                                              in_=mloA[0:n, i, half * 128:half * 128 + 128],
                                              identity=identb[0:n, 0:n])
                    return ins
                pe(fnT2, reads=[('mlo', i_) for (_, i_, _, _, _) in items] + [('identb',)], writes=PST)
                for (tb, i, pt, pb_, PK) in items:
                    t0, n = blk(tb)
                    for half in range(2):
                        cidx = 8 + 2 * h + half
                        A_copy(big[:, cidx, t0:t0 + n], psT[:, (2 * i + half) * 128:(2 * i + half) * 128 + n],
                               reads=PST, writes=[('big', cidx)])

            for grp in [[0, 1, 2], [3, 4, 5], [6, 7]]:
                items = []
                for tb in grp:
                    t0, n = blk(tb)

                    def fnN(e, t0=t0, n=n, tb=tb):
                        e.matmul(psB[0:n, tb % 3, 0:257], lhsT=qT[:, t0:t0 + n], rhs=CdbA[:, tb, 0:257], start=True,
                                 stop=False)
                        return e.matmul(psB[0:n, tb % 3, 0:257], lhsT=sTA[0:n, tb, 0:n], rhs=vwA[0:n, tb, 0:257],
                                        start=False, stop=True)
                    pe(fnN, reads=KQ + [('Cdb', tb), ('sT', tb), ('vw', tb)], writes=[('ps', 3 + tb % 3)])
                    items.append((tb, tb % 3, psB, tb % 3, [('ps', 3 + tb % 3)]))
                norm_group(items)
            tb = 8
            dve(lambda e: e.memset(qTm[:, :, :], 0.0), reads=[], writes=[('qTm',)])
            for b in range(8):
                dve(lambda e, b=b: e.tensor_copy(out=qTm[:, b, b * 8:b * 8 + 8], in_=qT[:, 1024 + b * 8:1032 + b * 8]),
                    reads=KQ, writes=[('qTm',)])
            dve(lambda e: e.tensor_scalar(out=wbs[0:64, :], in0=ind[0:64, :], scalar1=wfc[0:64, tb, h:h + 1],
                                          scalar2=None, op0=ALU.mult), reads=[('ind',), ('wfc',)], writes=[('wbs',)])
            NCI = len(Cin)

            def s_load(b):
                cb_ = b % NCI
                ldw(Cin[cb_][:, 0:256], st_C[j, ti * 8 + b, h], writes=[('Cin', cb_)])

            def s_comp(b):
                cb_ = b % NCI
                dve(lambda e: e.tensor_copy(out=Cin[cb_][:, 256:257], in_=nin[:, b, h:h + 1]),
                    reads=[('nin',)], writes=[('Cin', cb_)])
                act(lambda e: e.activation(out=CdbA[:, b, 0:257], in_=Cin[cb_][:, 0:257], func=AF.Copy,
                                           scale=decb[:, h, 8 + b:9 + b]),
                    reads=[('Cin', cb_), ('decb',)], writes=[('Cdb', b)])
                pe(lambda e: e.matmul(psA[0:64, 0, 0:257], lhsT=qTm[:, b, :], rhs=CdbA[:, b, 0:257],
                                      start=(b == 0), stop=False), reads=[('qTm',), ('Cdb', b)], writes=[('ps', 0)])
                dve(lambda e: e.tensor_scalar(out=vwb[0:64, b % 2, 0:257], in0=vex[0:64, tb, 0:257],
                                              scalar1=wbs[0:64, b:b + 1], scalar2=None, op0=ALU.mult),
                    reads=[('vex', tb), ('wbs',)], writes=[('vwb', b % 2)])
                pe(lambda e: e.matmul(psB[:, b % 3, 0:257], lhsT=ktm[0:64, tb, :], rhs=vwb[0:64, b % 2, 0:257],
                                      start=True, stop=True), reads=KTM + [('vwb', b % 2)],
                   writes=[('ps', 3 + b % 3)])
                dve(lambda e: e.scalar_tensor_tensor(
                    out=Cin[cb_][:, 0:257], in0=Cin[cb_][:, 0:257], scalar=decb[:, h, 8 + b:9 + b],
                    in1=psB[:, b % 3, 0:257], op0=ALU.mult, op1=ALU.add),
                    reads=[('Cin', cb_), ('decb',), ('ps', 3 + b % 3)], writes=[('Cin', cb_)])
                dve(lambda e: e.tensor_copy(out=nout[:, b, h:h + 1], in_=Cin[cb_][:, 256:257]),
                    reads=[('Cin', cb_)], writes=[('nout',)])
                ld(o_C[ti, j, b, h], Cin[cb_][:, 0:256], reads=[('Cin', cb_)])
            for b in range(min(NCI - 1, 8)):
                s_load(b)
            for b in range(8):
                if b + NCI - 1 < 8:
                    s_load(b + NCI - 1)
                s_comp(b)
            pe(lambda e: e.matmul(psA[0:64, 0, 0:257], lhsT=sTA[0:64, tb, 0:64], rhs=vwA[0:64, tb, 0:257],
                                  start=False, stop=True), reads=[('sT', tb), ('vw', tb)], writes=[('ps', 0)])
            norm_group([(tb, 0, psA, 0, [('ps', 0)])])
        for b_ in range(8):
            ld(nin[:, b_, :], st_n[j, ti * 8 + b_].rearrange("h d -> d h"), writes=[('nin',)], nc_ok=True)
        for h_ in range(4):
            head(h_)
        for s_ in range(9):
            ld(o_n[ti, j, s_].rearrange("h d -> d h"), nout[:, s_, :], reads=[('nout',)], nc_ok=True)
        out_proj(w_out_even[j], norm_ffn[layer])

    rgp = sb("rgp", [128, 6, 8])
    onec = sb("onec", [128, 1])
    wbd = sb("wbd", [128, 8, 2, 128], BF16)
    mlr = [S[i][0:4, 0:T] for i in range(6)]
    mlb = sb("mlb", [4, 3])
    m0s = sb("m0s", [4, 8])
    mue = sb("mue", [4, 16])
    muc = sb("muc", [4, 16])
    dec = sb("dec", [4, 16])
    decx = sb("decx", [4, 4, 16])
    decb = sb("decb", [128, 4, 16])
    mnew = sb("mnew", [4, 9])
    wfc = sb("wfc", [128, NB, 36])
    gml = xres[1][:, 0:1024]
    wbs = sb("wbs", [64, 8])
    vwb = sb("vwb", [64, 2, 258], BF16)
    Cin = [sb("Cin%d" % i, [128, 257]) for i in range(4)]
    nin = sb("nin", [128, 8, 4])
    nout = sb("nout", [128, 9, 4])
    sm = sb("sm", [128, 3, 8])
    Ctmp = sb("Ctmp", [128, 257])
    dve(lambda e: e.memset(onec[:], 1.0), writes=[('onec',)])

    def odd_mixer(ti, layer):
        j = layer // 2
        sqacc = xres[0]
        wsn = S[3][:, 0:1024].rearrange("p (g s) -> p g s", s=128)
        bbc = S[3][:, 0:1024]
        rmsnorm_phase(norm_mix[layer])
        w = w_in_odd[j]
        ALLX = [('xnT', k) for k in range(16)]
        rhsx = lambda kc, c0, n: xnT[:, kc, c0:c0 + n]
        ld(lng[:, 0, :], sgu_ln_g[j].rearrange("(c p) -> p c", p=128), writes=[('lng',)], nc_ok=True)
        ld(lng[:, 1, :], sgu_ln_b[j].rearrange("(c p) -> p c", p=128), writes=[('lng',)], nc_ok=True)
        for c in range(0, 16, 2):
            wi, wv = load_w2(w, 2048 + c * 128, 2048 + c * 128 + 128)
            for h2 in range(2):
                cc = c + h2
                mm_fm(psA, PSA, lambda kc, wv=wv, h2=h2: wv[:, kc, h2 * 128:h2 * 128 + 128], 16, rhsx,
                      [('w', wi)] + ALLX)
                vg = S[cc % 2]
                act(lambda e, vg=vg: e.activation(out=vg[:, 0:1024], in_=psA[:, 0:2, :].rearrange("p a n -> p (a n)"),
                                                  func=AF.Gelu_apprx_tanh), reads=PSA[0:2], writes=[('S', cc % 2)])
                act(lambda e, vg=vg: e.activation(out=vg[:, 1024:T], in_=psA[:, 2, 0:64], func=AF.Gelu_apprx_tanh),
                    reads=PSA[2:3], writes=[('S', cc % 2)])
                if cc == 0:
                    act(lambda e, vg=vg: e.activation(out=sqacc[:, 0:T], in_=vg[:, 0:T], func=AF.Square),
                        reads=[('S', cc % 2)], writes=[('xres', 0)])
                    dve(lambda e, vg=vg: e.tensor_copy(out=S[4][:, 0:T], in_=vg[:, 0:T]), reads=[('S', cc % 2)],
                        writes=[('S', 4)])
                else:
                    vq = S[2 + cc % 2]
                    act(lambda e, vg=vg, vq=vq: e.activation(out=vq[:, 0:T], in_=vg[:, 0:T], func=AF.Square),
                        reads=[('S', cc % 2)], writes=[('S', 2 + cc % 2)])
                    dve(lambda e, vg=vg: e.tensor_tensor(out=S[4][:, 0:T], in0=S[4][:, 0:T], in1=vg[:, 0:T],
                                                         op=ALU.add), reads=[('S', 4), ('S', cc % 2)],
                        writes=[('S', 4)])
                    dve(lambda e, vq=vq: e.tensor_tensor(out=sqacc[:, 0:T], in0=sqacc[:, 0:T], in1=vq[:, 0:T],
                                                         op=ALU.add), reads=[('xres', 0), ('S', 2 + cc % 2)],
                        writes=[('xres', 0)])
                dve(lambda e, vg=vg, cc=cc: e.tensor_copy(out=big[:, 16 + cc, 0:1024], in_=vg[:, 0:1024]),
                    reads=[('S', cc % 2)], writes=[('big', 16 + cc)])
                dve(lambda e, vg=vg, cc=cc: e.tensor_copy(out=vs32[:, cc, :], in_=vg[:, 1024:T]),
                    reads=[('S', cc % 2)], writes=[('vs32',)])

        def fnS1(e):
            ins = None
            for nt, (c0, n) in enumerate(NTL):
                ins = e.matmul(psB[:, nt, 0:n], lhsT=onesf[:, :], rhs=S[4][:, c0:c0 + n], start=True, stop=True)
            return ins
        pe(fnS1, reads=[('S', 4), ('onesf',)], writes=PSB)

        def fnS2(e):
            ins = None
            for nt, (c0, n) in enumerate(NTL):
                ins = e.matmul(psA[:, nt, 0:n], lhsT=onesf[:, :], rhs=sqacc[:, c0:c0 + n], start=True, stop=True)
            return ins
        pe(fnS2, reads=[('xres', 0), ('onesf',)], writes=PSA)
        mean = S[4]
        rstd = S[5]
        act(lambda e: e.activation(out=mean[:, 0:1024], in_=psB[:, 0:2, :].rearrange("p a n -> p (a n)"),
                                   func=AF.Copy, scale=1.0 / D), reads=PSB[0:2], writes=[('S', 4)])
        act(lambda e: e.activation(out=mean[:, 1024:T], in_=psB[:, 2, 0:64], func=AF.Copy, scale=1.0 / D),
            reads=PSB[2:3], writes=[('S', 4)])
        dve(lambda e: e.tensor_tensor(out=rstd[:, 0:T], in0=mean[:, 0:T], in1=mean[:, 0:T], op=ALU.mult),
            reads=[('S', 4)], writes=[('S', 5)])
        dve(lambda e: e.scalar_tensor_tensor(out=rstd[:, 0:1024], in0=psA[:, 0:2, :].rearrange("p a n -> p (a n)"),
                                             scalar=1.0 / D, in1=rstd[:, 0:1024], op0=ALU.mult, op1=ALU.subtract),
            reads=PSA[0:2] + [('S', 5)], writes=[('S', 5)])
        dve(lambda e: e.scalar_tensor_tensor(out=rstd[:, 1024:T], in0=psA[:, 2, 0:64], scalar=1.0 / D,
                                             in1=rstd[:, 1024:T], op0=ALU.mult, op1=ALU.subtract),
            reads=PSA[2:3] + [('S', 5)], writes=[('S', 5)])
        act(lambda e: e.activation(out=rstd[:, 0:T], in_=rstd[:, 0:T], func=AF.Sqrt, bias=epsc[:, 0:1], scale=1.0),
            reads=[('S', 5), ('epsc',)], writes=[('S', 5)])
        dve(lambda e: e.reciprocal(out=rstd[:, 0:T], in_=rstd[:, 0:T]), reads=[('S', 5)], writes=[('S', 5)])
        for c in range(0, 16, 2):
            wi, wv = load_w2(w, c * 128, c * 128 + 128)
            for h2 in range(2):
                cc = c + h2
                psg, PSG = (psA, PSA) if h2 == 0 else (psB, PSB)
                mm_fm(psg, PSG, lambda kc, wv=wv, h2=h2: wv[:, kc, h2 * 128:h2 * 128 + 128], 16, rhsx,
                      [('w', wi)] + ALLX)
                act(lambda e, psg=psg, cc=cc: e.activation(out=big[:, cc, 0:1024],
                                                           in_=psg[:, 0:2, :].rearrange("p a n -> p (a n)"),
                                                           func=AF.Gelu_apprx_tanh), reads=PSG[0:2], writes=[('big', cc)])
                act(lambda e, psg=psg, cc=cc: e.activation(out=big[:, cc, 1024:T], in_=psg[:, 2, 0:64], func=AF.Gelu_apprx_tanh),
                    reads=PSG[2:3], writes=[('big', cc)])
        vtm = xnT[:, :, :].rearrange("p a t -> p (a t)")[:, 0:8 * 2048].rearrange("p (n c) -> p n c", c=2048)
        VTM = [('xnT', k) for k in range(16)]
        for cc in range(16):
            t1 = S[cc % 2]
            dve(lambda e, cc=cc, t1=t1: e.tensor_tensor(out=t1[:, 0:1024], in0=big[:, 16 + cc, 0:1024],
                                                        in1=mean[:, 0:1024], op=ALU.subtract),
                reads=[('big', 16 + cc), ('S', 4)], writes=[('S', cc % 2)])
            dve(lambda e, cc=cc, t1=t1: e.tensor_tensor(out=t1[:, 1024:T], in0=vs32[:, cc, :], in1=mean[:, 1024:T],
                                                        op=ALU.subtract), reads=[('vs32',), ('S', 4)],
                writes=[('S', cc % 2)])
            dve(lambda e, t1=t1: e.tensor_tensor(out=t1[:, 0:T], in0=t1[:, 0:T], in1=rstd[:, 0:T], op=ALU.mult),
                reads=[('S', cc % 2), ('S', 5)], writes=[('S', cc % 2)])
            act(lambda e, cc=cc, t1=t1: e.activation(out=big[:, 16 + cc, 0:1024], in_=t1[:, 0:1024], func=AF.Identity,
                                                     bias=lng[:, 1, cc:cc + 1], scale=lng[:, 0, cc:cc + 1]),
                reads=[('S', cc % 2), ('lng',)], writes=[('big', 16 + cc)])
            act(lambda e, cc=cc, t1=t1: e.activation(out=vs32[:, cc, :], in_=t1[:, 1024:T], func=AF.Identity,
                                                     bias=lng[:, 1, cc:cc + 1], scale=lng[:, 0, cc:cc + 1]),
                reads=[('S', cc % 2), ('lng',)], writes=[('vs32',)])
            def tr_(cc):
                for q2 in range(2):
                    kslot = (cc * 2 + q2) % 7
                    if kslot == 6:
                        pv, pk = psT[:, 0:1024], ('ps', 7)
                    else:
                        pt_, pb__ = (psA, kslot) if kslot < 3 else (psB, kslot - 3)
                        pv, pk = pt_[:, pb__, :].bitcast(BF16), ('ps', kslot)

                    def fnT(e, cc=cc, q2=q2, pv=pv):
                        ins = None
                        for i4 in range(4):
                            n_ = q2 * 4 + i4
                            ins = e.transpose(out=pv[:, i4 * 128:i4 * 128 + 128],
                                              in_=big[:, 16 + cc, n_ * 128:n_ * 128 + 128], identity=identb[:, :])
                        return ins
                    pe(fnT, reads=[('big', 16 + cc), ('identb',)], writes=[pk])
                    A_copy(vtm[:, q2 * 4:q2 * 4 + 4, cc * 128:cc * 128 + 128],
                           pv[:, 0:512].rearrange("p (a n) -> p a n", n=128), reads=[pk], writes=VTM)
                pe(lambda e, cc=cc: e.matmul(psM[0:64, 0:128], lhsT=vs32[:, cc, :], rhs=identf[:, :], start=True,
                                             stop=True), reads=[('vs32',), ('identf',)], writes=PSM)
                ob = ovst[cc % 2]
                A_copy(ob[0:64, :], psM[0:64, 0:128], reads=PSM, writes=[('ovst', cc % 2)])
                ld(o_v[ti, j, :, cc * 128:cc * 128 + 128], ob[0:64, :], reads=[('ovst', cc % 2)])
                dve(lambda e, cc=cc, ob=ob: e.tensor_copy(out=vstm[0:64, cc * 128:cc * 128 + 128], in_=ob[0:64, :]),
                    reads=[('ovst', cc % 2)], writes=[('vstm',)])
            if cc >= 1:
                tr_(cc - 1)
        tr_(15)
        ld(wsn[:, :, :], sgu_ws[j].rearrange("g t s -> t g s"), writes=[('S', 3)])
        for g in range(8):
            pe(lambda e, g=g: e.matmul(psM[:, 0:128], lhsT=wsn[:, g, :], rhs=identf[:, :], start=True, stop=True),
               reads=[('S', 3), ('identf',)], writes=PSM)
            dve(lambda e, g=g: e.tensor_tensor(out=wsT[:, g, :], in0=psM[:, 0:128], in1=causal[:, :], op=ALU.mult),
                reads=PSM + [('causal',)], writes=[('wsT',)])
        pe(lambda e: e.matmul(psM[0:64, 0:512].rearrange("p (g b t) -> p g b t", g=8, b=8),
                              lhsT=rep[0:8, 0:64],
                              rhs=wsT[0:8, :, 0:8].unsqueeze(2).to_broadcast([8, 8, 8, 8]), start=True, stop=True),
           reads=[('wsT',), ('rep',)], writes=PSM)
        dve(lambda e: e.tensor_tensor(out=bdT[0:64, :, :], in0=psM[0:64, 0:512].rearrange("p (g n) -> p g n", g=8),
                                      in1=bdm[0:64, :].unsqueeze(1).to_broadcast([64, 8, 64]), op=ALU.mult),
            reads=PSM + [('bdm',)], writes=[('bdT',)])
        ld(bbc[:, :], sgu_b[j].rearrange("g t -> (g t)").partition_broadcast(128), writes=[('S', 3)])
        BK6 = [(psA, 0, ('ps', 0)), (psA, 1, ('ps', 1)), (psA, 2, ('ps', 2)),
               (psB, 0, ('ps', 3)), (psB, 1, ('ps', 4)), (psB, 2, ('ps', 5))]
        kq = [0]
        TMPS = [S[0], S[1], S[2]]
        for g in range(8):
            for half in range(2):
                cc = 2 * g + half
                for q2 in range(3):
                    pt, pb_, pk = BK6[kq[0] % 6]
                    tmp = TMPS[kq[0] % 3]
                    tk_ = [('S', kq[0] % 3)]
                    kq[0] += 1
                    if q2 < 2:
                        def fnM(e, g=g, cc=cc, q2=q2, pt=pt, pb_=pb_):
                            ins = None
                            for i4 in range(4):
                                n_ = q2 * 4 + i4
                                ins = e.matmul(pt[:, pb_, i4 * 128:i4 * 128 + 128],
                                               lhsT=vtm[:, n_, cc * 128:cc * 128 + 128], rhs=wsT[:, g, :],
                                               start=True, stop=True)
                            return ins
                        pe(fnM, reads=VTM + [('wsT',)], writes=[pk])
                        dve(lambda e, g=g, tmp=tmp, pt=pt, pb_=pb_: e.tensor_tensor(
                            out=tmp[:, 0:512].rearrange("p (a t) -> p a t", t=128),
                            in0=pt[:, pb_, 0:512].rearrange("p (a t) -> p a t", t=128),
                            in1=bbc[:, g * 128:g * 128 + 128].unsqueeze(1).to_broadcast([128, 4, 128]), op=ALU.add),
                            reads=[pk, ('S', 3)], writes=tk_)
                        dve(lambda e, cc=cc, q2=q2, tmp=tmp: e.tensor_tensor(
                            out=big[:, cc, q2 * 512:q2 * 512 + 512], in0=big[:, cc, q2 * 512:q2 * 512 + 512],
                            in1=tmp[:, 0:512], op=ALU.mult), reads=tk_ + [('big', cc)], writes=[('big', cc)])
                    else:
                        pe(lambda e, g=g, cc=cc, pt=pt, pb_=pb_: e.matmul(
                            pt[:, pb_, 0:64], lhsT=vstm[0:64, cc * 128:cc * 128 + 128], rhs=bdT[0:64, g, :],
                            start=True, stop=True), reads=[('vstm',), ('bdT',)], writes=[pk])
                        dve(lambda e, g=g, tmp=tmp, pt=pt, pb_=pb_: e.tensor_tensor(
                            out=tmp[:, 0:64].rearrange("p (b t) -> p b t", t=8),
                            in0=pt[:, pb_, 0:64].rearrange("p (b t) -> p b t", t=8),
                            in1=bbc[:, g * 128:g * 128 + 8].unsqueeze(1).to_broadcast([128, 8, 8]), op=ALU.add),
                            reads=[pk, ('S', 3)], writes=tk_)
                        dve(lambda e, cc=cc, tmp=tmp: e.tensor_tensor(out=big[:, cc, 1024:T], in0=big[:, cc, 1024:T],
                                                                      in1=tmp[:, 0:64], op=ALU.mult),
                            reads=tk_ + [('big', cc)], writes=[('big', cc)])
        out_proj(w_out_odd[j], norm_ffn[layer])

    lng = sb("lng", [128, 2, 16])
    vs32 = sb("vs32", [128, 16, 64])
    ovst = [sb("ovst%d" % i, [64, 128]) for i in range(2)]
    vstm = sb("vstm", [64, 2048], BF16)
    wsT = sb("wsT", [128, 8, 128], BF16)
    bdT = sb("bdT", [64, 8, 64], BF16)

    for ti in range(2):
        load_x_tile(ti)
        for layer in range(depth):
            if layer % 2 == 0:
                even_mixer(ti, layer)
            else:
                odd_mixer(ti, layer)
            ffn_phase(ti, layer)
        rmsnorm_phase(norm_final, final=True, ti=ti)
    tk.finish()
    tk.emit()
    st.close()
    return nc


_CONST = None


def _consts():
    global _CONST
    if _CONST is None:
        bf = ml_dtypes.bfloat16
        c = {}
        c["c_identf"] = np.eye(128, dtype=np.float32)
        c["c_identb"] = np.eye(128, dtype=np.float32).astype(bf)
        c["c_onesb"] = np.ones((128, 128), np.float32).astype(bf)
        c["c_onesf"] = np.ones((128, 128), np.float32)
        c["c_causal"] = np.triu(np.ones((128, 128), np.float32))
        bd = np.kron(np.eye(8, dtype=np.float32), np.ones((8, 8), np.float32))
        c["c_bd"] = bd
        c["c_bdc"] = bd * np.triu(np.ones((64, 64), np.float32))
        c["c_ind"] = np.kron(np.eye(8, dtype=np.float32), np.ones((8, 1), np.float32))
        e4 = np.zeros((4, 4, 16), np.float32)
        for i in range(4):
            e4[i, i, :] = 1.0
        c["c_eye4"] = e4
        c["c_rep"] = np.tile(np.eye(8, dtype=np.float32), (1, 8)).astype(bf)
        _CONST = c
    return _CONST


_NC = {}


def kernel(**inputs):
    depth = DEPTH
    if depth not in _NC:
        _NC[depth] = build(depth)
    nc = _NC[depth]
    f = lambda k: np.ascontiguousarray(np.asarray(inputs[k], dtype=np.float32))
    xp = f("x_prompt")
    xs = f("x_sample")
    wnames = ["norm_mix", "norm_ffn", "norm_final", "w_in_even", "conv_even_w", "conv_even_b", "rg_wa", "rg_ba",
              "rg_wx", "rg_bx", "rg_lambda", "ml_gate_b", "ml_norm_g", "w_out_even", "w_in_odd", "sgu_ln_g",
              "sgu_ln_b", "sgu_ws", "sgu_b", "w_out_odd", "ffn_w_up", "ffn_conv_w", "ffn_conv_b", "ffn_w_down"]
    shared = {k: f(k) for k in wnames}
    shared.update(_consts())
    sc, sh, sC, sn, smm, sf = (f("state_conv_mix"), f("state_rglru_h"), f("state_mlstm_C"), f("state_mlstm_n"),
                               f("state_mlstm_m"), f("state_ffn_conv"))
    in_maps = []
    for c in range(8):
        p = c % 4
        xin = np.empty((2, T, D), np.float32)
        for ti in range(2):
            xin[ti, :TP] = xp[p, ti * TP:(ti + 1) * TP]
            xin[ti, TP:] = xs[16 * c + 8 * ti:16 * c + 8 * ti + 8].reshape(TS, D)
        m = dict(shared)
        m["xin"] = xin
        sl = slice(16 * c, 16 * c + 16)
        m["st_conv"] = np.ascontiguousarray(sc[:, sl])
        m["st_h"] = np.ascontiguousarray(sh[:, sl])
        m["st_C"] = np.ascontiguousarray(sC[:, sl])
        m["st_n"] = np.ascontiguousarray(sn[:, sl])
        m["st_m"] = np.ascontiguousarray(smm[:, sl])
        m["st_f"] = np.ascontiguousarray(sf[:, sl])
        in_maps.append(m)
    res = run_bass_kernel_spmd(nc, in_maps, core_ids=list(range(8)))
    R = res.results
    y_p = np.empty((4, 2048, D), np.float32)
    y_s = np.empty((128, 8, D), np.float32)
    conv_p = np.empty((2, 4, 3, D), np.float32)
    conv_s = np.empty((2, 128, 3, D), np.float32)
    h_p = np.empty((2, 4, 1024), np.float32)
    h_s = np.empty((2, 128, 1024), np.float32)
    C_p = np.empty((2, 4, 4, 128, 256), np.float32)
    C_s = np.empty((2, 128, 4, 128, 256), np.float32)
    n_p = np.empty((2, 4, 4, 128), np.float32)
    n_s = np.empty((2, 128, 4, 128), np.float32)
    m_p = np.empty((2, 4, 4), np.float32)
    m_s = np.empty((2, 128, 4), np.float32)
    v_s = np.empty((2, 128, 8, D), np.float32)
    f_p = np.empty((4, 4, 2, DFF), np.float32)
    f_s = np.empty((4, 128, 2, DFF), np.float32)
    for c in range(8):
        r = R[c]
        p = c % 4
        for ti in range(2):
            ss = slice(16 * c + 8 * ti, 16 * c + 8 * ti + 8)
            y_s[ss] = r["y_d"][ti, TP:].reshape(8, 8, D)
            conv_s[:, ss] = r["o_conv"][ti][:, 0:8]
            h_s[:, ss] = r["o_h"][ti][:, 0:8]
            C_s[:, ss] = r["o_C"][ti][:, 0:8]
            n_s[:, ss] = r["o_n"][ti][:, 0:8]
            m_s[:, ss] = r["o_m"][ti][:, 0:8]
            v_s[:, ss] = r["o_v"][ti].reshape(2, 8, 8, D)
            f_s[:, ss] = r["o_f"][ti][:, 0:8]
            if c < 4:
                y_p[p, ti * TP:(ti + 1) * TP] = r["y_d"][ti, :TP]
        if c < 4:
            conv_p[:, p] = r["o_conv"][1][:, 8]
            h_p[:, p] = r["o_h"][1][:, 8]
            C_p[:, p] = r["o_C"][1][:, 8]
            n_p[:, p] = r["o_n"][1][:, 8]
            m_p[:, p] = r["o_m"][1][:, 8]
            f_p[:, p] = r["o_f"][1][:, 8]
    return (y_p, y_s, conv_p, conv_s, h_p, h_s, C_p, C_s, n_p, n_s, m_p, m_s, v_s, f_p, f_s)
```

```python
from contextlib import ExitStack
import numpy as np
import ml_dtypes
import concourse.bass as bass
import concourse.mybir as mybir
from concourse.bass_utils import run_bass_kernel_spmd

F32 = mybir.dt.float32
BF16 = mybir.dt.bfloat16
I32 = mybir.dt.int32
AF = mybir.ActivationFunctionType
ALU = mybir.AluOpType

D = 2048
DEPTH = 4
TP = 1024
NSQ = 8
TS = 64
T = TP + TS
NTL = [(0, 512), (512, 512), (1024, 64)]
NB = 9
DFF = 5632
NJ = 44
EIN = 5128
EPS = 1e-6
SW_ = 1120


def blk(tb):
    return (tb * 128, 128 if tb < 8 else 64)


class Tracker:
    def __init__(self, nc, stack):
        self.nc = nc
        self.engs = ['pe', 'act', 'dve', 'pool', 'sp']
        self.sem = {k: stack.enter_context(nc.semaphore("s_" + k)) for k in ['pe', 'act', 'dve', 'pool']}
        self.cnt = {k: 0 for k in self.sem}
        self.q = {}
        for qn, issuer, R in [('sp', 'sp', 14), ('gq', 'pool', 8)]:
            self.q[qn] = dict(issuer=issuer, R=R, n=0,
                              sems=[stack.enter_context(nc.semaphore("q_%s%d" % (qn, i))) for i in range(R)])
        self.prog = {k: [] for k in self.engs}
        self.waited = {k: {} for k in self.engs}
        self.W = {}
        self.Rd = {}
        self.alias = {}

    def _exp(self, keys):
        out = []
        seen = set()
        stack = list(keys)
        while stack:
            k = stack.pop()
            if k in seen:
                continue
            seen.add(k)
            out.append(k)
            if k in self.alias:
                stack.extend(self.alias[k])
        return out

    def _deps(self, reads, writes):
        reads = self._exp(reads)
        writes = self._exp(writes)
        d = {}

        def add(m):
            for k, v in m.items():
                if d.get(k, 0) < v:
                    d[k] = v
        for r in reads:
            add(self.W.get(r, {}))
        for w in writes:
            add(self.W.get(w, {}))
            add(self.Rd.get(w, {}))
        return d

    def _waits(self, issuer, deps):
        wd = self.waited[issuer]
        wl = []
        for k, v in deps.items():
            if wd.get(k, 0) < v:
                wd[k] = v
                wl.append((self._semof(k), v))
        return wl

    def _semof(self, k):
        if k[0] == 'e':
            return self.sem[k[1]]
        return self.q[k[1]]['sems'][k[2]]

    def _record(self, key, val, reads, writes):
        reads = self._exp(reads)
        writes = self._exp(writes)
        for r in reads:
            self.Rd.setdefault(r, {})[key] = val
        for w in writes:
            self.W.setdefault(w, {})[key] = val

    def op(self, eng, fn, reads=(), writes=()):
        writes = list(writes) + [k for k in reads if k[0] == 'ps']
        deps = self._deps(reads, writes)
        wl = self._waits(eng, deps)
        self.cnt[eng] += 1
        v = self.cnt[eng]
        self.prog[eng].append((wl, fn, self.sem[eng], 1))
        self._record(('e', eng), v, reads, writes)

    def dma(self, qn, out, in_, reads=(), writes=(), nc_ok=False):
        q = self.q[qn]
        n = q['n']
        q['n'] += 1
        slot = n % q['R']
        k = n // q['R']
        deps = self._deps(reads, writes)
        key = ('q', qn, slot)
        if k >= 1:
            deps[key] = max(deps.get(key, 0), 16 * k)
        wl = self._waits(q['issuer'], deps)

        def fn(e, out=out, in_=in_, nc_ok=nc_ok):
            if nc_ok:
                return e.dma_start(out=out, in_=in_, allow_slow_non_contiguous=True)
            return e.dma_start(out=out, in_=in_)
        self.prog[q['issuer']].append((wl, fn, q['sems'][slot], 16))
        self._record(key, 16 * (k + 1), reads, writes)

    def finish(self):
        deps = {}
        for qn, q in self.q.items():
            for s in range(q['R']):
                cnt = (q['n'] - s + q['R'] - 1) // q['R']
                if cnt > 0:
                    deps[('q', qn, s)] = 16 * cnt
        for e in self.sem:
            if self.cnt[e] > 0:
                deps[('e', e)] = self.cnt[e]
        wl = self._waits('sp', deps)
        self.prog['sp'].append((wl, None, None, 0))

    def emit(self):
        nc = self.nc
        prog = self.prog

        def run(name, e):
            for wl, fn, sem, inc in prog[name]:
                for s, v in wl:
                    e.wait_ge(s, v)
                if fn is not None:
                    ins = fn(e)
                    ins.then_inc(sem, inc)
        with nc.Block() as block:
            @block.tensor
            def _(e):
                run('pe', e)

            @block.scalar
            def _(e):
                run('act', e)

            @block.vector
            def _(e):
                run('dve', e)

            @block.gpsimd
            def _(e):
                run('pool', e)

            @block.sync
            def _(e):
                run('sp', e)


def build(depth=DEPTH):
    nc = bass.Bass("TRN2", target_bir_lowering=False)
    st = ExitStack()
    tk = Tracker(nc, st)

    def din(name, shape, dt=F32):
        return nc.dram_tensor(name, list(shape), dt, kind="ExternalInput").ap()

    def dout(name, shape, dt=F32):
        return nc.dram_tensor(name, list(shape), dt, kind="ExternalOutput").ap()

    xin = din("xin", [2, T, D])
    st_conv = din("st_conv", [2, 16, 3, D])
    st_h = din("st_h", [2, 16, 1024])
    st_C = din("st_C", [2, 16, 4, 128, 256])
    st_n = din("st_n", [2, 16, 4, 128])
    st_m = din("st_m", [2, 16, 4])
    st_f = din("st_f", [4, 16, 2, DFF])
    norm_mix = din("norm_mix", [4, D])
    norm_ffn = din("norm_ffn", [4, D])
    norm_final = din("norm_final", [D])
    w_in_even = din("w_in_even", [2, D, EIN])
    conv_even_w = din("conv_even_w", [2, 4, D])
    conv_even_b = din("conv_even_b", [2, D])
    rg_wa = din("rg_wa", [2, 16, 64, 64])
    rg_ba = din("rg_ba", [2, 1024])
    rg_wx = din("rg_wx", [2, 16, 64, 64])
    rg_bx = din("rg_bx", [2, 1024])
    rg_lambda = din("rg_lambda", [2, 1024])
    ml_gate_b = din("ml_gate_b", [2, 2, 4])
    ml_norm_g = din("ml_norm_g", [2, 4, 256])
    w_out_even = din("w_out_even", [2, D, D])
    w_in_odd = din("w_in_odd", [2, D, 2 * D])
    sgu_ln_g = din("sgu_ln_g", [2, D])
    sgu_ln_b = din("sgu_ln_b", [2, D])
    sgu_ws = din("sgu_ws", [2, 8, 128, 128])
    sgu_b = din("sgu_b", [2, 8, 128])
    w_out_odd = din("w_out_odd", [2, D, D])
    ffn_w_up = din("ffn_w_up", [4, D, 2 * DFF])
    ffn_conv_w = din("ffn_conv_w", [4, 3, DFF])
    ffn_conv_b = din("ffn_conv_b", [4, DFF])
    ffn_w_down = din("ffn_w_down", [4, DFF, D])
    c_identf = din("c_identf", [128, 128])
    c_identb = din("c_identb", [128, 128], BF16)
    c_onesb = din("c_onesb", [128, 128], BF16)
    c_onesf = din("c_onesf", [128, 128])
    c_causal = din("c_causal", [128, 128])
    c_bdc = din("c_bdc", [64, 64])
    c_bd = din("c_bd", [64, 64])
    c_ind = din("c_ind", [64, 8])
    c_eye4 = din("c_eye4", [4, 4, 16])
    c_rep = din("c_rep", [8, 64], BF16)

    y_d = dout("y_d", [2, T, D])
    o_conv = dout("o_conv", [2, 2, 9, 3, D])
    o_h = dout("o_h", [2, 2, 9, 1024])
    o_C = dout("o_C", [2, 2, 9, 4, 128, 256])
    o_n = dout("o_n", [2, 2, 9, 4, 128])
    o_m = dout("o_m", [2, 2, 9, 4])
    o_v = dout("o_v", [2, 2, 64, D])
    o_f = dout("o_f", [2, 4, 9, 2, DFF])
    xTd = nc.dram_tensor("xTd", [128, 16, T], F32, kind="Internal").ap()

    def sb(name, shape, dt=F32):
        return st.enter_context(nc.sbuf_tensor(name, list(shape), dt))

    def ps(name, shape, dt=F32):
        return st.enter_context(nc.psum_tensor(name, list(shape), dt))

    xnT = sb("xnT", [128, 16, T], BF16)
    big = sb("big", [128, 32, T], BF16)
    xf = big[:].bitcast(F32)
    NSLOT = 3
    wring = [sb("wr%d" % i, [128, 16 * 256], BF16) for i in range(NSLOT)]
    S = [sb("S%d" % i, [128, SW_], F32) for i in range(6)]
    xres = [sb("xres%d" % i, [128, T], F32) for i in range(2)]
    identf = sb("identf", [128, 128])
    identb = sb("identb", [128, 128], BF16)
    onesb = sb("onesb", [128, 128], BF16)
    onesf = sb("onesf", [128, 128])
    causal = sb("causal", [128, 128])
    bdc = sb("bdc", [64, 64])
    bdm = sb("bdm", [64, 64])
    ind = sb("ind", [64, 8])
    eye4 = sb("eye4", [4, 4, 16])
    rep = sb("rep", [8, 64], BF16)
    epsc = sb("epsc", [128, 1])
    gcol = sb("gcol", [128, 16])
    fcw = sb("fcw", [128, NJ, 3])
    fcb = sb("fcb", [128, NJ])
    fcarry = sb("fcarry", [128, 4, NJ, 2])
    ecw = sb("ecw", [128, 16, 4])
    ecb = sb("ecb", [128, 16])
    ecarry = sb("ecarry", [128, 2, 16, 3])
    hcarry = sb("hcarry", [128, 2, 8])
    Cst = sb("Cst", [128, 2, 4, 257])
    mcarry = sb("mcarry", [4, 2])
    strow4 = sb("strow", [32, 4, 128])
    orow4 = sb("orow", [32, 4, 128])
    stage4 = sb("stage", [128, 4, 32])
    strow = strow4[:, 3, :]
    orow = orow4[:, 3, :]
    stage = stage4[:, 3, :]

    psA = ps("psA", [128, 3, 512])
    psB = ps("psB", [128, 3, 512])
    psM = ps("psM", [128, 512])
    psT = ps("psT", [128, 1024], BF16)

    def xfc(kc):
        return big[:, 2 * kc:2 * kc + 2, :].rearrange("p a t -> p (a t)").bitcast(F32)

    def XF(kc):
        return [('big', 2 * kc), ('big', 2 * kc + 1)]

    PSA = [('ps', 0), ('ps', 1), ('ps', 2)]
    PSB = [('ps', 3), ('ps', 4), ('ps', 5)]
    PSM = [('ps', 6)]
    PST = [('ps', 7)]
    HB = [(('hq',), 0, 2176), (('hk',), 2176, 2176), (('ktm',), 4352, 2304)]
    HB += [(('vex', t_), 6656 + t_ * 516, 516) for t_ in range(NB)]
    HB += [(('osg', t_), 11304 + t_ * 1024, 1024) for t_ in range(NB)]
    HB += [(('vw', t_), 20520 + t_ * 516, 516) for t_ in range(NB)]
    HB += [(('sT', t_), 25164 + t_ * 256, 256) for t_ in range(NB)]
    HB += [(('Cdb', t_), 27468 + t_ * 516, 516) for t_ in range(8)]
    HB += [(('mlo', t_), 31596 + t_ * 512, 512) for t_ in range(3)]
    HB += [(('qTm',), 33132, 1024)]

    def ov(lo, hi):
        return [k for (k, o, n_) in HB if o < hi and o + n_ > lo]
    for c_ in range(16):
        lo, hi = c_ * 2176, (c_ + 1) * 2176
        tk.alias[('big', 16 + c_)] = [('S1', i) for i in range(6) if i * SW_ * 4 < hi and (i + 1) * SW_ * 4 > lo] \
            + ov(lo, hi)
    for i in range(6):
        tk.alias[('S1', i)] = ov(i * SW_ * 4, (i + 1) * SW_ * 4)
    tk.alias[('xres', 0)] = [('hm', 0), ('hm', 1), ('hm', 2), ('xst', 0), ('xst', 1)]
    tk.alias[('xres', 1)] = [('gml',), ('xst', 2), ('xst', 3)]
    for k_ in range(8):
        lo, hi = k_ * 2176, (k_ + 1) * 2176
        tk.alias[('xnT', k_)] = [('yrow', hb_, q_) for hb_ in range(2) for q_ in range(4)
                                 if hb_ * 8192 + q_ * 2048 < hi and hb_ * 8192 + (q_ + 1) * 2048 > lo]

    def act(fn, reads=(), writes=()):
        tk.op('act', fn, reads, writes)

    def dve(fn, reads=(), writes=()):
        tk.op('dve', fn, reads, writes)

    def pe(fn, reads=(), writes=()):
        tk.op('pe', fn, reads, writes)

    uq = {'n': 0}

    def ld(out, in_, reads=(), writes=None, nc_ok=False):
        if writes is None:
            uq['n'] += 1
            writes = [('uq', uq['n'])]
        tk.dma('sp', out, in_, reads, writes, nc_ok)

    def ldw(out, in_, reads=(), writes=()):
        tk.dma('gq', out, in_, reads, writes)

    wstate = {'n': 0}

    def wslot():
        i = wstate['n'] % NSLOT
        wstate['n'] += 1
        return i

    def A_copy(out, in_, reads, writes):
        act(lambda e: e.activation(out=out, in_=in_, func=AF.Copy), reads, writes)

    for (t_, d_, k_) in [(identf, c_identf, 'identf'), (identb, c_identb, 'identb'), (onesb, c_onesb, 'onesb'),
                         (onesf, c_onesf, 'onesf'), (causal, c_causal, 'causal'), (bdc, c_bdc, 'bdc'),
                         (bdm, c_bd, 'bdm'), (ind, c_ind, 'ind'), (eye4, c_eye4, 'eye4'), (rep, c_rep, 'rep')]:
        ld(t_[:], d_, writes=[(k_,)])
    dve(lambda e: e.memset(epsc[:], EPS), writes=[('epsc',)])
    dve(lambda e: e.memset(fcarry[:], 0.0), writes=[('fcarry',)])
    dve(lambda e: e.memset(ecarry[:], 0.0), writes=[('ecarry',)])
    dve(lambda e: e.memset(hcarry[:], 0.0), writes=[('hcarry',)])
    dve(lambda e: e.memset(Cst[:], 0.0), writes=[('Cst',)])
    dve(lambda e: e.memset(mcarry[:], 0.0), writes=[('mcarry',)])
    dve(lambda e: e.memset(stage4[:], 0.0), writes=[('stage',), ('stage', 0), ('stage', 1), ('stage', 2), ('stage', 3)])

    def mm_fm(psg, PSG, lhs_fn, nk, rhs_fn, reads):
        def fn(e):
            ins = None
            for kc in range(nk):
                for nt, (c0, n) in enumerate(NTL):
                    ins = e.matmul(psg[:, nt, 0:n], lhsT=lhs_fn(kc), rhs=rhs_fn(kc, c0, n),
                                   start=(kc == 0), stop=(kc == nk - 1))
            return ins
        pe(fn, reads, PSG)

    def load_w2(dram2d, c0, c1):
        i = wslot()
        w = wring[i]
        v = w[:, 0:16 * 256].rearrange("p (k n) -> p k n", n=256)
        src = dram2d.rearrange("(k p) n -> p k n", p=128)
        if c1 == c0 + 128:
            ldw(v[:, :, :], src[:, :, c0:c0 + 256], writes=[('w', i)])
        else:
            ldw(v[:, :, 0:128], src[:, :, c0:c0 + 128], writes=[('w', i)])
            if c1 is not None:
                ldw(v[:, :, 128:256], src[:, :, c1:c1 + 128], writes=[('w', i)])
        return i, v

    def load_wk(dram2d, k0, nk, c0):
        i = wslot()
        w = wring[i]
        v = w[:, 0:nk * 128].rearrange("p (k n) -> p k n", n=128)
        src = dram2d.rearrange("(k p) n -> p k n", p=128)
        ldw(v[:, :, :], src[:, k0:k0 + nk, c0:c0 + 128], writes=[('w', i)])
        return i, v

    rs_state = {'n': 0}

    def resid_add(psg, PSG, m, sq=False):
        b = rs_state['n'] % 2
        rs_state['n'] += 1
        xr = xres[b]
        ld(xr[:, :], xTd[:, m, :], reads=[('xTd', m)], writes=[('xres', b)])
        dve(lambda e: e.tensor_tensor(out=xr[:, 0:1024], in0=xr[:, 0:1024],
                                      in1=psg[:, 0:2, :].rearrange("p a n -> p (a n)"), op=ALU.add),
            reads=[('xres', b)] + PSG[0:2], writes=[('xres', b)])
        dve(lambda e: e.tensor_tensor(out=xr[:, 1024:T], in0=xr[:, 1024:T], in1=psg[:, 2, 0:64], op=ALU.add),
            reads=[('xres', b)] + PSG[2:3], writes=[('xres', b)])
        ld(xTd[:, m, :], xr[:, :], reads=[('xres', b)], writes=[('xTd', m)])
        if sq:
            if m == 0:
                act(lambda e: e.activation(out=S[3][:, 0:T], in_=xr[:, 0:T], func=AF.Square),
                    reads=[('xres', b)], writes=[('S', 3)])
            else:
                act(lambda e: e.activation(out=S[4][:, 0:T], in_=xr[:, 0:T], func=AF.Square),
                    reads=[('xres', b)], writes=[('S', 4)])
                dve(lambda e: e.tensor_tensor(out=S[3][:, 0:T], in0=S[3][:, 0:T], in1=S[4][:, 0:T], op=ALU.add),
                    reads=[('S', 3), ('S', 4)], writes=[('S', 3)])
            pre_sq[0] = True
            if gfill[0]:
                dve(lambda e: e.tensor_scalar(out=xnT[:, m, :], in0=xr[:, 0:T], scalar1=gcol[:, m:m + 1], scalar2=None,
                                              op0=ALU.mult), reads=[('xres', b), ('gcol',)], writes=[('xnT', m)])
                pre_x[0] = True

    pre_sq = [False]
    pre_x = [False]
    gfill = [False]

    def prep_next_norm(g_dram):
        if g_dram is None:
            gfill[0] = False
            return
        ld(gcol[:, :], g_dram.rearrange("(k p) -> p k", p=128), writes=[('gcol',)], nc_ok=True)
        gfill[0] = True

    def load_x_tile(ti):
        BK6 = [(psA, 0, ('ps', 0)), (psA, 1, ('ps', 1)), (psA, 2, ('ps', 2)),
               (psB, 0, ('ps', 3)), (psB, 1, ('ps', 4)), (psB, 2, ('ps', 5))]
        k_ = 0
        for tb in range(NB):
            t0, n = blk(tb)
            b = tb % 3
            ld(S[2 * b][0:n, 0:1024], xin[ti, t0:t0 + n, 0:1024], writes=[('S', 2 * b)])
            ld(S[2 * b + 1][0:n, 0:1024], xin[ti, t0:t0 + n, 1024:2048], writes=[('S', 2 * b + 1)])
            for q4 in range(4):
                pt, pb_, pk = BK6[k_ % 6]
                sti = k_ % 4
                k_ += 1

                def fn(e, q4=q4, n=n, b=b, pt=pt, pb_=pb_):
                    ins = None
                    for i4 in range(4):
                        kc = q4 * 4 + i4
                        src = S[2 * b + kc // 8][0:n, (kc % 8) * 128:(kc % 8) * 128 + 128]
                        ins = e.matmul(pt[:, pb_, i4 * 128:i4 * 128 + n], lhsT=src, rhs=identf[0:n, 0:n],
                                       start=True, stop=True)
                    return ins
                pe(fn, reads=[('S', 2 * b), ('S', 2 * b + 1), ('identf',)], writes=[pk])
                xr = xres[sti // 2][:, (sti % 2) * 512:(sti % 2) * 512 + 512]
                dst = xr.rearrange("p (a n) -> p a n", n=128)[:, :, 0:n]
                src_ = pt[:, pb_, :].rearrange("p (a n) -> p a n", n=128)[:, :, 0:n]
                if k_ % 2 == 0:
                    A_copy(dst, src_, reads=[pk], writes=[('xst', sti)])
                else:
                    dve(lambda e, dst=dst, src_=src_: e.tensor_copy(out=dst, in_=src_), reads=[pk],
                        writes=[('xst', sti)])
                ld(xTd[:, q4 * 4:q4 * 4 + 4, t0:t0 + n], dst, reads=[('xst', sti)],
                   writes=[('xTd', q4 * 4 + i) for i in range(4)])

    def rmsnorm_phase(g_dram, final=False, ti=0):
        if pre_x[0] and pre_sq[0] and not final:
            pre_x[0] = False
            pre_sq[0] = False
            gfill[0] = False

            def fnq0(e):
                ins = None
                for nt, (c0, n) in enumerate(NTL):
                    ins = e.matmul(psA[:, nt, 0:n], lhsT=onesf[:, :], rhs=S[3][:, c0:c0 + n], start=True, stop=True)
                return ins
            pe(fnq0, reads=[('S', 3), ('onesf',)], writes=PSA)
            rt0 = S[2]
            act(lambda e: e.activation(out=rt0[:, 0:1024], in_=psA[:, 0:2, :].rearrange("p a n -> p (a n)"),
                                       func=AF.Sqrt, bias=epsc[:, 0:1], scale=1.0 / D),
                reads=PSA[0:2] + [('epsc',)], writes=[('S', 2)])
            act(lambda e: e.activation(out=rt0[:, 1024:T], in_=psA[:, 2, 0:64], func=AF.Sqrt, bias=epsc[:, 0:1],
                                       scale=1.0 / D), reads=PSA[2:3] + [('epsc',)], writes=[('S', 2)])
            dve(lambda e: e.reciprocal(out=rt0[:, 0:T], in_=rt0[:, 0:T]), reads=[('S', 2)], writes=[('S', 2)])
            for kc in range(16):
                dve(lambda e, kc=kc: e.tensor_tensor(out=xnT[:, kc, :], in0=xnT[:, kc, :], in1=rt0[:, 0:T],
                                                     op=ALU.mult), reads=[('xnT', kc), ('S', 2)], writes=[('xnT', kc)])
            return
        pre_x[0] = False
        gfill[0] = False
        ld(gcol[:, :], g_dram.rearrange("(k p) -> p k", p=128), writes=[('gcol',)], nc_ok=True)
        KCO = list(range(11, 16)) + list(range(0, 11))
        for kc in KCO:
            ld(xfc(kc), xTd[:, kc, :], reads=[('xTd', kc)], writes=XF(kc))
        if pre_sq[0]:
            pre_sq[0] = False

            def fnq(e):
                ins = None
                for nt, (c0, n) in enumerate(NTL):
                    ins = e.matmul(psA[:, nt, 0:n], lhsT=onesf[:, :], rhs=S[3][:, c0:c0 + n], start=True, stop=True)
                return ins
            pe(fnq, reads=[('S', 3), ('onesf',)], writes=PSA)
        else:
            for kc in range(16):
                sqb = S[kc % 2][:, 0:T // 2].bitcast(BF16)
                act(lambda e, kc=kc, sqb=sqb: e.activation(out=sqb, in_=xfc(kc), func=AF.Square),
                    reads=XF(kc), writes=[('S', kc % 2)])

                def fn(e, kc=kc, sqb=sqb):
                    ins = None
                    for nt, (c0, n) in enumerate(NTL):
                        ins = e.matmul(psA[:, nt, 0:n], lhsT=onesb[:, :], rhs=sqb[:, c0:c0 + n],
                                       start=(kc == 0), stop=(kc == 15))
                    return ins
                pe(fn, reads=[('S', kc % 2), ('onesb',)], writes=PSA)
        rt = S[2]
        act(lambda e: e.activation(out=rt[:, 0:1024], in_=psA[:, 0:2, :].rearrange("p a n -> p (a n)"),
                                   func=AF.Sqrt, bias=epsc[:, 0:1], scale=1.0 / D),
            reads=PSA[0:2] + [('epsc',)], writes=[('S', 2)])
        act(lambda e: e.activation(out=rt[:, 1024:T], in_=psA[:, 2, 0:64], func=AF.Sqrt, bias=epsc[:, 0:1],
                                   scale=1.0 / D),
            reads=PSA[2:3] + [('epsc',)], writes=[('S', 2)])
        dve(lambda e: e.reciprocal(out=rt[:, 0:T], in_=rt[:, 0:T]), reads=[('S', 2)], writes=[('S', 2)])
        for kc in KCO:
            if not final:
                dve(lambda e, kc=kc: e.scalar_tensor_tensor(out=xnT[:, kc, :], in0=xfc(kc), scalar=gcol[:, kc:kc + 1],
                                                            in1=rt[:, 0:T], op0=ALU.mult, op1=ALU.mult),
                    reads=XF(kc) + [('gcol',), ('S', 2)], writes=[('xnT', kc)])
            else:
                dve(lambda e, kc=kc: e.scalar_tensor_tensor(out=xfc(kc), in0=xfc(kc), scalar=gcol[:, kc:kc + 1],
                                                            in1=rt[:, 0:T], op0=ALU.mult, op1=ALU.mult),
                    reads=XF(kc) + [('gcol',), ('S', 2)], writes=XF(kc))
        if final:
            yrow = xnT[:, :, :].rearrange("p a t -> p (a t)").bitcast(F32)
            BK6 = [(psA, 0, ('ps', 0)), (psA, 1, ('ps', 1)), (psA, 2, ('ps', 2)),
                   (psB, 0, ('ps', 3)), (psB, 1, ('ps', 4)), (psB, 2, ('ps', 5))]
            k_ = 0
            for tb in range(NB):
                t0, n = blk(tb)
                hb = tb % 2
                for q4 in range(4):
                    pt, pb_, pk = BK6[k_ % 6]
                    k_ += 1

                    def fn(e, q4=q4, n=n, t0=t0, pt=pt, pb_=pb_):
                        ins = None
                        for i4 in range(4):
                            kc = q4 * 4 + i4
                            ins = e.matmul(pt[0:n, pb_, i4 * 128:i4 * 128 + 128], lhsT=xfc(kc)[:, t0:t0 + n],
                                           rhs=identf[:, :], start=True, stop=True)
                        return ins
                    pe(fn, reads=sum([XF(q4 * 4 + i) for i in range(4)], []) + [('identf',)], writes=[pk])
                    dst = yrow[0:n, hb * 2048 + q4 * 512: hb * 2048 + q4 * 512 + 512]
                    if k_ % 2 == 0:
                        A_copy(dst, pt[0:n, pb_, :], reads=[pk], writes=[('yrow', hb, q4)])
                    else:
                        dve(lambda e, dst=dst, pt=pt, pb_=pb_, n=n: e.tensor_copy(out=dst, in_=pt[0:n, pb_, :]),
                            reads=[pk], writes=[('yrow', hb, q4)])
                ld(y_d[ti, t0:t0 + n, :], yrow[0:n, hb * 2048:hb * 2048 + 2048],
                   reads=[('yrow', hb, q) for q in range(4)], writes=[('y_d', tb)])

    def conv_begin(K, carry, st_rows, xpad, XP, par):
        H = K - 1
        SWd = H + 8
        bs = H + 1024
        xp_s = xpad[:, bs:bs + 8 * SWd].rearrange("p (b w) -> p b w", w=SWd)
        strw = strow4[:, par, :]
        pmh = psM[:, par * 32:par * 32 + 8 * H]
        ldw(strw[0:8 * H, :], st_rows, writes=[('strow', par)])
        pe(lambda e: e.matmul(pmh, lhsT=strw[0:8 * H, :], rhs=identf[0:8 * H, 0:8 * H],
                              start=True, stop=True), reads=[('strow', par), ('identf',)], writes=PSM)
        A_copy(xpad[:, 0:H], carry, reads=[('carry',)], writes=XP)
        A_copy(xp_s[:, :, 0:H], pmh.rearrange("p (b h) -> p b h", h=H), reads=PSM, writes=XP)

    def conv_chunk(psg, PSG, K, wtap, bcol, carry, st_rows, out_rows, xpad, XP, acc, AC, wkeys, par, begun=False):
        H = K - 1
        SWd = H + 8
        bs = H + 1024
        xp_s = xpad[:, bs:bs + 8 * SWd].rearrange("p (b w) -> p b w", w=SWd)
        orw = orow4[:, par, :]
        stg = stage4[:, par, :]
        pmo = psM[0:9 * H, 128 + par * 128:256 + par * 128]
        PMO = PSM
        if not begun:
            conv_begin(K, carry, st_rows, xpad, XP, par)
        A_copy(xpad[:, H:H + 1024], psg[:, 0:2, :].rearrange("p a n -> p (a n)"), reads=PSG[0:2], writes=XP)
        A_copy(xp_s[:, :, H:H + 8], psg[:, 2, 0:64].rearrange("p (b t) -> p b t", t=8), reads=PSG[2:3], writes=XP)
        acc_s = acc[:, 1024:T].rearrange("p (b t) -> p b t", t=8)
        act(lambda e: e.activation(out=acc[:, 0:1024], in_=xpad[:, H:H + 1024], func=AF.Identity, bias=bcol,
                                   scale=wtap(K - 1)), reads=XP + wkeys, writes=AC)
        act(lambda e: e.activation(out=acc_s, in_=xp_s[:, :, H:H + 8], func=AF.Identity, bias=bcol,
                                   scale=wtap(K - 1)), reads=XP + wkeys, writes=AC)
        for k in range(K - 1):
            dve(lambda e, k=k: e.scalar_tensor_tensor(out=acc[:, 0:1024], in0=xpad[:, k:k + 1024], scalar=wtap(k),
                                                      in1=acc[:, 0:1024], op0=ALU.mult, op1=ALU.add),
                reads=XP + AC + wkeys, writes=AC)
            dve(lambda e, k=k: e.scalar_tensor_tensor(out=acc_s, in0=xp_s[:, :, k:k + 8], scalar=wtap(k),
                                                      in1=acc_s, op0=ALU.mult, op1=ALU.add),
                reads=XP + AC + wkeys, writes=AC)
        A_copy(carry, xpad[:, 1024:1024 + H], reads=XP, writes=[('carry',)])
        A_copy(stg[:, 0:8 * H].rearrange("p (b h) -> p b h", h=H), xp_s[:, :, 8:8 + H], reads=XP,
               writes=[('stage', par)])
        A_copy(stg[:, 8 * H:9 * H], xpad[:, 1024:1024 + H], reads=XP, writes=[('stage', par)])

        def finish():
            pe(lambda e: e.matmul(pmo, lhsT=stg[:, 0:9 * H], rhs=identf[:, :], start=True, stop=True),
               reads=[('stage', par), ('identf',)], writes=PMO)
            A_copy(orw[0:9 * H, :], pmo, reads=PMO, writes=[('orow', par)])
            ld(out_rows, orw[0:9 * H, :], reads=[('orow', par)])
        return acc, finish

    def ffn_phase(ti, layer):
        rmsnorm_phase(norm_ffn[layer])
        for k_ in range(3):
            ld(fcw[:, :, k_], ffn_conv_w[layer, k_].rearrange("(j p) -> p j", p=128), writes=[('fcw',)], nc_ok=True)
        ld(fcb[:, :], ffn_conv_b[layer].rearrange("(j p) -> p j", p=128), writes=[('fcw',)], nc_ok=True)
        wup = ffn_w_up[layer]
        wdn = ffn_w_down[layer]
        pend = [None]
        for half in range(2):
            for jj in range(22):
                j = half * 22 + jj
                xi, ai, gi = (0, 1, 2) if jj % 2 == 0 else (3, 4, 5)
                stf = st_f[layer, ti * 8:ti * 8 + 8, :, j * 128:j * 128 + 128].rearrange("b r c -> (b r) c")
                conv_begin(3, fcarry[:, layer, j, :], stf, S[xi], [('S', xi)], jj % 2)
                wi, wv = load_w2(wup, j * 128, DFF + j * 128)
                mm_fm(psA, PSA, lambda kc, wv=wv: wv[:, kc, 0:128], 16,
                      lambda kc, c0, n: xnT[:, kc, c0:c0 + n], [('w', wi)] + [('xnT', k) for k in range(16)])
                mm_fm(psB, PSB, lambda kc, wv=wv: wv[:, kc, 128:256], 16,
                      lambda kc, c0, n: xnT[:, kc, c0:c0 + n], [('w', wi)] + [('xnT', k) for k in range(16)])
                prev_fin = pend[0]
                acc, pend[0] = conv_chunk(
                    psA, PSA, 3, lambda k, j=j: fcw[:, j, k:k + 1], fcb[:, j:j + 1], fcarry[:, layer, j, :],
                    stf, o_f[ti, layer, :, :, j * 128:j * 128 + 128].rearrange("s r c -> (s r) c"),
                    S[xi], [('S', xi)], S[ai], [('S', ai)], [('fcw',)], jj % 2, begun=True)
                if prev_fin is not None:
                    prev_fin()
                gl = S[gi]
                act(lambda e, acc=acc, gl=gl: e.activation(out=gl[:, 0:T], in_=acc[:, 0:T], func=AF.Gelu_apprx_tanh),
                    reads=[('S', ai)], writes=[('S', gi)])
                dve(lambda e, gl=gl, jj=jj: e.tensor_tensor(out=big[:, jj, 0:1024], in0=gl[:, 0:1024],
                                                          in1=psB[:, 0:2, :].rearrange("p a n -> p (a n)"),
                                                          op=ALU.mult),
                    reads=[('S', gi)] + PSB[0:2], writes=[('big', jj)])
                dve(lambda e, gl=gl, jj=jj: e.tensor_tensor(out=big[:, jj, 1024:T], in0=gl[:, 1024:T],
                                                          in1=psB[:, 2, 0:64], op=ALU.mult),
                    reads=[('S', gi)] + PSB[2:3], writes=[('big', jj)])
            if pend[0] is not None:
                pend[0]()
                pend[0] = None
            if half == 1:
                prep_next_norm(norm_mix[layer + 1] if layer + 1 < depth else None)
            else:
                gfill[0] = False
            for m in range(16):
                wi, wv = load_wk(wdn, half * 22, 22, m * 128)
                psg, PSG = (psA, PSA) if m % 2 == 0 else (psB, PSB)
                mm_fm(psg, PSG, lambda kc, wv=wv: wv[:, kc, :], 22,
                      lambda kc, c0, n: big[:, kc, c0:c0 + n], [('w', wi)] + [('big', k) for k in range(22)])
                resid_add(psg, PSG, m, sq=(half == 1))

    def out_proj(wdram, gnext=None):
        prep_next_norm(gnext)
        for m in range(0, 16, 2):
            wi, wv = load_w2(wdram, m * 128, m * 128 + 128)
            for h2 in range(2):
                psg, PSG = (psA, PSA) if h2 == 0 else (psB, PSB)
                mm_fm(psg, PSG, lambda kc, wv=wv, h2=h2: wv[:, kc, h2 * 128:h2 * 128 + 128], 16,
                      lambda kc, c0, n: big[:, kc, c0:c0 + n], [('w', wi)] + [('big', k) for k in range(16)])
                resid_add(psg, PSG, m + h2, sq=True)

    from_even = {}

    def even_mixer(ti, layer):
        j = layer // 2
        rmsnorm_phase(norm_mix[layer])
        w = w_in_even[j]
        ALLX = [('xnT', k) for k in range(16)]
        rhsx = lambda kc, c0, n: xnT[:, kc, c0:c0 + n]
        for k_ in range(4):
            ld(ecw[:, :, k_], conv_even_w[j, k_].rearrange("(c p) -> p c", p=128), writes=[('ecw',)], nc_ok=True)
        ld(ecb[:, :], conv_even_b[j].rearrange("(c p) -> p c", p=128), writes=[('ecw',)], nc_ok=True)
        ld(rgp[:, 0, :], rg_ba[j].rearrange("(c p) -> p c", p=128), writes=[('rgp',)], nc_ok=True)
        ld(rgp[:, 1, :], rg_bx[j].rearrange("(c p) -> p c", p=128), writes=[('rgp',)], nc_ok=True)
        ld(rgp[:, 2, :], rg_lambda[j].rearrange("(c p) -> p c", p=128), writes=[('rgp',)], nc_ok=True)
        act(lambda e: e.activation(out=rgp[:, 3, :], in_=rgp[:, 2, :], func=AF.Exp, scale=-1.0),
            reads=[('rgp',)], writes=[('rgp',)])
        act(lambda e: e.activation(out=rgp[:, 3, :], in_=rgp[:, 3, :], func=AF.Ln, bias=onec[:, 0:1], scale=1.0),
            reads=[('rgp',), ('onec',)], writes=[('rgp',)])
        dve(lambda e: e.tensor_scalar(out=rgp[:, 4, :], in0=rgp[:, 3, :], scalar1=-8.0, scalar2=None, op0=ALU.mult),
            reads=[('rgp',)], writes=[('rgp',)])
        dve(lambda e: e.tensor_scalar(out=rgp[:, 5, :], in0=rgp[:, 3, :], scalar1=-16.0, scalar2=None, op0=ALU.mult),
            reads=[('rgp',)], writes=[('rgp',)])
        dve(lambda e: e.memset(wbd[:], 0.0), writes=[('wbd',)])
        for gi_, wsrc in enumerate([rg_wa[j], rg_wx[j]]):
            for half in range(2):
                ldw(wbd[half * 64:half * 64 + 64, :, gi_, half * 64:half * 64 + 64],
                    wsrc.rearrange("(c two) i o -> two i c o", two=2)[half], writes=[('wbd',)])

        R16 = big[:, 16:32, :].rearrange("p a t -> p (a t)").bitcast(F32)
        SETS = [
            [(S[i], [('S', i)]) for i in range(6)],
            [(R16[:, i * SW_:(i + 1) * SW_], [('S1', i)]) for i in range(6)],
        ]

        def rg_stage1(c):
            st_ = SETS[c % 2]
            (xpad, XP), (xcf, XC), (gg, GG) = st_[0], st_[1], st_[2]
            stc = st_conv[j, ti * 8:ti * 8 + 8, :, c * 128:c * 128 + 128].rearrange("b r c -> (b r) c")
            conv_begin(4, ecarry[:, j, c, :], stc, xpad, XP, c % 2)
            wi, wv = load_w2(w, c * 128, 2048 + c * 128)
            mm_fm(psA, PSA, lambda kc, wv=wv: wv[:, kc, 0:128], 16, rhsx, [('w', wi)] + ALLX)
            mm_fm(psB, PSB, lambda kc, wv=wv: wv[:, kc, 128:256], 16, rhsx, [('w', wi)] + ALLX)
            parh = 2 + c % 2
            strwh = strow4[:, parh, :]
            ldw(strwh[0:8, :], st_h[j, ti * 8:ti * 8 + 8, c * 128:c * 128 + 128], writes=[('strow', parh)])
            pT32 = psT[:, :].bitcast(F32)
            pe(lambda e: e.matmul(pT32[:, 384:392], lhsT=strwh[0:8, :], rhs=identf[0:8, 0:8], start=True, stop=True),
               reads=[('strow', parh), ('identf',)], writes=PST)
            A_copy(stage4[:, parh, 0:8], pT32[:, 384:392], reads=PST, writes=[('stage', parh)])
            xc, fin = conv_chunk(psA, PSA, 4, lambda k, c=c: ecw[:, c, k:k + 1], ecb[:, c:c + 1], ecarry[:, j, c, :],
                                 stc, o_conv[ti, j, :, :, c * 128:c * 128 + 128].rearrange("s r c -> (s r) c"),
                                 xpad, XP, xcf, XC, [('ecw',)], c % 2, begun=True)
            xcb1 = xpad[:, 0:T // 2].bitcast(BF16)
            dve(lambda e: e.tensor_copy(out=xcb1, in_=xc[:, 0:T]), reads=XC, writes=XP)
            act(lambda e: e.activation(out=gg[:, 0:1024], in_=psB[:, 0:2, :].rearrange("p a n -> p (a n)"),
                                       func=AF.Gelu_apprx_tanh), reads=PSB[0:2], writes=GG)
            act(lambda e: e.activation(out=gg[:, 1024:T], in_=psB[:, 2, 0:64], func=AF.Gelu_apprx_tanh), reads=PSB[2:3],
                writes=GG)
            return fin

        def rg_stage3(c):
            st_ = SETS[c % 2]
            (xpad, XP), (xc, XC), (gg, GG), (r_, RR), (i_, II), (a_, AA) = st_
            par = 2 + c % 2
            strw = strow4[:, par, :]
            orw = orow4[:, par, :]
            stg = stage4[:, par, :]
            pmx = psT[:, :].bitcast(F32)[:, 0:128]
            PMX = PST
            xcb = xpad[:, 0:T // 2].bitcast(BF16)
            for gi_, (psg, PSG) in enumerate([(psA, PSA), (psB, PSB)]):
                def fn(e, psg=psg, gi_=gi_):
                    ins = None
                    for nt, (c0, n) in enumerate(NTL):
                        ins = e.matmul(psg[:, nt, 0:n], lhsT=wbd[:, c, gi_, :], rhs=xcb[:, c0:c0 + n],
                                       start=True, stop=True)
                    return ins
                pe(fn, reads=[('wbd',)] + XP, writes=PSG)
            for (dst, DD, psg, PSG, bi) in [(r_, RR, psA, PSA, 0), (i_, II, psB, PSB, 1)]:
                act(lambda e, dst=dst, psg=psg, bi=bi: e.activation(
                    out=dst[:, 0:1024], in_=psg[:, 0:2, :].rearrange("p a n -> p (a n)"), func=AF.Sigmoid,
                    bias=rgp[:, bi, c:c + 1], scale=1.0), reads=PSG[0:2] + [('rgp',)], writes=DD)
                act(lambda e, dst=dst, psg=psg, bi=bi: e.activation(
                    out=dst[:, 1024:T], in_=psg[:, 2, 0:64], func=AF.Sigmoid, bias=rgp[:, bi, c:c + 1], scale=1.0),
                    reads=PSG[2:3] + [('rgp',)], writes=DD)
            act(lambda e: e.activation(out=a_[:, 0:T], in_=r_[:, 0:T], func=AF.Exp, scale=rgp[:, 4, c:c + 1]),
                reads=RR + [('rgp',)], writes=AA)
            act(lambda e: e.activation(out=r_[:, 0:T], in_=r_[:, 0:T], func=AF.Exp, scale=rgp[:, 5, c:c + 1]),
                reads=RR + [('rgp',)], writes=RR)
            dve(lambda e: e.tensor_scalar(out=r_[:, 0:T], in0=r_[:, 0:T], scalar1=-1.0, scalar2=1.0, op0=ALU.mult,
                                          op1=ALU.add), reads=RR, writes=RR)
            act(lambda e: e.activation(out=r_[:, 0:T], in_=r_[:, 0:T], func=AF.Sqrt), reads=RR, writes=RR)
            dve(lambda e: e.tensor_tensor(out=i_[:, 0:T], in0=i_[:, 0:T], in1=r_[:, 0:T], op=ALU.mult),
                reads=RR + II, writes=II)
            dve(lambda e: e.tensor_tensor(out=i_[:, 0:T], in0=i_[:, 0:T], in1=xc[:, 0:T], op=ALU.mult),
                reads=XC + II, writes=II)
            dve(lambda e: e.tensor_tensor_scan(out=r_[:, 0:1024], data0=a_[:, 0:1024], data1=i_[:, 0:1024],
                                               initial=hcarry[:, j, c:c + 1], op0=ALU.mult, op1=ALU.add),
                reads=AA + II + [('hcarry',)], writes=RR)
            for b in range(8):
                dve(lambda e, b=b: e.tensor_tensor_scan(out=r_[:, 1024 + b * 8:1032 + b * 8],
                                                        data0=a_[:, 1024 + b * 8:1032 + b * 8],
                                                        data1=i_[:, 1024 + b * 8:1032 + b * 8],
                                                        initial=stg[:, b:b + 1], op0=ALU.mult, op1=ALU.add),
                    reads=AA + II + [('stage', par)], writes=RR)
            dve(lambda e: e.tensor_copy(out=hcarry[:, j, c:c + 1], in_=r_[:, 1023:1024]), reads=RR,
                writes=[('hcarry',)])
            A_copy(stg[:, 16:24], r_[:, 1024:T].rearrange("p (b t) -> p b t", t=8)[:, :, 7], reads=RR,
                   writes=[('stage', par)])
            A_copy(stg[:, 24:25], r_[:, 1023:1024], reads=RR, writes=[('stage', par)])
            dve(lambda e: e.tensor_tensor(out=big[:, c, :], in0=gg[:, 0:T], in1=r_[:, 0:T], op=ALU.mult),
                reads=GG + RR, writes=[('big', c)])

            def finishH():
                pe(lambda e: e.matmul(pmx[0:9, 0:128], lhsT=stg[:, 16:25], rhs=identf[:, :], start=True, stop=True),
                   reads=[('stage', par), ('identf',)], writes=PMX)
                A_copy(orw[0:9, :], pmx[0:9, 0:128], reads=PMX, writes=[('orow', par)])
                ld(o_h[ti, j, :, c * 128:c * 128 + 128], orw[0:9, :], reads=[('orow', par)])
            return finishH

        fin = {0: rg_stage1(0)}
        finH = {}
        for c in range(8):
            if c + 1 < 8:
                fin[c + 1] = rg_stage1(c + 1)
            fin[c]()
            finH[c] = rg_stage3(c)
            if c >= 1:
                finH[c - 1]()
        finH[7]()

        wi, wv = load_w2(w, 5120 - 120, None)
        irow, frow = mlr[0], mlr[1]
        for gi_, dst in enumerate([irow, frow]):
            def fn(e, gi_=gi_, wv=wv):
                ins = None
                for kc in range(16):
                    for nt, (c0, n) in enumerate(NTL):
                        ins = e.matmul(psA[0:4, nt, 0:n], lhsT=wv[:, kc, 120 + gi_ * 4:124 + gi_ * 4],
                                       rhs=xnT[:, kc, c0:c0 + n], start=(kc == 0), stop=(kc == 15))
                return ins
            pe(fn, reads=[('w', wi)] + ALLX, writes=PSA)
            A_copy(dst[:, 0:1024], psA[0:4, 0:2, :].rearrange("p a n -> p (a n)"), reads=PSA[0:2],
                   writes=[('S', gi_)])
            A_copy(dst[:, 1024:T], psA[0:4, 2, 0:64], reads=PSA[2:3], writes=[('S', gi_)])
        ld(mlb[:, 0:2], ml_gate_b[j].rearrange("g h -> h g"), writes=[('mlb',)], nc_ok=True)
        dve(lambda e: e.tensor_scalar(out=mlb[:, 2:3], in0=mlb[:, 1:2], scalar1=-1.0, scalar2=None, op0=ALU.mult),
            reads=[('mlb',)], writes=[('mlb',)])
        MLR = [('S', i) for i in range(6)]
        G, A_, MU, WR = mlr[2], mlr[3], mlr[4], mlr[5]
        act(lambda e: e.activation(out=frow[:, 0:T], in_=frow[:, 0:T], func=AF.Exp, bias=mlb[:, 2:3], scale=-1.0),
            reads=MLR + [('mlb',)], writes=[('S', 1)])
        act(lambda e: e.activation(out=frow[:, 0:T], in_=frow[:, 0:T], func=AF.Ln, bias=onec[0:4, 0:1], scale=1.0),
            reads=MLR + [('onec',)], writes=[('S', 1)])
        ones4 = S[5][0:4, 0:1024]
        dve(lambda e: e.memset(ones4, 1.0), writes=[('S', 5)])
        dve(lambda e: e.tensor_tensor_scan(out=G[:, 0:1024], data0=ones4[:, 0:1024], data1=frow[:, 0:1024],
                                           initial=0.0, op0=ALU.mult, op1=ALU.add),
            reads=MLR + [('S', 5)], writes=[('S', 2)])
        for b in range(8):
            sl = slice(1024 + b * 8, 1032 + b * 8)
            dve(lambda e, sl=sl: e.tensor_tensor_scan(out=G[:, sl], data0=ones4[:, 0:8], data1=frow[:, sl],
                                                      initial=0.0, op0=ALU.mult, op1=ALU.add),
                reads=MLR + [('S', 5)], writes=[('S', 2)])
        dve(lambda e: e.scalar_tensor_tensor(out=A_[:, 0:T], in0=irow[:, 0:T], scalar=mlb[:, 0:1], in1=G[:, 0:T],
                                             op0=ALU.add, op1=ALU.add), reads=MLR + [('mlb',)], writes=[('S', 3)])
        ld(m0s[:, :], st_m[j, ti * 8:ti * 8 + 8, :].rearrange("b h -> h b"), writes=[('m0s',)], nc_ok=True)
        dve(lambda e: e.tensor_tensor_scan(out=MU[:, 0:1024], data0=ones4[:, 0:1024], data1=A_[:, 0:1024],
                                           initial=mcarry[:, j:j + 1], op0=ALU.mult, op1=ALU.max),
            reads=MLR + [('S', 5), ('mcarry',)], writes=[('S', 4)])
        for b in range(8):
            sl = slice(1024 + b * 8, 1032 + b * 8)
            dve(lambda e, sl=sl, b=b: e.tensor_tensor_scan(out=MU[:, sl], data0=ones4[:, 0:8], data1=A_[:, sl],
                                                           initial=m0s[:, b:b + 1], op0=ALU.mult, op1=ALU.max),
                reads=MLR + [('S', 5), ('m0s',)], writes=[('S', 4)])
        A_copy(mue[:, 0:8], MU[:, 0:1024].rearrange("p (n t) -> p n t", t=128)[:, :, 127], reads=MLR,
               writes=[('mue',)])
        A_copy(mue[:, 8:16], MU[:, 1024:T].rearrange("p (n t) -> p n t", t=8)[:, :, 7], reads=MLR, writes=[('mue',)])
        A_copy(muc[:, 0:1], mcarry[:, j:j + 1], reads=[('mcarry',)], writes=[('muc',)])
        A_copy(muc[:, 1:8], mue[:, 0:7], reads=[('mue',)], writes=[('muc',)])
        A_copy(muc[:, 8:16], m0s[:, 0:8], reads=[('m0s',)], writes=[('muc',)])
        dve(lambda e: e.tensor_tensor(out=dec[:, :], in0=muc[:, :], in1=mue[:, :], op=ALU.subtract),
            reads=[('muc',), ('mue',)], writes=[('dec',)])
        act(lambda e: e.activation(out=dec[:, :], in_=dec[:, :], func=AF.Exp), reads=[('dec',)], writes=[('dec',)])
        A_copy(mnew[:, 0:8], G[:, 1024:T].rearrange("p (n t) -> p n t", t=8)[:, :, 7], reads=MLR, writes=[('mnew',)])
        A_copy(mnew[:, 8:9], G[:, 1023:1024], reads=MLR, writes=[('mnew',)])
        dve(lambda e: e.tensor_tensor(out=mnew[:, 0:8], in0=mue[:, 8:16], in1=mnew[:, 0:8], op=ALU.subtract),
            reads=[('mue',), ('mnew',)], writes=[('mnew',)])
        dve(lambda e: e.tensor_tensor(out=mnew[:, 8:9], in0=mue[:, 7:8], in1=mnew[:, 8:9], op=ALU.subtract),
            reads=[('mue',), ('mnew',)], writes=[('mnew',)])
        dve(lambda e: e.tensor_copy(out=mcarry[:, j:j + 1], in_=mnew[:, 8:9]), reads=[('mnew',), ('muc',)],
            writes=[('mcarry',)])
        ld(o_m[ti, j].rearrange("s h -> h s"), mnew[:, 0:9], reads=[('mnew',)], nc_ok=True)
        WF = S[0][0:36, 0:T]
        dve(lambda e: e.memset(WF, 0.0), reads=MLR, writes=[('S', 0)])
        dve(lambda e: e.tensor_tensor(out=WR[:, 0:1024].rearrange("p (n t) -> p n t", t=128),
                                      in0=A_[:, 0:1024].rearrange("p (n t) -> p n t", t=128),
                                      in1=mue[:, 0:8].unsqueeze(2).to_broadcast([4, 8, 128]), op=ALU.subtract),
            reads=MLR + [('mue',)], writes=[('S', 5)])
        dve(lambda e: e.tensor_tensor(out=WR[:, 1024:T].rearrange("p (n t) -> p n t", t=8),
                                      in0=A_[:, 1024:T].rearrange("p (n t) -> p n t", t=8),
                                      in1=mue[:, 8:16].unsqueeze(2).to_broadcast([4, 8, 8]), op=ALU.subtract),
            reads=MLR + [('mue',)], writes=[('S', 5)])
        act(lambda e: e.activation(out=WF[0:4, 0:T], in_=WR[:, 0:T], func=AF.Exp), reads=MLR, writes=[('S', 0)])
        dve(lambda e: e.tensor_tensor(out=WR[:, 0:1024].rearrange("p (n t) -> p n t", t=128),
                                      in0=G[:, 0:1024].rearrange("p (n t) -> p n t", t=128),
                                      in1=mue[:, 0:8].unsqueeze(2).to_broadcast([4, 8, 128]), op=ALU.subtract),
            reads=MLR + [('mue',), ('S', 0)], writes=[('S', 5)])
        dve(lambda e: e.tensor_tensor(out=WR[:, 1024:T].rearrange("p (n t) -> p n t", t=8),
                                      in0=G[:, 1024:T].rearrange("p (n t) -> p n t", t=8),
                                      in1=mue[:, 8:16].unsqueeze(2).to_broadcast([4, 8, 8]), op=ALU.subtract),
            reads=MLR + [('mue',), ('S', 0)], writes=[('S', 5)])
        act(lambda e: e.activation(out=WF[32:36, 0:T], in_=WR[:, 0:T], func=AF.Exp), reads=MLR, writes=[('S', 0)])
        for tb in range(NB):
            t0, n = blk(tb)
            pe(lambda e, t0=t0, n=n: e.matmul(psM[0:n, 0:36], lhsT=WF[0:36, t0:t0 + n], rhs=identf[0:36, 0:36],
                                              start=True, stop=True), reads=[('S', 0), ('identf',)], writes=PSM)
            A_copy(wfc[0:n, tb, :], psM[0:n, 0:36], reads=PSM, writes=[('wfc',)])
        dve(lambda e: e.tensor_tensor(out=decx[:, :, :], in0=eye4[:, :, :],
                                      in1=dec[:, :].unsqueeze(1).to_broadcast([4, 4, 16]), op=ALU.mult),
            reads=[('dec',), ('eye4',)], writes=[('decx',)])
        pe(lambda e: e.matmul(psM[:, 0:64], lhsT=onesf[0:4, :], rhs=decx[:, :, :].rearrange("p a b -> p (a b)"),
                              start=True, stop=True), reads=[('decx',), ('onesf',)], writes=PSM)
        A_copy(decb[:, :, :].rearrange("p a b -> p (a b)"), psM[:, 0:64], reads=PSM, writes=[('decb',)])
        ld(gml[:, :], ml_norm_g[j].rearrange("h d -> (h d)").partition_broadcast(128), writes=[('gml',)])

        RB = big[:, 16:32, :].rearrange("p a t -> p (a t)")
        RF = RB.bitcast(F32)

        def rb(off, n):
            return RB[:, off // 2: off // 2 + n]

        qT = rb(0, T)
        kT = rb(2176, T)
        ktm = rb(4352, NB * 128).rearrange("p (b d) -> p b d", d=128)
        vex = rb(6656, NB * 258).rearrange("p (b d) -> p b d", d=258)
        osg = RF[:, 11304 // 4: 11304 // 4 + NB * 256].rearrange("p (b d) -> p b d", d=256)
        vwA = rb(20520, NB * 258).rearrange("p (b d) -> p b d", d=258)
        sTA = rb(25164, NB * 128).rearrange("p (b d) -> p b d", d=128)
        CdbA = rb(27468, 8 * 258).rearrange("p (b d) -> p b d", d=258)
        mloA = rb(31596, 3 * 256).rearrange("p (b d) -> p b d", d=256)
        qTm = rb(33132, 512).rearrange("p (b t) -> p b t", t=64)
        hmA = xres[0][:, 0:768].rearrange("p (b d) -> p b d", d=256)
        KQ = [('hq',), ('hk',)]
        KTM = [('ktm',)]
        BANKS6 = [(psA, 0), (psA, 1), (psA, 2), (psB, 0), (psB, 1), (psB, 2)]

        def bk(i):
            t_, b_ = BANKS6[i]
            return t_, b_, [('ps', b_ if t_ is psA else 3 + b_)]

        def head(h):
            wi, wv = load_w2(w, 1024 + h * 128, 1536 + h * 128)
            fins = []
            for (psg, PSG, half, cc, si) in [(psA, PSA, 0, 8 + h, 0), (psB, PSB, 1, 12 + h, 3)]:
                stc = st_conv[j, ti * 8:ti * 8 + 8, :, cc * 128:cc * 128 + 128].rearrange("b r c -> (b r) c")
                conv_begin(4, ecarry[:, j, cc, :], stc, S[si], [('S', si)], half)
                mm_fm(psg, PSG, lambda kc, wv=wv, half=half: wv[:, kc, half * 128:half * 128 + 128], 16, rhsx,
                      [('w', wi)] + ALLX)
                acc, fin_ = conv_chunk(psg, PSG, 4, lambda k, cc=cc: ecw[:, cc, k:k + 1], ecb[:, cc:cc + 1],
                                       ecarry[:, j, cc, :], stc,
                                       o_conv[ti, j, :, :, cc * 128:cc * 128 + 128].rearrange("s r c -> (s r) c"),
                                       S[si], [('S', si)], S[si + 1], [('S', si + 1)], [('ecw',)], half, begun=True)
                fins.append(fin_)
                if half == 0:
                    act(lambda e, acc=acc: e.activation(out=qT[:, 0:T], in_=acc[:, 0:T], func=AF.Silu),
                        reads=[('S', si + 1)], writes=[('hq',)])
                else:
                    act(lambda e, acc=acc: e.activation(out=S[5][:, 0:T], in_=acc[:, 0:T], func=AF.Silu),
                        reads=[('S', si + 1)], writes=[('S', 5)])
                    dve(lambda e: e.tensor_scalar(out=kT[:, 0:T], in0=S[5][:, 0:T], scalar1=128 ** -0.5,
                                                  scalar2=None, op0=ALU.mult), reads=[('S', 5)], writes=[('hk',)])
            wiv, wvv = load_w2(w, 3072 + h * 256, 3072 + h * 256 + 128)
            wio, wvo = load_w2(w, 4096 + h * 256, 4096 + h * 256 + 128)
            dve(lambda e: e.memset(vex[:, :, 256:257], 1.0), reads=[], writes=[('vex', t_) for t_ in range(NB)])

            def vo_mm(tb):
                t0, n = blk(tb)
                for (wi2, wv2, psg, pb) in [(wiv, wvv, psA, 0), (wio, wvo, psB, 3)]:
                    def fn(e, wv2=wv2, t0=t0, n=n, psg=psg, tb=tb):
                        ins = None
                        for kc in range(16):
                            ins = e.matmul(psg[0:n, tb % 3, 0:256], lhsT=xnT[:, kc, t0:t0 + n], rhs=wv2[:, kc, :],
                                           start=(kc == 0), stop=(kc == 15))
                        return ins
                    pe(fn, reads=[('w', wi2)] + ALLX, writes=[('ps', pb + tb % 3)])

            def vo_ev(tb):
                t0, n = blk(tb)
                A_copy(vex[0:n, tb, 0:256], psA[0:n, tb % 3, 0:256], reads=[('ps', tb % 3)], writes=[('vex', tb)])
                act(lambda e, n=n, tb=tb: e.activation(out=osg[0:n, tb, :], in_=psB[0:n, tb % 3, 0:256],
                                                       func=AF.Sigmoid), reads=[('ps', 3 + tb % 3)],
                    writes=[('osg', tb)])
            for tb in range(NB):
                vo_mm(tb)
                if tb >= 2:
                    vo_ev(tb - 2)
            vo_ev(NB - 2)
            vo_ev(NB - 1)
            for f_ in fins:
                f_()
            def fnT(e):
                ins = None
                for tb in range(8):
                    ins = e.transpose(out=psT[:, tb * 128:tb * 128 + 128], in_=kT[:, tb * 128:tb * 128 + 128],
                                      identity=identb[:, :])
                return ins
            pe(fnT, reads=KQ + [('identb',)], writes=PST)
            A_copy(ktm[:, 0:8, :], psT[:, :].rearrange("p (b d) -> p b d", d=128), reads=PST, writes=KTM)
            pe(lambda e: e.transpose(out=psT[0:64, 0:128], in_=kT[:, 1024:T], identity=identb[:, :]),
               reads=KQ + [('identb',)], writes=PST)
            A_copy(ktm[0:64, 8, :], psT[0:64, 0:128], reads=PST, writes=KTM)
            def sc_mm(tb):
                t0, n = blk(tb)
                pt, pb_, PK = bk(tb % 6)
                pe(lambda e, t0=t0, n=n, pt=pt, pb_=pb_: e.matmul(pt[0:n, pb_, 0:n], lhsT=kT[:, t0:t0 + n],
                                                                 rhs=qT[:, t0:t0 + n], start=True, stop=True),
                   reads=KQ, writes=PK)

            def sc_ev(tb):
                t0, n = blk(tb)
                pt, pb_, PK = bk(tb % 6)
                mask = causal if tb < 8 else bdc
                mkey = ('causal',) if tb < 8 else ('bdc',)
                dve(lambda e, n=n, mask=mask, pt=pt, pb_=pb_, tb=tb: e.tensor_tensor(
                    out=sTA[0:n, tb, 0:n], in0=pt[0:n, pb_, 0:n], in1=mask[0:n, 0:n], op=ALU.mult),
                    reads=PK + [mkey], writes=[('sT', tb)])
            for tb in range(NB):
                sc_mm(tb)
                if tb >= 4:
                    sc_ev(tb - 4)
            for tb in range(NB - 4, NB):
                sc_ev(tb)
            for tb in range(NB):
                t0, n = blk(tb)
                dve(lambda e, n=n, tb=tb: e.tensor_scalar(out=vwA[0:n, tb, 0:257], in0=vex[0:n, tb, 0:257],
                                                          scalar1=wfc[0:n, tb, h:h + 1], scalar2=None, op0=ALU.mult),
                    reads=[('vex', tb), ('wfc',)], writes=[('vw', tb)])
            Cpp = [Cst[:, j, h, :], Ctmp[:, :]]
            CK = [[('Cst',)], [('Ctmp',)]]

            def kv_mm(tb):
                pe(lambda e, tb=tb: e.matmul(psA[:, tb % 3, 0:257], lhsT=ktm[:, tb, :], rhs=vwA[:, tb, 0:257],
                                             start=True, stop=True), reads=KTM + [('vw', tb)],
                   writes=[('ps', tb % 3)])

            def chain(tb):
                prev, PK_ = Cpp[tb % 2], CK[tb % 2]
                new, NK_ = Cpp[(tb + 1) % 2], CK[(tb + 1) % 2]
                act(lambda e, tb=tb, prev=prev: e.activation(out=CdbA[:, tb, 0:257], in_=prev, func=AF.Copy,
                                                             scale=decb[:, h, tb:tb + 1]),
                    reads=PK_ + [('decb',)], writes=[('Cdb', tb)])
                dve(lambda e, tb=tb, prev=prev, new=new: e.scalar_tensor_tensor(
                    out=new, in0=prev, scalar=decb[:, h, tb:tb + 1], in1=psA[:, tb % 3, 0:257], op0=ALU.mult,
                    op1=ALU.add), reads=PK_ + [('decb',), ('ps', tb % 3)], writes=NK_)
            for tb in range(3):
                kv_mm(tb)
            for tb in range(8):
                chain(tb)
                if tb + 3 < 8:
                    kv_mm(tb + 3)
            ld(o_C[ti, j, 8, h], Cst[:, j, h, 0:256], reads=[('Cst',)])
            dve(lambda e: e.tensor_copy(out=nout[:, 8, h:h + 1], in_=Cst[:, j, h, 256:257]), reads=[('Cst',)],
                writes=[('nout',)])

            def norm_group(items):
                for (tb, i, pt, pb_, PK) in items:
                    t0, n = blk(tb)
                    act(lambda e, n=n, i=i, pt=pt, pb_=pb_: e.activation(out=sm[0:n, i, 0:1],
                                                                         in_=pt[0:n, pb_, 256:257], func=AF.Abs),
                        reads=PK, writes=[('sm', i)])
                for (tb, i, pt, pb_, PK) in items:
                    t0, n = blk(tb)
                    dve(lambda e, n=n, i=i, tb=tb: e.tensor_tensor(out=sm[0:n, i, 0:1], in0=sm[0:n, i, 0:1],
                                                                   in1=wfc[0:n, tb, 32 + h:33 + h], op=ALU.max),
                        reads=[('sm', i), ('wfc',)], writes=[('sm', i)])
                    dve(lambda e, n=n, i=i: e.reciprocal(out=sm[0:n, i, 1:2], in_=sm[0:n, i, 0:1]),
                        reads=[('sm', i)], writes=[('sm', i)])
                for (tb, i, pt, pb_, PK) in items:
                    t0, n = blk(tb)
                    act(lambda e, n=n, i=i, pt=pt, pb_=pb_: e.activation(out=hmA[0:n, i, :], in_=pt[0:n, pb_, 0:256],
                                                                         func=AF.Copy, scale=sm[0:n, i, 1:2]),
                        reads=PK + [('sm', i)], writes=[('hm', i)])
                for (tb, i, pt, pb_, PK) in items:
                    t0, n = blk(tb)
                    act(lambda e, n=n, i=i: e.activation(out=mloA[0:n, i, :], in_=hmA[0:n, i, :], func=AF.Square,
                                                         accum_out=sm[0:n, i, 2:3]), reads=[('hm', i)],
                        writes=[('sm', i), ('mlo', i)])
                for (tb, i, pt, pb_, PK) in items:
                    t0, n = blk(tb)
                    act(lambda e, n=n, i=i: e.activation(out=sm[0:n, i, 3:4], in_=sm[0:n, i, 2:3], func=AF.Sqrt,
                                                         bias=epsc[0:n, 0:1], scale=1.0 / 256),
                        reads=[('sm', i), ('epsc',)], writes=[('sm', i)])
                for (tb, i, pt, pb_, PK) in items:
                    t0, n = blk(tb)
                    dve(lambda e, n=n, i=i: e.reciprocal(out=sm[0:n, i, 4:5], in_=sm[0:n, i, 3:4]),
                        reads=[('sm', i)], writes=[('sm', i)])
                    dve(lambda e, n=n, i=i: e.scalar_tensor_tensor(out=hmA[0:n, i, :], in0=hmA[0:n, i, :],
                                                                  scalar=sm[0:n, i, 4:5],
                                                                  in1=gml[0:n, h * 256:h * 256 + 256], op0=ALU.mult,
                                                                  op1=ALU.mult),
                        reads=[('hm', i), ('sm', i), ('gml',)], writes=[('hm', i)])
                    dve(lambda e, n=n, i=i, tb=tb: e.tensor_tensor(out=mloA[0:n, i, :], in0=hmA[0:n, i, :],
                                                                   in1=osg[0:n, tb, :], op=ALU.mult),
                        reads=[('hm', i), ('osg', tb)], writes=[('mlo', i)])

                def fnT2(e):
                    ins = None
                    for (tb, i, pt, pb_, PK) in items:
                        t0, n = blk(tb)
                        for half in range(2):
                            ins = e.transpose(out=psT[:, (2 * i + half) * 128:(2 * i + half) * 128 + n],
                                              in_=mloA[0:n, i, half * 128:half * 128 + 128],
                                              identity=identb[0:n, 0:n])
                    return ins
                pe(fnT2, reads=[('mlo', i_) for (_, i_, _, _, _) in items] + [('identb',)], writes=PST)
                for (tb, i, pt, pb_, PK) in items:
                    t0, n = blk(tb)
                    for half in range(2):
                        cidx = 8 + 2 * h + half
                        A_copy(big[:, cidx, t0:t0 + n], psT[:, (2 * i + half) * 128:(2 * i + half) * 128 + n],
                               reads=PST, writes=[('big', cidx)])

            for grp in [[0, 1, 2], [3, 4, 5], [6, 7]]:
                items = []
                for tb in grp:
                    t0, n = blk(tb)

                    def fnN(e, t0=t0, n=n, tb=tb):
                        e.matmul(psB[0:n, tb % 3, 0:257], lhsT=qT[:, t0:t0 + n], rhs=CdbA[:, tb, 0:257], start=True,
                                 stop=False)
                        return e.matmul(psB[0:n, tb % 3, 0:257], lhsT=sTA[0:n, tb, 0:n], rhs=vwA[0:n, tb, 0:257],
                                        start=False, stop=True)
                    pe(fnN, reads=KQ + [('Cdb', tb), ('sT', tb), ('vw', tb)], writes=[('ps', 3 + tb % 3)])
                    items.append((tb, tb % 3, psB, tb % 3, [('ps', 3 + tb % 3)]))
                norm_group(items)
            tb = 8
            dve(lambda e: e.memset(qTm[:, :, :], 0.0), reads=[], writes=[('qTm',)])
            for b in range(8):
                dve(lambda e, b=b: e.tensor_copy(out=qTm[:, b, b * 8:b * 8 + 8], in_=qT[:, 1024 + b * 8:1032 + b * 8]),
                    reads=KQ, writes=[('qTm',)])
            dve(lambda e: e.tensor_scalar(out=wbs[0:64, :], in0=ind[0:64, :], scalar1=wfc[0:64, tb, h:h + 1],
                                          scalar2=None, op0=ALU.mult), reads=[('ind',), ('wfc',)], writes=[('wbs',)])
            NCI = len(Cin)

            def s_load(b):
                cb_ = b % NCI
                ldw(Cin[cb_][:, 0:256], st_C[j, ti * 8 + b, h], writes=[('Cin', cb_)])

            def s_comp(b):
                cb_ = b % NCI
                dve(lambda e: e.tensor_copy(out=Cin[cb_][:, 256:257], in_=nin[:, b, h:h + 1]),
                    reads=[('nin',)], writes=[('Cin', cb_)])
                act(lambda e: e.activation(out=CdbA[:, b, 0:257], in_=Cin[cb_][:, 0:257], func=AF.Copy,
                                           scale=decb[:, h, 8 + b:9 + b]),
                    reads=[('Cin', cb_), ('decb',)], writes=[('Cdb', b)])
                pe(lambda e: e.matmul(psA[0:64, 0, 0:257], lhsT=qTm[:, b, :], rhs=CdbA[:, b, 0:257],
                                      start=(b == 0), stop=False), reads=[('qTm',), ('Cdb', b)], writes=[('ps', 0)])
                dve(lambda e: e.tensor_scalar(out=vwb[0:64, b % 2, 0:257], in0=vex[0:64, tb, 0:257],
                                              scalar1=wbs[0:64, b:b + 1], scalar2=None, op0=ALU.mult),
                    reads=[('vex', tb), ('wbs',)], writes=[('vwb', b % 2)])
                pe(lambda e: e.matmul(psB[:, b % 3, 0:257], lhsT=ktm[0:64, tb, :], rhs=vwb[0:64, b % 2, 0:257],
                                      start=True, stop=True), reads=KTM + [('vwb', b % 2)],
                   writes=[('ps', 3 + b % 3)])
                dve(lambda e: e.scalar_tensor_tensor(
                    out=Cin[cb_][:, 0:257], in0=Cin[cb_][:, 0:257], scalar=decb[:, h, 8 + b:9 + b],
                    in1=psB[:, b % 3, 0:257], op0=ALU.mult, op1=ALU.add),
                    reads=[('Cin', cb_), ('decb',), ('ps', 3 + b % 3)], writes=[('Cin', cb_)])
                dve(lambda e: e.tensor_copy(out=nout[:, b, h:h + 1], in_=Cin[cb_][:, 256:257]),
                    reads=[('Cin', cb_)], writes=[('nout',)])
                ld(o_C[ti, j, b, h], Cin[cb_][:, 0:256], reads=[('Cin', cb_)])
            for b in range(min(NCI - 1, 8)):
                s_load(b)
            for b in range(8):
                if b + NCI - 1 < 8:
                    s_load(b + NCI - 1)
                s_comp(b)
            pe(lambda e: e.matmul(psA[0:64, 0, 0:257], lhsT=sTA[0:64, tb, 0:64], rhs=vwA[0:64, tb, 0:257],
                                  start=False, stop=True), reads=[('sT', tb), ('vw', tb)], writes=[('ps', 0)])
            norm_group([(tb, 0, psA, 0, [('ps', 0)])])
        for b_ in range(8):
            ld(nin[:, b_, :], st_n[j, ti * 8 + b_].rearrange("h d -> d h"), writes=[('nin',)], nc_ok=True)
        for h_ in range(4):
            head(h_)
        for s_ in range(9):
            ld(o_n[ti, j, s_].rearrange("h d -> d h"), nout[:, s_, :], reads=[('nout',)], nc_ok=True)
        out_proj(w_out_even[j], norm_ffn[layer])

    rgp = sb("rgp", [128, 6, 8])
    onec = sb("onec", [128, 1])
    wbd = sb("wbd", [128, 8, 2, 128], BF16)
    mlr = [S[i][0:4, 0:T] for i in range(6)]
    mlb = sb("mlb", [4, 3])
    m0s = sb("m0s", [4, 8])
    mue = sb("mue", [4, 16])
    muc = sb("muc", [4, 16])
    dec = sb("dec", [4, 16])
    decx = sb("decx", [4, 4, 16])
    decb = sb("decb", [128, 4, 16])
    mnew = sb("mnew", [4, 9])
    wfc = sb("wfc", [128, NB, 36])
    gml = xres[1][:, 0:1024]
    wbs = sb("wbs", [64, 8])
    vwb = sb("vwb", [64, 2, 258], BF16)
    Cin = [sb("Cin%d" % i, [128, 257]) for i in range(4)]
    nin = sb("nin", [128, 8, 4])
    nout = sb("nout", [128, 9, 4])
    sm = sb("sm", [128, 3, 8])
    Ctmp = sb("Ctmp", [128, 257])
    dve(lambda e: e.memset(onec[:], 1.0), writes=[('onec',)])

    def odd_mixer(ti, layer):
        j = layer // 2
        sqacc = xres[0]
        wsn = S[3][:, 0:1024].rearrange("p (g s) -> p g s", s=128)
        bbc = S[3][:, 0:1024]
        rmsnorm_phase(norm_mix[layer])
        w = w_in_odd[j]
        ALLX = [('xnT', k) for k in range(16)]
        rhsx = lambda kc, c0, n: xnT[:, kc, c0:c0 + n]
        ld(lng[:, 0, :], sgu_ln_g[j].rearrange("(c p) -> p c", p=128), writes=[('lng',)], nc_ok=True)
        ld(lng[:, 1, :], sgu_ln_b[j].rearrange("(c p) -> p c", p=128), writes=[('lng',)], nc_ok=True)
        for c in range(0, 16, 2):
            wi, wv = load_w2(w, 2048 + c * 128, 2048 + c * 128 + 128)
            for h2 in range(2):
                cc = c + h2
                mm_fm(psA, PSA, lambda kc, wv=wv, h2=h2: wv[:, kc, h2 * 128:h2 * 128 + 128], 16, rhsx,
                      [('w', wi)] + ALLX)
                vg = S[cc % 2]
                act(lambda e, vg=vg: e.activation(out=vg[:, 0:1024], in_=psA[:, 0:2, :].rearrange("p a n -> p (a n)"),
                                                  func=AF.Gelu_apprx_tanh), reads=PSA[0:2], writes=[('S', cc % 2)])
                act(lambda e, vg=vg: e.activation(out=vg[:, 1024:T], in_=psA[:, 2, 0:64], func=AF.Gelu_apprx_tanh),
                    reads=PSA[2:3], writes=[('S', cc % 2)])
                if cc == 0:
                    act(lambda e, vg=vg: e.activation(out=sqacc[:, 0:T], in_=vg[:, 0:T], func=AF.Square),
                        reads=[('S', cc % 2)], writes=[('xres', 0)])
                    dve(lambda e, vg=vg: e.tensor_copy(out=S[4][:, 0:T], in_=vg[:, 0:T]), reads=[('S', cc % 2)],
                        writes=[('S', 4)])
                else:
                    vq = S[2 + cc % 2]
                    act(lambda e, vg=vg, vq=vq: e.activation(out=vq[:, 0:T], in_=vg[:, 0:T], func=AF.Square),
                        reads=[('S', cc % 2)], writes=[('S', 2 + cc % 2)])
                    dve(lambda e, vg=vg: e.tensor_tensor(out=S[4][:, 0:T], in0=S[4][:, 0:T], in1=vg[:, 0:T],
                                                         op=ALU.add), reads=[('S', 4), ('S', cc % 2)],
                        writes=[('S', 4)])
                    dve(lambda e, vq=vq: e.tensor_tensor(out=sqacc[:, 0:T], in0=sqacc[:, 0:T], in1=vq[:, 0:T],
                                                         op=ALU.add), reads=[('xres', 0), ('S', 2 + cc % 2)],
                        writes=[('xres', 0)])
                dve(lambda e, vg=vg, cc=cc: e.tensor_copy(out=big[:, 16 + cc, 0:1024], in_=vg[:, 0:1024]),
                    reads=[('S', cc % 2)], writes=[('big', 16 + cc)])
                dve(lambda e, vg=vg, cc=cc: e.tensor_copy(out=vs32[:, cc, :], in_=vg[:, 1024:T]),
                    reads=[('S', cc % 2)], writes=[('vs32',)])

        def fnS1(e):
            ins = None
            for nt, (c0, n) in enumerate(NTL):
                ins = e.matmul(psB[:, nt, 0:n], lhsT=onesf[:, :], rhs=S[4][:, c0:c0 + n], start=True, stop=True)
            return ins
        pe(fnS1, reads=[('S', 4), ('onesf',)], writes=PSB)

        def fnS2(e):
            ins = None
            for nt, (c0, n) in enumerate(NTL):
                ins = e.matmul(psA[:, nt, 0:n], lhsT=onesf[:, :], rhs=sqacc[:, c0:c0 + n], start=True, stop=True)
            return ins
        pe(fnS2, reads=[('xres', 0), ('onesf',)], writes=PSA)
        mean = S[4]
        rstd = S[5]
        act(lambda e: e.activation(out=mean[:, 0:1024], in_=psB[:, 0:2, :].rearrange("p a n -> p (a n)"),
                                   func=AF.Copy, scale=1.0 / D), reads=PSB[0:2], writes=[('S', 4)])
        act(lambda e: e.activation(out=mean[:, 1024:T], in_=psB[:, 2, 0:64], func=AF.Copy, scale=1.0 / D),
            reads=PSB[2:3], writes=[('S', 4)])
        dve(lambda e: e.tensor_tensor(out=rstd[:, 0:T], in0=mean[:, 0:T], in1=mean[:, 0:T], op=ALU.mult),
            reads=[('S', 4)], writes=[('S', 5)])
        dve(lambda e: e.scalar_tensor_tensor(out=rstd[:, 0:1024], in0=psA[:, 0:2, :].rearrange("p a n -> p (a n)"),
                                             scalar=1.0 / D, in1=rstd[:, 0:1024], op0=ALU.mult, op1=ALU.subtract),
            reads=PSA[0:2] + [('S', 5)], writes=[('S', 5)])
        dve(lambda e: e.scalar_tensor_tensor(out=rstd[:, 1024:T], in0=psA[:, 2, 0:64], scalar=1.0 / D,
                                             in1=rstd[:, 1024:T], op0=ALU.mult, op1=ALU.subtract),
            reads=PSA[2:3] + [('S', 5)], writes=[('S', 5)])
        act(lambda e: e.activation(out=rstd[:, 0:T], in_=rstd[:, 0:T], func=AF.Sqrt, bias=epsc[:, 0:1], scale=1.0),
            reads=[('S', 5), ('epsc',)], writes=[('S', 5)])
        dve(lambda e: e.reciprocal(out=rstd[:, 0:T], in_=rstd[:, 0:T]), reads=[('S', 5)], writes=[('S', 5)])
        for c in range(0, 16, 2):
            wi, wv = load_w2(w, c * 128, c * 128 + 128)
            for h2 in range(2):
                cc = c + h2
                psg, PSG = (psA, PSA) if h2 == 0 else (psB, PSB)
                mm_fm(psg, PSG, lambda kc, wv=wv, h2=h2: wv[:, kc, h2 * 128:h2 * 128 + 128], 16, rhsx,
                      [('w', wi)] + ALLX)
                act(lambda e, psg=psg, cc=cc: e.activation(out=big[:, cc, 0:1024],
                                                           in_=psg[:, 0:2, :].rearrange("p a n -> p (a n)"),
                                                           func=AF.Gelu_apprx_tanh), reads=PSG[0:2], writes=[('big', cc)])
                act(lambda e, psg=psg, cc=cc: e.activation(out=big[:, cc, 1024:T], in_=psg[:, 2, 0:64], func=AF.Gelu_apprx_tanh),
                    reads=PSG[2:3], writes=[('big', cc)])
        vtm = xnT[:, :, :].rearrange("p a t -> p (a t)")[:, 0:8 * 2048].rearrange("p (n c) -> p n c", c=2048)
        VTM = [('xnT', k) for k in range(16)]
        for cc in range(16):
            t1 = S[cc % 2]
            dve(lambda e, cc=cc, t1=t1: e.tensor_tensor(out=t1[:, 0:1024], in0=big[:, 16 + cc, 0:1024],
                                                        in1=mean[:, 0:1024], op=ALU.subtract),
                reads=[('big', 16 + cc), ('S', 4)], writes=[('S', cc % 2)])
            dve(lambda e, cc=cc, t1=t1: e.tensor_tensor(out=t1[:, 1024:T], in0=vs32[:, cc, :], in1=mean[:, 1024:T],
                                                        op=ALU.subtract), reads=[('vs32',), ('S', 4)],
                writes=[('S', cc % 2)])
            dve(lambda e, t1=t1: e.tensor_tensor(out=t1[:, 0:T], in0=t1[:, 0:T], in1=rstd[:, 0:T], op=ALU.mult),
                reads=[('S', cc % 2), ('S', 5)], writes=[('S', cc % 2)])
            act(lambda e, cc=cc, t1=t1: e.activation(out=big[:, 16 + cc, 0:1024], in_=t1[:, 0:1024], func=AF.Identity,
                                                     bias=lng[:, 1, cc:cc + 1], scale=lng[:, 0, cc:cc + 1]),
                reads=[('S', cc % 2), ('lng',)], writes=[('big', 16 + cc)])
            act(lambda e, cc=cc, t1=t1: e.activation(out=vs32[:, cc, :], in_=t1[:, 1024:T], func=AF.Identity,
                                                     bias=lng[:, 1, cc:cc + 1], scale=lng[:, 0, cc:cc + 1]),
                reads=[('S', cc % 2), ('lng',)], writes=[('vs32',)])
            def tr_(cc):
                for q2 in range(2):
                    kslot = (cc * 2 + q2) % 7
                    if kslot == 6:
                        pv, pk = psT[:, 0:1024], ('ps', 7)
                    else:
                        pt_, pb__ = (psA, kslot) if kslot < 3 else (psB, kslot - 3)
                        pv, pk = pt_[:, pb__, :].bitcast(BF16), ('ps', kslot)

                    def fnT(e, cc=cc, q2=q2, pv=pv):
                        ins = None
                        for i4 in range(4):
                            n_ = q2 * 4 + i4
                            ins = e.transpose(out=pv[:, i4 * 128:i4 * 128 + 128],
                                              in_=big[:, 16 + cc, n_ * 128:n_ * 128 + 128], identity=identb[:, :])
                        return ins
                    pe(fnT, reads=[('big', 16 + cc), ('identb',)], writes=[pk])
                    A_copy(vtm[:, q2 * 4:q2 * 4 + 4, cc * 128:cc * 128 + 128],
                           pv[:, 0:512].rearrange("p (a n) -> p a n", n=128), reads=[pk], writes=VTM)
                pe(lambda e, cc=cc: e.matmul(psM[0:64, 0:128], lhsT=vs32[:, cc, :], rhs=identf[:, :], start=True,
                                             stop=True), reads=[('vs32',), ('identf',)], writes=PSM)
                ob = ovst[cc % 2]
                A_copy(ob[0:64, :], psM[0:64, 0:128], reads=PSM, writes=[('ovst', cc % 2)])
                ld(o_v[ti, j, :, cc * 128:cc * 128 + 128], ob[0:64, :], reads=[('ovst', cc % 2)])
                dve(lambda e, cc=cc, ob=ob: e.tensor_copy(out=vstm[0:64, cc * 128:cc * 128 + 128], in_=ob[0:64, :]),
                    reads=[('ovst', cc % 2)], writes=[('vstm',)])
            if cc >= 1:
                tr_(cc - 1)
        tr_(15)
        ld(wsn[:, :, :], sgu_ws[j].rearrange("g t s -> t g s"), writes=[('S', 3)])
        for g in range(8):
            pe(lambda e, g=g: e.matmul(psM[:, 0:128], lhsT=wsn[:, g, :], rhs=identf[:, :], start=True, stop=True),
               reads=[('S', 3), ('identf',)], writes=PSM)
            dve(lambda e, g=g: e.tensor_tensor(out=wsT[:, g, :], in0=psM[:, 0:128], in1=causal[:, :], op=ALU.mult),
                reads=PSM + [('causal',)], writes=[('wsT',)])
        pe(lambda e: e.matmul(psM[0:64, 0:512].rearrange("p (g b t) -> p g b t", g=8, b=8),
                              lhsT=rep[0:8, 0:64],
                              rhs=wsT[0:8, :, 0:8].unsqueeze(2).to_broadcast([8, 8, 8, 8]), start=True, stop=True),
           reads=[('wsT',), ('rep',)], writes=PSM)
        dve(lambda e: e.tensor_tensor(out=bdT[0:64, :, :], in0=psM[0:64, 0:512].rearrange("p (g n) -> p g n", g=8),
                                      in1=bdm[0:64, :].unsqueeze(1).to_broadcast([64, 8, 64]), op=ALU.mult),
            reads=PSM + [('bdm',)], writes=[('bdT',)])
        ld(bbc[:, :], sgu_b[j].rearrange("g t -> (g t)").partition_broadcast(128), writes=[('S', 3)])
        BK6 = [(psA, 0, ('ps', 0)), (psA, 1, ('ps', 1)), (psA, 2, ('ps', 2)),
               (psB, 0, ('ps', 3)), (psB, 1, ('ps', 4)), (psB, 2, ('ps', 5))]
        kq = [0]
        TMPS = [S[0], S[1], S[2]]
        for g in range(8):
            for half in range(2):
                cc = 2 * g + half
                for q2 in range(3):
                    pt, pb_, pk = BK6[kq[0] % 6]
                    tmp = TMPS[kq[0] % 3]
                    tk_ = [('S', kq[0] % 3)]
                    kq[0] += 1
                    if q2 < 2:
                        def fnM(e, g=g, cc=cc, q2=q2, pt=pt, pb_=pb_):
                            ins = None
                            for i4 in range(4):
                                n_ = q2 * 4 + i4
                                ins = e.matmul(pt[:, pb_, i4 * 128:i4 * 128 + 128],
                                               lhsT=vtm[:, n_, cc * 128:cc * 128 + 128], rhs=wsT[:, g, :],
                                               start=True, stop=True)
                            return ins
                        pe(fnM, reads=VTM + [('wsT',)], writes=[pk])
                        dve(lambda e, g=g, tmp=tmp, pt=pt, pb_=pb_: e.tensor_tensor(
                            out=tmp[:, 0:512].rearrange("p (a t) -> p a t", t=128),
                            in0=pt[:, pb_, 0:512].rearrange("p (a t) -> p a t", t=128),
                            in1=bbc[:, g * 128:g * 128 + 128].unsqueeze(1).to_broadcast([128, 4, 128]), op=ALU.add),
                            reads=[pk, ('S', 3)], writes=tk_)
                        dve(lambda e, cc=cc, q2=q2, tmp=tmp: e.tensor_tensor(
                            out=big[:, cc, q2 * 512:q2 * 512 + 512], in0=big[:, cc, q2 * 512:q2 * 512 + 512],
                            in1=tmp[:, 0:512], op=ALU.mult), reads=tk_ + [('big', cc)], writes=[('big', cc)])
                    else:
                        pe(lambda e, g=g, cc=cc, pt=pt, pb_=pb_: e.matmul(
                            pt[:, pb_, 0:64], lhsT=vstm[0:64, cc * 128:cc * 128 + 128], rhs=bdT[0:64, g, :],
                            start=True, stop=True), reads=[('vstm',), ('bdT',)], writes=[pk])
                        dve(lambda e, g=g, tmp=tmp, pt=pt, pb_=pb_: e.tensor_tensor(
                            out=tmp[:, 0:64].rearrange("p (b t) -> p b t", t=8),
                            in0=pt[:, pb_, 0:64].rearrange("p (b t) -> p b t", t=8),
                            in1=bbc[:, g * 128:g * 128 + 8].unsqueeze(1).to_broadcast([128, 8, 8]), op=ALU.add),
                            reads=[pk, ('S', 3)], writes=tk_)
                        dve(lambda e, cc=cc, tmp=tmp: e.tensor_tensor(out=big[:, cc, 1024:T], in0=big[:, cc, 1024:T],
                                                                      in1=tmp[:, 0:64], op=ALU.mult),
                            reads=tk_ + [('big', cc)], writes=[('big', cc)])
        out_proj(w_out_odd[j], norm_ffn[layer])

    lng = sb("lng", [128, 2, 16])
    vs32 = sb("vs32", [128, 16, 64])
    ovst = [sb("ovst%d" % i, [64, 128]) for i in range(2)]
    vstm = sb("vstm", [64, 2048], BF16)
    wsT = sb("wsT", [128, 8, 128], BF16)
    bdT = sb("bdT", [64, 8, 64], BF16)

    for ti in range(2):
        load_x_tile(ti)
        for layer in range(depth):
            if layer % 2 == 0:
                even_mixer(ti, layer)
            else:
                odd_mixer(ti, layer)
            ffn_phase(ti, layer)
        rmsnorm_phase(norm_final, final=True, ti=ti)
    tk.finish()
    tk.emit()
    st.close()
    return nc


_CONST = None


def _consts():
    global _CONST
    if _CONST is None:
        bf = ml_dtypes.bfloat16
        c = {}
        c["c_identf"] = np.eye(128, dtype=np.float32)
        c["c_identb"] = np.eye(128, dtype=np.float32).astype(bf)
        c["c_onesb"] = np.ones((128, 128), np.float32).astype(bf)
        c["c_onesf"] = np.ones((128, 128), np.float32)
        c["c_causal"] = np.triu(np.ones((128, 128), np.float32))
        bd = np.kron(np.eye(8, dtype=np.float32), np.ones((8, 8), np.float32))
        c["c_bd"] = bd
        c["c_bdc"] = bd * np.triu(np.ones((64, 64), np.float32))
        c["c_ind"] = np.kron(np.eye(8, dtype=np.float32), np.ones((8, 1), np.float32))
        e4 = np.zeros((4, 4, 16), np.float32)
        for i in range(4):
            e4[i, i, :] = 1.0
        c["c_eye4"] = e4
        c["c_rep"] = np.tile(np.eye(8, dtype=np.float32), (1, 8)).astype(bf)
        _CONST = c
    return _CONST


_NC = {}


def kernel(**inputs):
    depth = DEPTH
    if depth not in _NC:
        _NC[depth] = build(depth)
    nc = _NC[depth]
    f = lambda k: np.ascontiguousarray(np.asarray(inputs[k], dtype=np.float32))
    xp = f("x_prompt")
    xs = f("x_sample")
    wnames = ["norm_mix", "norm_ffn", "norm_final", "w_in_even", "conv_even_w", "conv_even_b", "rg_wa", "rg_ba",
              "rg_wx", "rg_bx", "rg_lambda", "ml_gate_b", "ml_norm_g", "w_out_even", "w_in_odd", "sgu_ln_g",
              "sgu_ln_b", "sgu_ws", "sgu_b", "w_out_odd", "ffn_w_up", "ffn_conv_w", "ffn_conv_b", "ffn_w_down"]
    shared = {k: f(k) for k in wnames}
    shared.update(_consts())
    sc, sh, sC, sn, smm, sf = (f("state_conv_mix"), f("state_rglru_h"), f("state_mlstm_C"), f("state_mlstm_n"),
                               f("state_mlstm_m"), f("state_ffn_conv"))
    in_maps = []
    for c in range(8):
        p = c % 4
        xin = np.empty((2, T, D), np.float32)
        for ti in range(2):
            xin[ti, :TP] = xp[p, ti * TP:(ti + 1) * TP]
            xin[ti, TP:] = xs[16 * c + 8 * ti:16 * c + 8 * ti + 8].reshape(TS, D)
        m = dict(shared)
        m["xin"] = xin
        sl = slice(16 * c, 16 * c + 16)
        m["st_conv"] = np.ascontiguousarray(sc[:, sl])
        m["st_h"] = np.ascontiguousarray(sh[:, sl])
        m["st_C"] = np.ascontiguousarray(sC[:, sl])
        m["st_n"] = np.ascontiguousarray(sn[:, sl])
        m["st_m"] = np.ascontiguousarray(smm[:, sl])
        m["st_f"] = np.ascontiguousarray(sf[:, sl])
        in_maps.append(m)
    res = run_bass_kernel_spmd(nc, in_maps, core_ids=list(range(8)))
    R = res.results
    y_p = np.empty((4, 2048, D), np.float32)
    y_s = np.empty((128, 8, D), np.float32)
    conv_p = np.empty((2, 4, 3, D), np.float32)
    conv_s = np.empty((2, 128, 3, D), np.float32)
    h_p = np.empty((2, 4, 1024), np.float32)
    h_s = np.empty((2, 128, 1024), np.float32)
    C_p = np.empty((2, 4, 4, 128, 256), np.float32)
    C_s = np.empty((2, 128, 4, 128, 256), np.float32)
    n_p = np.empty((2, 4, 4, 128), np.float32)
    n_s = np.empty((2, 128, 4, 128), np.float32)
    m_p = np.empty((2, 4, 4), np.float32)
    m_s = np.empty((2, 128, 4), np.float32)
    v_s = np.empty((2, 128, 8, D), np.float32)
    f_p = np.empty((4, 4, 2, DFF), np.float32)
    f_s = np.empty((4, 128, 2, DFF), np.float32)
    for c in range(8):
        r = R[c]
        p = c % 4
        for ti in range(2):
            ss = slice(16 * c + 8 * ti, 16 * c + 8 * ti + 8)
            y_s[ss] = r["y_d"][ti, TP:].reshape(8, 8, D)
            conv_s[:, ss] = r["o_conv"][ti][:, 0:8]
            h_s[:, ss] = r["o_h"][ti][:, 0:8]
            C_s[:, ss] = r["o_C"][ti][:, 0:8]
            n_s[:, ss] = r["o_n"][ti][:, 0:8]
            m_s[:, ss] = r["o_m"][ti][:, 0:8]
            v_s[:, ss] = r["o_v"][ti].reshape(2, 8, 8, D)
            f_s[:, ss] = r["o_f"][ti][:, 0:8]
            if c < 4:
                y_p[p, ti * TP:(ti + 1) * TP] = r["y_d"][ti, :TP]
        if c < 4:
            conv_p[:, p] = r["o_conv"][1][:, 8]
            h_p[:, p] = r["o_h"][1][:, 8]
            C_p[:, p] = r["o_C"][1][:, 8]
            n_p[:, p] = r["o_n"][1][:, 8]
            m_p[:, p] = r["o_m"][1][:, 8]
            f_p[:, p] = r["o_f"][1][:, 8]
    return (y_p, y_s, conv_p, conv_s, h_p, h_s, C_p, C_s, n_p, n_s, m_p, m_s, v_s, f_p, f_s)
```
